# Optimizing a Trainium2 kernel written in Bass

```python
import math
import jax, jax.numpy as jnp
from jax import lax
import numpy as np

D_MODEL = 2048
BATCH = 4
SEQ = 4096
DEPTH = 2

BLOCK = 128
EPS = 1e-6
N_HEADS_MLA = 8
Q_LORA = 512
KV_LORA = 512
QK_NOPE = 128
QK_ROPE = 64
V_DIM = 128
ROPE_THETA = 10000.0
N_HEADS_DIL = 8
HEAD_DIM = 128
DIL_PATTERNS = ((128, 1), (512, 4), (2048, 16))
N_BUCKETS = 32
BUCKET_MAX_DIST = 2048
N_HEADS_SB = 16

MIX_A = N_HEADS_MLA * V_DIM
MIX_B = N_HEADS_DIL * HEAD_DIM
MIX_EVEN = MIX_A + MIX_B
MIX_ODD = N_HEADS_SB * HEAD_DIM
EVEN_SPLITS = (Q_LORA, KV_LORA, QK_ROPE, MIX_B, MIX_B, MIX_B, MIX_EVEN)
ODD_SPLITS = (MIX_ODD, MIX_ODD, MIX_ODD, MIX_ODD)
IN_EVEN = sum(EVEN_SPLITS)
IN_ODD = sum(ODD_SPLITS)
N_EVEN = (DEPTH + 1) // 2
N_ODD = DEPTH // 2

kernel_name = "hybrid_mla_dilated_stickbreaking"


def rms_norm(x, gain):
    x32 = x.astype(jnp.float32)
    y = x32 * lax.rsqrt(jnp.mean(x32 * x32, axis=-1, keepdims=True) + EPS)
    return y.astype(x.dtype) * gain


def split_cols(t, widths):
    offs = np.cumsum(widths)[:-1].tolist()
    return jnp.split(t, offs, axis=-1)


def to_heads(t, n):
    b, s, _ = t.shape
    return t.reshape(b, s, n, -1).transpose(0, 2, 1, 3)


def from_heads(t):
    b, n, s, d = t.shape
    return t.transpose(0, 2, 1, 3).reshape(b, s, n * d)


def rope(x, pos):
    half = QK_ROPE // 2
    inv = 1.0 / (ROPE_THETA ** (jnp.arange(half, dtype=jnp.float32) / half))
    ang = pos.astype(jnp.float32)[:, None] * inv[None, :]
    cos = jnp.cos(ang).astype(x.dtype)
    sin = jnp.sin(ang).astype(x.dtype)
    x1, x2 = x[..., :half], x[..., half:]
    return jnp.concatenate([x1 * cos - x2 * sin, x1 * sin + x2 * cos], axis=-1)


def t5_bucket(dist):
    max_exact = N_BUCKETS // 2
    d = jnp.maximum(dist.astype(jnp.float32), 1.0)
    large = max_exact + (jnp.log(d / max_exact) / math.log(BUCKET_MAX_DIST / max_exact)
                         * (N_BUCKETS - max_exact)).astype(jnp.int32)
    large = jnp.minimum(large, N_BUCKETS - 1)
    return jnp.where(dist < max_exact, dist, large)


def block_queries(q):
    b, h, s, d = q.shape
    return q.reshape(b, h, s // BLOCK, BLOCK, d).transpose(2, 0, 1, 3, 4)


def unblock(o):
    nb, b, h, blk, d = o.shape
    return o.transpose(1, 2, 0, 3, 4).reshape(b, h, nb * blk, d)


def causal_softmax_attention(q, k, v):
    s_len = q.shape[2]
    scale = q.shape[-1] ** -0.5
    kpos = jnp.arange(s_len)

    def one_block(args):
        qi, i = args
        s = jnp.einsum('bhqd,bhkd->bhqk', qi, k).astype(jnp.float32) * scale
        qpos = i * BLOCK + jnp.arange(BLOCK)
        s = jnp.where(kpos[None, :] <= qpos[:, None], s, -jnp.inf)
        p = jax.nn.softmax(s, axis=-1).astype(v.dtype)
        return jnp.einsum('bhqk,bhkd->bhqd', p, v)

    nb = s_len // BLOCK
    return unblock(lax.map(one_block, (block_queries(q), jnp.arange(nb))))


def dilated_partial(q, k, v, rel_bias, window, dilation):
    b, h, s_len, d = q.shape
    sub_len = s_len // dilation
    nb = -(-sub_len // BLOCK)
    padded = nb * BLOCK
    span = window // dilation

    def to_blocks(t):
        t = t.reshape(b, h, sub_len, dilation, d).transpose(0, 1, 3, 2, 4)
        t = jnp.pad(t, ((0, 0), (0, 0), (0, 0), (0, padded - sub_len), (0, 0)))
        return t.reshape(b, h, dilation, nb, BLOCK, d)

    def with_prev(t):
        prev = jnp.pad(t[:, :, :, :-1], ((0, 0), (0, 0), (0, 0), (1, 0), (0, 0), (0, 0)))
        return jnp.concatenate([prev, t], axis=4)

    qb = to_blocks(q)
    kb = with_prev(to_blocks(k))
    vb = with_prev(to_blocks(v))

    qi = jnp.arange(BLOCK)[:, None]
    kj = jnp.arange(2 * BLOCK)[None, :]
    rel = BLOCK + qi - kj
    blk = jnp.arange(nb)[:, None, None]
    valid = (rel >= 0) & (rel <= span) & (blk * BLOCK - BLOCK + kj >= 0)
    bias = rel_bias[t5_bucket(jnp.maximum(rel, 0) * dilation)]
    bias = bias.transpose(2, 0, 1).astype(jnp.float32)[:, None, None]

    s = jnp.einsum('bhrnqd,bhrnkd->bhrnqk', qb, kb).astype(jnp.float32) * (d ** -0.5)
    s = jnp.where(valid, s + bias, -jnp.inf)
    mx = jnp.max(s, axis=-1)
    p = jnp.exp(s - mx[..., None])
    den = jnp.sum(p, axis=-1)
    num = jnp.einsum('bhrnqk,bhrnkd->bhrnqd', p, vb.astype(jnp.float32))

    def from_blocks(t):
        t = t.reshape(b, h, dilation, padded, *t.shape[5:])[:, :, :, :sub_len]
        t = jnp.swapaxes(t, 2, 3)
        return t.reshape(b, h, s_len, *t.shape[4:])

    return from_blocks(num), from_blocks(den), from_blocks(mx)


def dilated_attention(q, k, v, rel_bias):
    parts = [dilated_partial(q, k, v, rel_bias, w, dl) for (w, dl) in DIL_PATTERNS]
    mx = parts[0][2]
    for part in parts[1:]:
        mx = jnp.maximum(mx, part[2])
    w0 = jnp.exp(parts[0][2] - mx)
    num = parts[0][0] * w0[..., None]
    den = parts[0][1] * w0
    for part in parts[1:]:
        wi = jnp.exp(part[2] - mx)
        num = num + part[0] * wi[..., None]
        den = den + part[1] * wi
    return (num / den[..., None]).astype(q.dtype)


def stick_breaking_attention(q, k, v):
    s_len = q.shape[2]
    scale = q.shape[-1] ** -0.5
    kpos = jnp.arange(s_len)

    def one_block(args):
        qi, i = args
        z = jnp.einsum('bhqd,bhkd->bhqk', qi, k).astype(jnp.float32) * scale
        qpos = i * BLOCK + jnp.arange(BLOCK)
        before = kpos[None, :] < qpos[:, None]
        log_beta = jax.nn.log_sigmoid(z)
        log_rest = jnp.where(before, jax.nn.log_sigmoid(-z), 0.0)
        between = lax.cumsum(log_rest, axis=3, reverse=True) - log_rest
        a = jnp.where(before, jnp.exp(log_beta + between), 0.0).astype(v.dtype)
        return jnp.einsum('bhqk,bhkd->bhqd', a, v)

    nb = s_len // BLOCK
    return unblock(lax.map(one_block, (block_queries(q), jnp.arange(nb))))


def even_mixer(h, w_in, q_norm_gain, kv_norm_gain, w_uq, w_ukv, w_out, rel_bias, pos):
    b, s_len, _ = h.shape
    proj = h @ w_in
    c_q, c_kv, k_rope, q_b, k_b, v_b, gate = split_cols(proj, EVEN_SPLITS)
    q_a = to_heads(rms_norm(c_q, q_norm_gain) @ w_uq, N_HEADS_MLA)
    q_nope, q_rot = q_a[..., :QK_NOPE], q_a[..., QK_NOPE:]
    kv_a = to_heads(rms_norm(c_kv, kv_norm_gain) @ w_ukv, N_HEADS_MLA)
    k_nope, v_a = kv_a[..., :QK_NOPE], kv_a[..., QK_NOPE:]
    k_rot = rope(k_rope, pos)[:, None]
    q_full = jnp.concatenate([q_nope, rope(q_rot, pos)], axis=-1)
    k_full = jnp.concatenate([k_nope, jnp.broadcast_to(k_rot, (b, N_HEADS_MLA, s_len, QK_ROPE))], axis=-1)
    o_a = from_heads(causal_softmax_attention(q_full, k_full, v_a))
    o_b = from_heads(dilated_attention(to_heads(q_b, N_HEADS_DIL), to_heads(k_b, N_HEADS_DIL),
                                       to_heads(v_b, N_HEADS_DIL), rel_bias))
    mix = jnp.concatenate([o_a, o_b], axis=-1) * jax.nn.silu(gate)
    return mix @ w_out


def odd_mixer(h, w_in, w_out):
    proj = h @ w_in
    q, k, v, gate = split_cols(proj, ODD_SPLITS)
    o = stick_breaking_attention(to_heads(q, N_HEADS_SB), to_heads(k, N_HEADS_SB), to_heads(v, N_HEADS_SB))
    return (from_heads(o) * jax.nn.silu(gate)) @ w_out


def setup_inputs(seed: int = 0) -> dict:
    key = jax.random.key(seed)
    ks = jax.random.split(key, 12)
    f32 = jnp.float32
    return {
        "x": jax.random.normal(ks[0], (BATCH, SEQ, D_MODEL), f32),
        "norm_gain": 1.0 + 0.02 * jax.random.normal(ks[1], (DEPTH, D_MODEL), f32),
        "w_in_even": jax.random.normal(ks[2], (N_EVEN, D_MODEL, IN_EVEN), f32) * D_MODEL ** -0.5,
        "q_norm_gain": 1.0 + 0.02 * jax.random.normal(ks[3], (N_EVEN, Q_LORA), f32),
        "kv_norm_gain": 1.0 + 0.02 * jax.random.normal(ks[4], (N_EVEN, KV_LORA), f32),
        "w_uq": jax.random.normal(ks[5], (N_EVEN, Q_LORA, N_HEADS_MLA * (QK_NOPE + QK_ROPE)), f32) * Q_LORA ** -0.5,
        "w_ukv": jax.random.normal(ks[6], (N_EVEN, KV_LORA, N_HEADS_MLA * (QK_NOPE + V_DIM)), f32) * KV_LORA ** -0.5,
        "w_out_even": jax.random.normal(ks[7], (N_EVEN, MIX_EVEN, D_MODEL), f32) * MIX_EVEN ** -0.5,
        "rel_bias": 0.5 * jax.random.normal(ks[8], (N_BUCKETS, N_HEADS_DIL), f32),
        "w_in_odd": jax.random.normal(ks[9], (N_ODD, D_MODEL, IN_ODD), f32) * D_MODEL ** -0.5,
        "w_out_odd": jax.random.normal(ks[10], (N_ODD, MIX_ODD, D_MODEL), f32) * MIX_ODD ** -0.5,
        "final_norm_gain": 1.0 + 0.02 * jax.random.normal(ks[11], (D_MODEL,), f32),
    }


def reference(x, norm_gain, w_in_even, q_norm_gain, kv_norm_gain, w_uq, w_ukv, w_out_even,
              rel_bias, w_in_odd, w_out_odd, final_norm_gain):
    pos = jnp.arange(x.shape[1])
    for layer in range(DEPTH):
        h = rms_norm(x, norm_gain[layer])
        j = layer // 2
        if layer % 2 == 0:
            x = x + even_mixer(h, w_in_even[j], q_norm_gain[j], kv_norm_gain[j], w_uq[j], w_ukv[j],
                               w_out_even[j], rel_bias, pos)
        else:
            x = x + odd_mixer(h, w_in_odd[j], w_out_odd[j])
    return rms_norm(x, final_norm_gain)
```

```python
import contextlib
import math
import numpy as np
import ml_dtypes
import concourse.bass as bass
import concourse.mybir as mybir
from concourse.bass_utils import run_bass_kernel_spmd

F32 = mybir.dt.float32
BF16 = mybir.dt.bfloat16
AF = mybir.ActivationFunctionType
ALU = mybir.AluOpType

ENGS = ["pe", "act", "dve", "pool", "sp"]
NEG = -30000.0
EMBED_WAIT = True
EPS = 1e-6


class Slot:
    def __init__(self, sem):
        self.sem = sem
        self.count = 0


class Sched:
    def __init__(self, nc, es, same_engine_sync=("act", "dve", "pool")):
        self.nc = nc
        self.es = es
        self.same = set(same_engine_sync)
        self.ops = {e: [] for e in ENGS}
        self.phase_id = 0
        self._begin_phase()

    def _begin_phase(self):
        self.phase_id += 1
        if not hasattr(self, "pool"):
            self.pool = {"eng": [], "sp": [], "pool": []}
            self.semval = {}
        self.pool_pos = {k: 0 for k in self.pool}
        self.sem = {}
        self.sem_key = {}
        self.cnt = {}
        for e in ENGS:
            self.sem[e], self.sem_key[e] = self._alloc("eng")
            self.cnt[e] = self.semval[self.sem_key[e]]
        self.waited = {e: {} for e in ENGS}
        self.slots = []

    def _alloc(self, kind):
        pool = self.pool[kind]
        i = self.pool_pos[kind]
        if i >= len(pool):
            pool.append(self.es.enter_context(self.nc.semaphore("sem_%s_%d" % (kind, len(pool)))))
            self.semval[(kind, i)] = 0
        self.pool_pos[kind] += 1
        return pool[i], (kind, i)

    def slot(self, kind="sp"):
        h, key = self._alloc(kind)
        sl = Slot(h)
        sl.count = self.semval[key]
        sl.key = key
        sl.kind = kind
        self.slots.append(sl)
        return sl

    def op(self, eng, fn, deps=()):
        self.cnt[eng] += 1
        tok = ("e", eng, self.cnt[eng])
        self.ops[eng].append((fn, self._flat(deps), tok))
        return tok

    def dma(self, eng, fn, slot, deps=()):
        assert slot.kind == eng, (slot.kind, eng)
        slot.count += 16
        tok = ("d", slot, slot.count)
        self.ops[eng].append((fn, self._flat(deps), tok))
        return tok

    def _flat(self, deps):
        out = []
        for d in deps:
            if d is None:
                continue
            if isinstance(d, list):
                out.extend(self._flat(d))
            else:
                out.append(d)
        return out

    def _emit_engine(self, eng, e):
        waited = self.waited[eng]
        for fn, deps, tok in self.ops[eng]:
            need = {}
            for d in deps:
                if d[0] == "e":
                    if d[1] == eng and (eng not in self.same or len(d) > 3):
                        continue
                    key = ("e", d[1])
                    sem = self.sem[d[1]]
                else:
                    key = ("d", id(d[1]))
                    sem = d[1].sem
                if waited.get(key, 0) >= d[2]:
                    continue
                if key not in need or need[key][1] < d[2]:
                    need[key] = (sem, d[2])
            items = list(need.items())
            for key, (sem, val) in items:
                waited[key] = val
            for key, (sem, val) in items[:-1]:
                e.wait_ge(sem, val)
            inst = fn(e)
            if items:
                sem, val = items[-1][1]
                if EMBED_WAIT:
                    inst._wait_ge(sem, val)
                else:
                    raise RuntimeError("standalone wait must precede instruction")
            if tok[0] == "e":
                inst.then_inc(self.sem[eng], 1)
            else:
                inst.then_inc(tok[1].sem, 16)
        self.ops[eng] = []

    def emit(self):
        nc = self.nc
        with nc.Block() as block:
            @block.tensor
            def _(e):
                self._emit_engine("pe", e)

            @block.scalar
            def _(e):
                self._emit_engine("act", e)

            @block.vector
            def _(e):
                self._emit_engine("dve", e)

            @block.gpsimd
            def _(e):
                self._emit_engine("pool", e)

            @block.sync
            def _(e):
                self._emit_engine("sp", e)
        for e in ENGS:
            self.semval[self.sem_key[e]] = self.cnt[e]
        for sl in self.slots:
            self.semval[sl.key] = sl.count
        self._begin_phase()


def war(tok):
    if tok is None:
        return None
    if isinstance(tok, list):
        return [war(t) for t in tok]
    if tok[0] == "e" and len(tok) == 3:
        return tok + ("war",)
    return tok


class Rot:
    def __init__(self, bufs):
        self.bufs = bufs
        self.free = [None] * len(bufs)
        self.k = 0

    def next(self):
        i = self.k % len(self.bufs)
        self.k += 1
        return i, self.bufs[i], war(self.free[i])

    def release(self, i, tok):
        self.free[i] = tok


class Ctx:
    uid = 0

    def __init__(self, nc, S):
        self.nc = nc
        self.S = S
        self.es = contextlib.ExitStack()
        self.n = 0
        self.stages = []

    def __enter__(self):
        self.es.__enter__()
        return self

    def __exit__(self, *a):
        return self.es.__exit__(*a)

    def sb(self, shape, dt, name=None):
        Ctx.uid += 1
        return self.es.enter_context(self.nc.sbuf_tensor("%s_%d" % (name or "t", Ctx.uid), shape, dt))

    def ps(self, shape, dt, name=None):
        Ctx.uid += 1
        return self.es.enter_context(self.nc.psum_tensor("%s_%d" % (name or "p", Ctx.uid), shape, dt))

    def flush(self, extra=()):
        toks = list(extra)
        for st in self.stages:
            toks.extend([t for t in st.last if t is not None])
        self.S.op("sp", lambda e: e.nop(), toks)
        self.S.emit()


class OutStage:
    def __init__(self, cx, shape, dt, n=3, dma_eng="pool", name="stg"):
        self.S = cx.S
        self.tiles = [cx.sb(shape, dt, name) for _ in range(n)]
        self.slots = [cx.S.slot(dma_eng) for _ in range(n)]
        self.last = [None] * n
        self.k = 0
        self.dma_eng = dma_eng
        cx.stages.append(self)

    def put(self, eng, compute, deps, dram_ap, sub=None):
        i = self.k % len(self.tiles)
        self.k += 1
        t = self.tiles[i]
        tc = self.S.op(eng, lambda e: compute(e, t), list(deps) + [self.last[i]])
        src = t[:] if sub is None else sub(t)
        self.last[i] = self.S.dma(self.dma_eng, lambda e: e.dma_start(out=dram_ap, in_=src), self.slots[i], [tc])
        return tc

    def put_multi(self, computes, dram_ap, sub=None):
        i = self.k % len(self.tiles)
        self.k += 1
        t = self.tiles[i]
        toks = []
        prev = self.last[i]
        for eng, fn, deps in computes:
            prev = self.S.op(eng, lambda e, fn=fn: fn(e, t), list(deps) + [prev])
            toks.append(prev)
        src = t[:] if sub is None else sub(t)
        self.last[i] = self.S.dma(self.dma_eng, lambda e: e.dma_start(out=dram_ap, in_=src), self.slots[i], [prev])
        return toks


def evac(eng, e, out, in_, scale=1.0, func=None):
    if eng == "act":
        return e.activation(out=out, in_=in_, func=(func or AF.Copy), scale=scale)
    assert func is None
    if scale == 1.0:
        return e.tensor_copy(out=out, in_=in_)
    return e.tensor_scalar(out=out, in0=in_, scalar1=float(scale), scalar2=None, op0=ALU.mult)


class Cfg:
    SL = 4096
    D = 2048

    def __init__(self, NR=1):
        self.NR = NR
        self.HA = 8 // NR
        self.HB = 8 // NR
        self.HS = 16 // NR

    @property
    def FM0(self):
        return (self.HA + self.HB) * 128

    @property
    def FM1(self):
        return self.HS * 128

    @property
    def W0C(self):
        return 1088 + 3 * self.HB * 128 + self.FM0

    @property
    def W1C(self):
        return 4 * self.HS * 128


def phase_norm(nc, S, cfg, x_dram, g_dram, hT, ident):
    SL, D = cfg.SL, cfg.D
    KC = D // 128
    with Ctx(nc, S) as cx:
        xt = [cx.sb([128, D], F32, "xt") for _ in range(2)]
        hb = [cx.sb([128, D], BF16, "hb") for _ in range(2)]
        junk = cx.sb([128, D], BF16, "junk")
        gt = cx.sb([128, D], F32, "gt")
        ss = cx.sb([128, SL // 128], F32, "ss")
        rs = cx.sb([128, SL // 128], F32, "rs")
        pT = Rot([cx.ps([128, 4, 128], BF16, "pT") for _ in range(4)])
        sx = [S.slot() for _ in range(2)]
        sg = S.slot()
        t_g = S.dma("sp", lambda e: e.dma_start(out=gt[:], in_=g_dram.partition_broadcast(128)), sg)
        x_free = [None, None]
        h_free = [None, None]
        nev = 0
        for tb in range(SL // 128):
            b = tb % 2
            t_x = S.dma("sp", lambda e, b=b, tb=tb: e.dma_start(out=xt[b][:], in_=x_dram[tb * 128:(tb + 1) * 128, :]),
                        sx[b], [x_free[b]])
            t_ss = S.op("act", lambda e, b=b, tb=tb: e.activation(out=junk[:], in_=xt[b][:], func=AF.Square,
                                                                  accum_out=ss[:, tb:tb + 1]), [t_x])
            t_a = S.op("act", lambda e, tb=tb: e.activation(out=rs[:, tb:tb + 1], in_=ss[:, tb:tb + 1], func=AF.Sqrt,
                                                            bias=EPS, scale=1.0 / D), [t_ss])
            t_r = S.op("dve", lambda e, tb=tb: e.reciprocal(out=rs[:, tb:tb + 1], in_=rs[:, tb:tb + 1]), [t_a])
            t_h = S.op("dve", lambda e, b=b, tb=tb: e.scalar_tensor_tensor(
                out=hb[b][:], in0=xt[b][:], scalar=rs[:, tb:tb + 1], in1=gt[:], op0=ALU.mult, op1=ALU.mult),
                [t_r, t_g, h_free[b]])
            x_free[b] = t_h
            tp = None
            for grp in range(KC // 4):
                i, pt, fr = pT.next()
                for j in range(4):
                    kc = grp * 4 + j
                    tp = S.op("pe", lambda e, pt=pt, j=j, kc=kc, b=b: e.transpose(
                        out=pt[:, j, :], in_=hb[b][:, kc * 128:(kc + 1) * 128], identity=ident[:]), [t_h, fr])
                eng = "act" if nev % 2 == 0 else "dve"
                nev += 1
                t_e = S.op(eng, lambda e, eng=eng, pt=pt, grp=grp, tb=tb: evac(
                    eng, e, hT[:, grp * 4:(grp + 1) * 4, tb * 128:(tb + 1) * 128], pt[:]), [tp])
                pT.release(i, t_e)
            h_free[b] = tp
        S.emit()


class Proj:
    def __init__(self, cx, cfg, srcT, KC, src_tok, nbanks=4, banks=None):
        self.cx, self.S, self.cfg = cx, cx.S, cfg
        self.srcT, self.KC, self.src_tok = srcT, KC, src_tok
        self.banks = banks or Rot([cx.ps([128, 512], F32, "pp") for _ in range(nbanks)])

    def mm_T(self, w, c0, ncol, tt, wtok, M=None):
        S = self.S
        i, bank, fr = self.banks.next()
        tm = None
        for kc in range(self.KC):
            tm = S.op("pe", lambda e, kc=kc, bank=bank: e.matmul(
                out=bank[0:ncol, :], lhsT=w[:, kc, c0:c0 + ncol], rhs=self.srcT[:, kc, tt * 512:(tt + 1) * 512],
                start=(kc == 0), stop=(kc == self.KC - 1)), [fr, wtok, self.src_tok])
        return i, bank, tm

    def mm_N(self, w, c0, ncol, tb, wtok):
        S = self.S
        i, bank, fr = self.banks.next()
        tm = None
        for kc in range(self.KC):
            tm = S.op("pe", lambda e, kc=kc, bank=bank: e.matmul(
                out=bank[:, 0:ncol], lhsT=self.srcT[:, kc, tb * 128:(tb + 1) * 128], rhs=w[:, kc, c0:c0 + ncol],
                start=(kc == 0), stop=(kc == self.KC - 1)), [fr, wtok, self.src_tok])
        return i, bank, tm


class WStream:
    def __init__(self, cx, KC, width=512, n=2):
        self.cx, self.S, self.KC = cx, cx.S, KC
        self.tiles = [cx.sb([128, KC, width], BF16, "wt") for _ in range(n)]
        self.slots = [cx.S.slot("pool") for _ in range(n)]
        self.free = [None] * n
        self.k = 0

    def load(self, w_ap, ncol):
        i = self.k % len(self.tiles)
        self.k += 1
        t = self.tiles[i]
        src = w_ap.rearrange("(k p) n -> p k n", p=128)
        tok = self.S.dma("pool", lambda e: e.dma_start(out=t[:, :, 0:ncol], in_=src), self.slots[i], [self.free[i]])
        return i, t, tok

    def release(self, i, tok):
        self.free[i] = tok


def phase_proj0(nc, S, cfg, hT, h_tok, w0, cs_dram, scr):
    SL, HA, HB = cfg.SL, cfg.HA, cfg.HB
    KC = cfg.D // 128
    NTT, NTB = SL // 512, SL // 128
    with Ctx(nc, S) as cx:
        pj = Proj(cx, cfg, hT, KC, h_tok)
        ws = WStream(cx, KC)
        st32 = OutStage(cx, [128, 512], F32, n=2, name="st32")
        st16 = OutStage(cx, [128, 512], BF16, n=4, name="st16")
        groups = []
        groups.append((0, 512, "T", ("f32", scr["cq_T"], 0, 1.0, None)))
        groups.append((512, 512, "T", ("f32", scr["ckv_T"], 0, 1.0, None)))
        groups.append((1024, 64, "R", None))
        c = 1088
        for g in range(HB * 128 // 512):
            groups.append((c + g * 512, 512, "T", ("bf", scr["qb_T"], g * 512, 128.0 ** -0.5, None)))
        c += HB * 128
        for g in range(HB * 128 // 512):
            groups.append((c + g * 512, 512, "T", ("bf", scr["kb_T"], g * 512, 1.0, None)))
        c += HB * 128
        for g in range(HB * 128 // 512):
            groups.append((c + g * 512, 512, "N", (scr["vb"], g * 4)))
        c += HB * 128
        for g in range(cfg.FM0 // 512):
            groups.append((c + g * 512, 512, "T", ("bf", scr["gate_T"], g * 512, 1.0, AF.Silu)))
        nxt = ws.load(w0[:, groups[0][0]:groups[0][0] + groups[0][1]], groups[0][1])
        nev = 0
        for gi, (c0, ncol, kind, spec) in enumerate(groups):
            wi, wt, wtok = nxt
            if gi + 1 < len(groups):
                n0, nn = groups[gi + 1][0], groups[gi + 1][1]
                nxt = ws.load(w0[:, n0:n0 + nn], nn)
            last = None
            if kind == "T":
                typ, dst, r0, scale, func = spec
                for cb in range(ncol // 128):
                    for tt in range(NTT):
                        i, bank, tm = pj.mm_T(wt, cb * 128, 128, tt, wtok)
                        last = tm
                        eng = "act" if (func is not None or nev % 2 == 0) else "dve"
                        nev += 1
                        stg = st32 if typ == "f32" else st16
                        tc = stg.put(eng, lambda e, t, eng=eng, bank=bank, scale=scale, func=func: evac(
                            eng, e, t[:], bank[:], scale, func), [tm],
                            dst[r0 + cb * 128:r0 + (cb + 1) * 128, tt * 512:(tt + 1) * 512])
                        pj.banks.release(i, tc)
            elif kind == "N":
                dst, h0 = spec
                for tb in range(NTB):
                    i, bank, tm = pj.mm_N(wt, 0, ncol, tb, wtok)
                    last = tm
                    eng = "act" if nev % 2 == 0 else "dve"
                    nev += 1
                    tc = st16.put(eng, lambda e, t, eng=eng, bank=bank: evac(eng, e, t[:], bank[:]), [tm],
                                  dst[h0:h0 + 4, tb * 128:(tb + 1) * 128, :].rearrange("h p f -> p h f"),
                                  sub=lambda t: t[:].rearrange("p (h f) -> p h f", h=4))
                    pj.banks.release(i, tc)
            else:
                last = Rope(cx, pj, cfg, cs_dram, st16).run(wt, 0, wtok, 1.0, scr["kr_T"])
            ws.release(wi, last)
        cx.flush()


def latent_norm(cx, cfg, c_dram, gain_dram, cnT, ones32, pbanks, ones_tok):
    S = cx.S
    SL = cfg.SL
    cT = cx.sb([128, 4, SL], F32, "cT")
    gn = cx.sb([128, 4], F32, "gn")
    sq = [cx.sb([128, 4, 512], F32, "sq") for _ in range(2)]
    rt = [cx.sb([128, 512], F32, "rt") for _ in range(2)]
    sl = S.slot()
    sg = S.slot()
    t_g = S.dma("sp", lambda e: e.dma_start(out=gn[:], in_=gain_dram), sg)
    t_c = None
    for k in range(4):
        t_c = S.dma("sp", lambda e, k=k: e.dma_start(out=cT[:, k, :], in_=c_dram[k * 128:(k + 1) * 128, :]), sl)
    sq_free = [None, None]
    rt_free = [None, None]
    last = None
    for tt in range(SL // 512):
        b = tt % 2
        t_s = S.op("act", lambda e, b=b, tt=tt: e.activation(out=sq[b][:], in_=cT[:, :, tt * 512:(tt + 1) * 512], func=AF.Square),
                   [t_c, sq_free[b]])
        i, bank, fr = pbanks.next()
        tm = None
        for k in range(4):
            tm = S.op("pe", lambda e, k=k, b=b, bank=bank: e.matmul(out=bank[:], lhsT=ones32[:], rhs=sq[b][:, k, :],
                                                                   start=(k == 0), stop=(k == 3)), [t_s, fr, ones_tok])
        sq_free[b] = tm
        t_a = S.op("act", lambda e, b=b, bank=bank: e.activation(out=rt[b][:], in_=bank[:], func=AF.Sqrt, bias=EPS, scale=1.0 / 512),
                   [tm, rt_free[b]])
        pbanks.release(i, t_a)
        t_r = S.op("dve", lambda e, b=b: e.reciprocal(out=rt[b][:], in_=rt[b][:]), [t_a])
        tn = t_r
        for k in range(4):
            tn = S.op("dve", lambda e, k=k, b=b, tt=tt: e.scalar_tensor_tensor(
                out=cnT[:, k, tt * 512:(tt + 1) * 512], in0=cT[:, k, tt * 512:(tt + 1) * 512], scalar=gn[:, k:k + 1],
                in1=rt[b][:], op0=ALU.mult, op1=ALU.mult), [t_r, t_g, tn])
        rt_free[b] = tn
        last = tn
    return last


def phase_up(nc, S, cfg, wuq_d, wukv_d, qg_d, kvg_d, cs_dram, ones32_d, scr):
    SL, HA = cfg.SL, cfg.HA
    NTT, NTB = SL // 512, SL // 128
    for which in ("q", "kv"):
        with Ctx(nc, S) as cx:
            ones32 = cx.sb([128, 128], F32, "ones32")
            so = S.slot()
            t_o = S.dma("sp", lambda e: e.dma_start(out=ones32[:], in_=ones32_d), so)
            cnT = cx.sb([128, 4, SL], BF16, "cnT")
            banks = Rot([cx.ps([128, 512], F32, "pp") for _ in range(4)])
            st16 = OutStage(cx, [128, 512], BF16, n=4, name="st16")
            ncols = HA * 192 if which == "q" else HA * 256
            wt = cx.sb([128, 4, ncols], BF16, "wup")
            sw = S.slot("pool")
            wd = wuq_d if which == "q" else wukv_d
            t_w = S.dma("pool", lambda e: e.dma_start(out=wt[:], in_=wd.rearrange("(k p) n -> p k n", p=128)), sw)
            t_n = latent_norm(cx, cfg, scr["cq_T"] if which == "q" else scr["ckv_T"],
                              qg_d if which == "q" else kvg_d, cnT, ones32, banks, t_o)
            pj = Proj(cx, cfg, cnT, 4, t_n, banks=banks)
            nev = 0
            if which == "q":
                sc = 192.0 ** -0.5
                rp = Rope(cx, pj, cfg, cs_dram, st16)
                for h in range(HA):
                    for tt in range(NTT):
                        i, bank, tm = pj.mm_T(wt, h * 192, 128, tt, t_w)
                        eng = "act" if nev % 2 == 0 else "dve"
                        nev += 1
                        tc = st16.put(eng, lambda e, t, eng=eng, bank=bank: evac(eng, e, t[:], bank[:], sc), [tm],
                                      scr["qa_T"][h, 0:128, tt * 512:(tt + 1) * 512])
                        pj.banks.release(i, tc)
                    rp.run(wt, h * 192 + 128, t_w, sc, scr["qa_T"][h, 128:192, :])
            else:
                for h in range(HA):
                    for tt in range(NTT):
                        i, bank, tm = pj.mm_T(wt, h * 256, 128, tt, t_w)
                        eng = "act" if nev % 2 == 0 else "dve"
                        nev += 1
                        tc = st16.put(eng, lambda e, t, eng=eng, bank=bank: evac(eng, e, t[:], bank[:]), [tm],
                                      scr["ka_T"][h, :, tt * 512:(tt + 1) * 512])
                        pj.banks.release(i, tc)
                    for tb in range(NTB):
                        i, bank, tm = pj.mm_N(wt, h * 256 + 128, 128, tb, t_w)
                        eng = "act" if nev % 2 == 0 else "dve"
                        nev += 1
                        tc = st16.put(eng, lambda e, t, eng=eng, bank=bank: evac(eng, e, t[:, 0:128], bank[:, 0:128]), [tm],
                                      scr["va"][h, tb * 128:(tb + 1) * 128, :], sub=lambda t: t[:, 0:128])
                        pj.banks.release(i, tc)
            cx.flush()


class Rope:
    def __init__(self, cx, pj, cfg, cs_dram, out_stage):
        self.cx, self.pj, self.cfg, self.cs_dram, self.out_stage = cx, pj, cfg, cs_dram, out_stage
        S = cx.S
        self.wr = cx.sb([128, pj.KC, 64], BF16, "wrot")
        self.cst = [cx.sb([64, 2, 512], F32, "cs") for _ in range(2)]
        self.css = [S.slot() for _ in range(2)]
        self.csfree = [None, None]
        self.tmp = [cx.sb([64, 2, 512], F32, "rtmp") for _ in range(2)]
        self.last_mm = None
        self.k = 0

    def run(self, w, c0, wtok, scale, dst_rows):
        S = self.cx.S
        pj, wr, cst, tmp = self.pj, self.wr, self.cst, self.tmp
        t1 = S.op("act", lambda e: e.activation(out=wr[:, :, 0:32], in_=w[:, :, c0 + 32:c0 + 64], func=AF.Copy, scale=-1.0),
                  [wtok, self.last_mm])
        t2 = S.op("act", lambda e: e.activation(out=wr[:, :, 32:64], in_=w[:, :, c0:c0 + 32], func=AF.Copy, scale=1.0),
                  [wtok, t1])
        m2 = None
        for tt in range(self.cfg.SL // 512):
            b = self.k % 2
            self.k += 1
            t_cs = S.dma("sp", lambda e, b=b, tt=tt: e.dma_start(
                out=cst[b][:], in_=self.cs_dram[:, :, tt * 512:(tt + 1) * 512].rearrange("c p t -> p c t")),
                self.css[b], [self.csfree[b]])
            i1, b1, m1 = pj.mm_T(w, c0, 64, tt, wtok)
            i2, b2, m2 = pj.mm_T(wr, 0, 64, tt, t2)
            ta = S.op("dve", lambda e, b=b, b1=b1: e.scalar_tensor_tensor(
                out=tmp[b][:, 0, :], in0=b1[0:64, :], scalar=float(scale), in1=cst[b][:, 0, :], op0=ALU.mult, op1=ALU.mult),
                [m1, t_cs, self.csfree[b]])
            tb_ = S.op("dve", lambda e, b=b, b2=b2: e.scalar_tensor_tensor(
                out=tmp[b][:, 1, :], in0=b2[0:64, :], scalar=float(scale), in1=cst[b][:, 1, :], op0=ALU.mult, op1=ALU.mult),
                [m2, t_cs, ta])
            pj.banks.release(i1, ta)
            pj.banks.release(i2, tb_)
            tc = self.out_stage.put("dve", lambda e, t, b=b: e.tensor_tensor(
                out=t[0:64, :], in0=tmp[b][:, 0, :], in1=tmp[b][:, 1, :], op=ALU.add),
                [ta, tb_], dst_rows[:, tt * 512:(tt + 1) * 512], sub=lambda t: t[0:64, :])
            self.csfree[b] = tc
        self.last_mm = m2
        return m2


def load_const(cx, dram_ap, shape, dt, name):
    t = cx.sb(shape, dt, name)
    sl = cx.S.slot()
    tok = cx.S.dma("sp", lambda e: e.dma_start(out=t[:], in_=dram_ap), sl)
    return t, tok


def phase_mla(nc, S, cfg, I, scr):
    SL, HA = cfg.SL, cfg.HA
    NQT, NB = SL // 512, SL // 128
    with Ctx(nc, S) as cx:
        ident, t_c0 = load_const(cx, I["ident"], [128, 128], BF16, "ident")
        ones, t_c1 = load_const(cx, I["ones_bf"], [128, 128], BF16, "ones")
        maskA, t_c2 = load_const(cx, I["maskA"], [128, 4, 512], BF16, "maskA")
        kr, t_kr = load_const(cx, scr["kr_T"], [64, SL], BF16, "kr")
        t_const = [t_c0, t_c1, t_c2, t_kr]
        qn = [cx.sb([128, SL], BF16, "qn") for _ in range(2)]
        qr = [cx.sb([64, SL], BF16, "qr") for _ in range(2)]
        kn = [cx.sb([128, SL], BF16, "kn") for _ in range(2)]
        vv = [cx.sb([128, NB, 128], BF16, "vv") for _ in range(2)]
        hs = [S.slot() for _ in range(2)]
        head_free = [None, None]
        pS = Rot([cx.ps([128, 512], F32, "pS") for _ in range(2)])
        pO = Rot([cx.ps([128, 512], F32, "pO") for _ in range(2)])
        pD = Rot([cx.ps([128, 512], F32, "pD") for _ in range(2)])
        pts = Rot([cx.sb([128, 512], BF16, "pt") for _ in range(3)])
        gts = Rot([cx.sb([128, 512], BF16, "gt") for _ in range(2)])
        gsl = [S.slot() for _ in range(2)]
        rden = Rot([cx.sb([128, 512], F32, "rden") for _ in range(2)])
        of = Rot([cx.sb([128, 512], F32, "of") for _ in range(2)])
        stg = OutStage(cx, [128, 512], BF16, n=2, name="mixst")

        def load_head(h):
            b = h % 2
            fr = head_free[b]
            S.dma("sp", lambda e: e.dma_start(out=qn[b][:], in_=scr["qa_T"][h, 0:128, :]), hs[b], [fr])
            S.dma("sp", lambda e: e.dma_start(out=qr[b][:], in_=scr["qa_T"][h, 128:192, :]), hs[b], [fr])
            S.dma("sp", lambda e: e.dma_start(out=kn[b][:], in_=scr["ka_T"][h, :, :]), hs[b], [fr])
            tok = None
            for j in range(4):
                tok = S.dma("sp", lambda e, j=j: e.dma_start(
                    out=vv[b][:, j * 8:(j + 1) * 8, :],
                    in_=scr["va"][h, j * 1024:(j + 1) * 1024, :].rearrange("(n p) f -> p n f", p=128)), hs[b], [fr])
            return tok

        blocks = []
        for h in range(HA):
            for qt in range(NQT):
                nkb = 4 * qt + 4
                for kb in range(nkb):
                    blocks.append((h, qt, kb, nkb))
        head_tok = {0: load_head(0)}
        state = {}

        def issue_S(bi):
            h, qt, kb, nkb = blocks[bi]
            b = h % 2
            i_s, Sb, sfr = pS.next()
            diag = kb >= 4 * qt
            S.op("pe", lambda e: e.matmul(out=Sb[:], lhsT=kn[b][:, kb * 128:(kb + 1) * 128], rhs=qn[b][:, qt * 512:(qt + 1) * 512],
                                         start=True, stop=False), [head_tok[h], sfr] + t_const)
            m = S.op("pe", lambda e: e.matmul(out=Sb[:], lhsT=kr[:, kb * 128:(kb + 1) * 128], rhs=qr[b][:, qt * 512:(qt + 1) * 512],
                                             start=False, stop=(not diag)), [])
            if diag:
                m = S.op("pe", lambda e: e.matmul(out=Sb[:], lhsT=ident[:], rhs=maskA[:, kb - 4 * qt, :], start=False, stop=True), [])
            state[bi] = (i_s, Sb, m)

        issue_S(0)
        cur = {}
        for bi, (h, qt, kb, nkb) in enumerate(blocks):
            b = h % 2
            if bi + 1 < len(blocks):
                issue_S(bi + 1)
            i_s, Sb, m = state.pop(bi)
            if kb == 0:
                io, O, ofr = pO.next()
                idn, Dn, dfr = pD.next()
                gi, gt, gfr = gts.next()
                t_gate = S.dma("sp", lambda e, gt=gt, h=h, qt=qt: e.dma_start(
                    out=gt[:], in_=scr["gate_T"][h * 128:(h + 1) * 128, qt * 512:(qt + 1) * 512]), gsl[gi], [gfr])
                cur = dict(io=io, O=O, ofr=ofr, idn=idn, Dn=Dn, dfr=dfr, gi=gi, gt=gt, t_gate=t_gate)
            O, Dn = cur["O"], cur["Dn"]
            ip, pt, pfr = pts.next()
            t_e = S.op("act", lambda e, pt=pt, Sb=Sb: e.activation(out=pt[:], in_=Sb[:], func=AF.Exp), [m, pfr])
            pS.release(i_s, t_e)
            first, last = kb == 0, kb == nkb - 1
            S.op("pe", lambda e, O=O, pt=pt, b=b, kb=kb, first=first, last=last: e.matmul(
                out=O[:], lhsT=vv[b][:, kb, :], rhs=pt[:], start=first, stop=last), [t_e, cur["ofr"] if first else None])
            m5 = S.op("pe", lambda e, Dn=Dn, pt=pt, first=first, last=last: e.matmul(
                out=Dn[:], lhsT=ones[:], rhs=pt[:], start=first, stop=last), [t_e, cur["dfr"] if first else None])
            pts.release(ip, m5)
            if last:
                ir, rd, rfr = rden.next()
                io2, o32, o32fr = of.next()
                t_r = S.op("dve", lambda e, rd=rd, Dn=Dn: e.reciprocal(out=rd[:], in_=Dn[:]), [m5, rfr])
                pD.release(cur["idn"], t_r)
                t_o = S.op("dve", lambda e, o32=o32, O=O, rd=rd: e.tensor_tensor(out=o32[:], in0=O[:], in1=rd[:], op=ALU.mult),
                           [m5, t_r, o32fr])
                pO.release(cur["io"], t_o)
                rden.release(ir, t_o)
                gt = cur["gt"]
                tc = stg.put("dve", lambda e, t, o32=o32, gt=gt: e.tensor_tensor(out=t[:], in0=o32[:], in1=gt[:], op=ALU.mult),
                             [t_o, cur["t_gate"]], scr["mix0_T"][h * 128:(h + 1) * 128, qt * 512:(qt + 1) * 512])
                of.release(io2, tc)
                gts.release(cur["gi"], tc)
                if qt == NQT - 1:
                    head_free[b] = m5
                if qt == 0 and h + 1 < HA:
                    head_tok[h + 1] = load_head(h + 1)
        cx.flush()


DILS = (1, 4, 16)


def dil_cols(T, d, r, n):
    if d == 1:
        return T[:, n * 128:(n + 1) * 128]
    return T[:, :].rearrange("p (m d) -> p d m", d=d)[:, r, n * 128:(n + 1) * 128]


def phase_dil(nc, S, cfg, I, scr):
    SL, HA, HB = cfg.SL, cfg.HA, cfg.HB
    NB = SL // 128
    NST = SL // 2048
    with Ctx(nc, S) as cx:
        ident, t_c0 = load_const(cx, I["ident"], [128, 128], BF16, "ident")
        ones, t_c1 = load_const(cx, I["ones_bf"], [128, 128], BF16, "ones")
        dmask, t_c2 = load_const(cx, I["dmask"], [128, 6, 128], F32, "dmask")
        t_const = [t_c0, t_c1, t_c2]
        qT = [cx.sb([128, SL], BF16, "dq") for _ in range(2)]
        kT = [cx.sb([128, SL], BF16, "dk") for _ in range(2)]
        vd = [[cx.sb([128, NB, 128], BF16, "dv") for _ in range(3)] for _ in range(2)]
        qP = {4: cx.sb([128, SL], BF16, "dqp4"), 16: cx.sb([128, SL], BF16, "dqp16")}
        kP = {4: cx.sb([128, SL], BF16, "dkp4"), 16: cx.sb([128, SL], BF16, "dkp16")}
        perm_free = [None]
        perm_head = [None]
        bmf1 = cx.sb([128, 6, 128], F32, "bmf")
        bsm1 = cx.sb([128, 2, 6, 128], BF16, "bsm")
        blo1 = cx.sb([128, 6, 128], F32, "blo")
        bmf, bsm, blo = [bmf1, bmf1], [bsm1, bsm1], [blo1, blo1]
        prep_free = [None]
        bh = [cx.sb([128, 6, 512], BF16, "bh") for _ in range(2)]
        bl = [cx.sb([128, 6, 512], BF16, "bl") for _ in range(2)]
        hs = [S.slot() for _ in range(2)]
        head_free = [None, None]
        pS = Rot([cx.ps([128, 512], F32, "pS") for _ in range(3)])
        pO = Rot([cx.ps([128, 512], F32, "pO") for _ in range(2)])
        pD = Rot([cx.ps([128, 512], F32, "pD") for _ in range(2)])
        pts = Rot([cx.sb([128, 512], BF16, "pt") for _ in range(4)])
        nacc = [cx.sb([128, 2048], F32, "nacc") for _ in range(2)]
        dacc = [cx.sb([128, 2048], F32, "dacc") for _ in range(2)]
        acc_free = [None, None]
        gts = [cx.sb([128, 2048], BF16, "dgt") for _ in range(2)]
        gsl = [S.slot() for _ in range(2)]
        g_free = [None, None]
        stg = OutStage(cx, [128, 2048], BF16, n=2, name="dmix")

        def load_head(hb):
            b = hb % 2
            fr = head_free[b]
            S.dma("sp", lambda e: e.dma_start(out=qT[b][:], in_=scr["qb_T"][hb * 128:(hb + 1) * 128, :]), hs[b], [fr])
            S.dma("sp", lambda e: e.dma_start(out=kT[b][:], in_=scr["kb_T"][hb * 128:(hb + 1) * 128, :]), hs[b], [fr])
            S.dma("sp", lambda e: e.dma_start(out=bmf[b][:], in_=I["dbias"][hb]), hs[b], [fr, prep_free[0]])
            for j in range(4):
                S.dma("sp", lambda e, j=j: e.dma_start(
                    out=vd[b][0][:, j * 8:(j + 1) * 8, :],
                    in_=scr["vb"][hb, j * 1024:(j + 1) * 1024, :].rearrange("(n p) f -> p n f", p=128)), hs[b], [fr])
            tok = None
            for p, d in ((1, 4), (2, 16)):
                nper = NB // d
                src = scr["vb"][hb].rearrange("(n i r) f -> r i n f", i=128, r=d)
                for r in range(d):
                    tok = S.dma("sp", lambda e, p=p, r=r, nper=nper, src=src: e.dma_start(
                        out=vd[b][p][:, r * nper:(r + 1) * nper, :], in_=src[r]), hs[b], [fr])
            t1 = S.op("dve", lambda e: e.tensor_tensor(out=bmf[b][:], in0=bmf[b][:], in1=dmask[:], op=ALU.add), [tok, t_c2, fr])
            t2 = S.op("dve", lambda e: e.tensor_copy(out=bsm[b][:, 0], in_=bmf[b][:]), [t1])
            t3 = S.op("dve", lambda e: e.tensor_tensor(out=blo[b][:], in0=bmf[b][:], in1=bsm[b][:, 0], op=ALU.subtract), [t2])
            t4 = S.op("dve", lambda e: e.tensor_copy(out=bsm[b][:, 1], in_=blo[b][:]), [t3])
            tl = t4
            for q in range(4):
                tl = S.op("dve", lambda e, q=q: e.tensor_copy(out=bh[b][:, :, q * 128:(q + 1) * 128], in_=bsm[b][:, 0]), [t4, tl])
                tl = S.op("dve", lambda e, q=q: e.tensor_copy(out=bl[b][:, :, q * 128:(q + 1) * 128], in_=bsm[b][:, 1]), [t4, tl])
            prep_free[0] = tl
            return [tok, tl]

        def permute_head(hb):
            b = hb % 2
            toks = []
            for d in (4, 16):
                for src, dst in ((qT[b], qP[d]), (kT[b], kP[d])):
                    toks.append(S.op("pool", lambda e, src=src, dst=dst, d=d: e.tensor_copy(
                        out=dst[:, :].rearrange("p (d m) -> p d m", d=d), in_=src[:, :].rearrange("p (m d) -> p d m", d=d)),
                        [head_tok[hb], perm_free[0]]))
            perm_head[0] = hb
            return toks

        passes = []
        for hb in range(HB):
            for st in range(NST):
                for p, d in enumerate(DILS):
                    for g in range(4):
                        quarters = []
                        for q in range(4):
                            if p == 0:
                                r, n = 0, st * 16 + g * 4 + q
                            elif p == 1:
                                r, n = q, st * 4 + g
                            else:
                                r, n = g * 4 + q, st
                            quarters.append((q, r, n))
                        has_prev = [qq for qq in quarters if qq[2] >= 1]
                        passes.append(dict(hb=hb, st=st, p=p, d=d, g=g, kind=1, qs=quarters, last=(len(has_prev) == 0)))
                        if has_prev:
                            passes.append(dict(hb=hb, st=st, p=p, d=d, g=g, kind=0, qs=has_prev, last=True))
        head_tok = {0: load_head(0)}
        state = {}

        def vslot(p, d, r, n):
            return r * (NB // d) + n

        def pcols(T, TP, d, r, n):
            if d == 1:
                return T[:, n * 128:(n + 1) * 128]
            c = r * (SL // d) + n * 128
            return TP[d][:, c:c + 128]

        def issue_S(pi):
            P = passes[pi]
            hb, st, p, d, g, kind = P["hb"], P["st"], P["p"], P["d"], P["g"], P["kind"]
            b = hb % 2
            if perm_head[0] != hb:
                head_tok[hb] = [head_tok[hb], permute_head(hb)]
            i_s, Sb, sfr = pS.next()
            q0 = P["qs"][0][0]
            c0 = q0 * 128
            bi = 2 * p + kind
            S.op("pe", lambda e: e.matmul(out=Sb[:, c0:512], lhsT=ident[:], rhs=bh[b][:, bi, c0:512], start=True, stop=False,
                                         skip_group_check=True), [head_tok[hb], sfr] + t_const)
            m = S.op("pe", lambda e: e.matmul(out=Sb[:, c0:512], lhsT=ident[:], rhs=bl[b][:, bi, c0:512], start=False, stop=False,
                                             skip_group_check=True), [])
            nq = len(P["qs"])
            for j, (q, r, n) in enumerate(P["qs"]):
                nk = n if kind == 1 else n - 1
                m = S.op("pe", lambda e, q=q, r=r, n=n, nk=nk, j=j: e.matmul(
                    out=Sb[:, q * 128:(q + 1) * 128], lhsT=pcols(kT[b], kP, d, r, nk), rhs=pcols(qT[b], qP, d, r, n),
                    start=False, stop=(j == nq - 1), skip_group_check=True), [])
            state[pi] = (i_s, Sb, m, c0)

        LOOK = 2
        issued = [0]

        def issue_upto(k):
            while issued[0] <= k and issued[0] < len(passes):
                issue_S(issued[0])
                issued[0] += 1

        issue_upto(0)
        cur = {}
        acc_toks = []
        for pi, P in enumerate(passes):
            hb, st, p, d, g, kind = P["hb"], P["st"], P["p"], P["d"], P["g"], P["kind"]
            b = hb % 2
            ab = (hb * NST + st) % 2
            lim = pi
            while lim + 1 < len(passes) and lim + 1 <= pi + LOOK and passes[lim + 1]["hb"] == hb:
                lim += 1
            issue_upto(lim)
            defer = pi + 1 < len(passes) and passes[pi + 1]["hb"] != hb
            i_s, Sb, m, c0 = state.pop(pi)
            if kind == 1:
                io, O, ofr = pO.next()
                idn, Dn, dfr = pD.next()
                cur = dict(io=io, O=O, ofr=ofr, idn=idn, Dn=Dn, dfr=dfr)
                if p == 0 and g == 0:
                    acc_toks = []
                    t_gate = S.dma("sp", lambda e, ab=ab, hb=hb, st=st: e.dma_start(
                        out=gts[ab][:], in_=scr["gate_T"][(HA + hb) * 128:(HA + hb + 1) * 128, st * 2048:(st + 1) * 2048]),
                        gsl[ab], [g_free[ab]])
            O, Dn = cur["O"], cur["Dn"]
            ip, pt, pfr = pts.next()
            t_e = S.op("act", lambda e, pt=pt, Sb=Sb, c0=c0: e.activation(out=pt[:, c0:512], in_=Sb[:, c0:512], func=AF.Exp), [m, pfr])
            pS.release(i_s, t_e)
            nq = len(P["qs"])
            for j, (q, r, n) in enumerate(P["qs"]):
                nk = n if kind == 1 else n - 1
                first = (kind == 1 and j == 0)
                S.op("pe", lambda e, O=O, pt=pt, q=q, b=b, p=p, sl=vslot(p, d, r, nk), first=first, lastmm=(P["last"] and j == nq - 1): e.matmul(
                    out=O[:, q * 128:(q + 1) * 128], lhsT=vd[b][p][:, sl, :], rhs=pt[:, q * 128:(q + 1) * 128],
                    start=first, stop=lastmm, skip_group_check=True), [t_e, cur["ofr"] if first else None])
            m5 = S.op("pe", lambda e, Dn=Dn, pt=pt, c0=c0, kind=kind, lastp=P["last"]: e.matmul(
                out=Dn[:, c0:512], lhsT=ones[:], rhs=pt[:, c0:512], start=(kind == 1), stop=lastp, skip_group_check=True),
                [t_e, cur["dfr"] if kind == 1 else None])
            pts.release(ip, m5)
            if P["last"]:
                if p == 0:
                    nv = nacc[ab][:, g * 512:(g + 1) * 512]
                    dv = dacc[ab][:, g * 512:(g + 1) * 512]
                    Ov, Dv = O[:], Dn[:]
                elif p == 1:
                    nv = nacc[ab][:, g * 512:(g + 1) * 512].rearrange("p (i r) -> p r i", r=4)
                    dv = dacc[ab][:, g * 512:(g + 1) * 512].rearrange("p (i r) -> p r i", r=4)
                    Ov = O[:, :].rearrange("p (r i) -> p r i", r=4)
                    Dv = Dn[:, :].rearrange("p (r i) -> p r i", r=4)
                else:
                    nv = nacc[ab][:, :].rearrange("p (i g r) -> p g r i", g=4, r=4)[:, g]
                    dv = dacc[ab][:, :].rearrange("p (i g r) -> p g r i", g=4, r=4)[:, g]
                    Ov = O[:, :].rearrange("p (r i) -> p r i", r=4)
                    Dv = Dn[:, :].rearrange("p (r i) -> p r i", r=4)
                if p == 0:
                    ta = S.op("act", lambda e, nv=nv, Ov=Ov: e.activation(out=nv, in_=Ov, func=AF.Copy), [m5, acc_free[ab]])
                    tb_ = S.op("dve", lambda e, dv=dv, Dv=Dv: e.tensor_copy(out=dv, in_=Dv), [m5, acc_free[ab]])
                else:
                    ta = S.op("dve", lambda e, nv=nv, Ov=Ov: e.tensor_tensor(out=nv, in0=Ov, in1=nv, op=ALU.add), [m5] + acc_toks)
                    tb_ = S.op("dve", lambda e, dv=dv, Dv=Dv: e.tensor_tensor(out=dv, in0=Dv, in1=dv, op=ALU.add), [m5, ta] + acc_toks)
                acc_toks = acc_toks + [ta, tb_]
                pO.release(cur["io"], ta)
                pD.release(cur["idn"], tb_)
                if p == 2 and g == 3:
                    t_r = S.op("dve", lambda e, ab=ab: e.reciprocal(out=dacc[ab][:], in_=dacc[ab][:]), acc_toks)
                    t_o = S.op("dve", lambda e, ab=ab: e.tensor_tensor(out=nacc[ab][:], in0=nacc[ab][:], in1=dacc[ab][:], op=ALU.mult), [t_r])
                    tc = stg.put("dve", lambda e, t, ab=ab: e.tensor_tensor(out=t[:], in0=nacc[ab][:], in1=gts[ab][:], op=ALU.mult),
                                 [t_o, t_gate], scr["mix0_T"][(HA + hb) * 128:(HA + hb + 1) * 128, st * 2048:(st + 1) * 2048])
                    acc_free[ab] = tc
                    g_free[ab] = tc
                    if st == NST - 1:
                        head_free[b] = m5
                        perm_free[0] = m5
                    if st == 0 and hb + 1 < HB:
                        head_tok[hb + 1] = load_head(hb + 1)
            if defer:
                issue_upto(pi + 1)
        cx.flush()


def phase_sb(nc, S, cfg, I, scr, extra_flush=()):
    SL, HS = cfg.SL, cfg.HS
    NQT, NB = SL // 512, SL // 128
    with Ctx(nc, S) as cx:
        ident, t_c0 = load_const(cx, I["ident"], [128, 128], BF16, "ident")
        ones, t_c1 = load_const(cx, I["ones_bf"], [128, 128], BF16, "ones")
        tri, t_c2 = load_const(cx, I["tri"], [128, 128], BF16, "tri")
        mSn, t_c3 = load_const(cx, I["maskSn"], [128, 4, 512], BF16, "mSn")
        mSp, t_c4 = load_const(cx, I["maskSp"], [128, 4, 512], BF16, "mSp")
        t_const = [t_c0, t_c1, t_c2, t_c3, t_c4]
        qT = [cx.sb([128, SL], BF16, "sq") for _ in range(2)]
        kT = [cx.sb([128, SL], BF16, "sk") for _ in range(2)]
        nkT = [cx.sb([128, SL], BF16, "snk") for _ in range(2)]
        vv = [cx.sb([128, NB, 128], BF16, "sv") for _ in range(2)]
        hs = [S.slot() for _ in range(2)]
        head_free = [None, None]
        pS = Rot([cx.ps([128, 512], F32, "pS") for _ in range(2)])
        pC = Rot([cx.ps([128, 512], F32, "pC") for _ in range(2)])
        pO = Rot([cx.ps([128, 512], F32, "pO") for _ in range(2)])
        Et = Rot([cx.sb([128, 512], F32, "Et") for _ in range(2)])
        Lt = Rot([cx.sb([128, 512], BF16, "Lt") for _ in range(4)])
        Ls = Rot([cx.sb([128, 512], BF16, "Ls") for _ in range(3)])
        At = Rot([cx.sb([128, 512], BF16, "At") for _ in range(3)])
        gts = Rot([cx.sb([128, 512], BF16, "sgt") for _ in range(2)])
        gsl = [S.slot() for _ in range(2)]
        stg = OutStage(cx, [128, 512], BF16, n=2, name="smix")

        def load_head(h):
            b = h % 2
            fr = head_free[b]
            S.dma("sp", lambda e: e.dma_start(out=qT[b][:], in_=scr["q1_T"][h * 128:(h + 1) * 128, :]), hs[b], [fr])
            S.dma("sp", lambda e: e.dma_start(out=kT[b][:], in_=scr["k1_T"][h * 128:(h + 1) * 128, :]), hs[b], [fr])
            tok = None
            for j in range(4):
                tok = S.dma("sp", lambda e, j=j: e.dma_start(
                    out=vv[b][:, j * 8:(j + 1) * 8, :],
                    in_=scr["v1"][h, j * 1024:(j + 1) * 1024, :].rearrange("(n p) f -> p n f", p=128)), hs[b], [fr])
            t_nk = S.op("pool", lambda e: e.tensor_scalar(out=nkT[b][:], in0=kT[b][:], scalar1=-1.0, scalar2=None, op0=ALU.mult),
                        [tok, fr])
            return [tok, t_nk]

        blocks = []
        for h in range(HS):
            for qt in range(NQT):
                for kb in range(4 * qt + 3, -1, -1):
                    blocks.append((h, qt, kb))
        NBK = len(blocks)
        head_tok = {0: load_head(0)}
        st1, st2, st3 = {}, {}, {}
        first_il = [None]
        first_users = [[]]
        ls_cur = {}
        ocur = {}

        def stage_S(bi):
            h, qt, kb = blocks[bi]
            b = h % 2
            i_s, Sb, sfr = pS.next()
            diag = kb >= 4 * qt
            m = S.op("pe", lambda e: e.matmul(out=Sb[:], lhsT=kT[b][:, kb * 128:(kb + 1) * 128], rhs=qT[b][:, qt * 512:(qt + 1) * 512],
                                             start=True, stop=(not diag)), [head_tok[h], sfr] + t_const)
            if diag:
                m = S.op("pe", lambda e: e.matmul(out=Sb[:], lhsT=ident[:], rhs=mSn[:, kb - 4 * qt, :], start=False, stop=True), [])
            st1[bi] = (i_s, Sb, m)

        def stage_EL(bi):
            h, qt, kb = blocks[bi]
            i_s, Sb, m = st1.pop(bi)
            ie, E, efr = Et.next()
            il, L, lfr = Lt.next()
            t_e = S.op("act", lambda e: e.activation(out=E[:], in_=Sb[:], func=AF.Exp), [m, efr])
            pS.release(i_s, t_e)
            t_l = S.op("act", lambda e: e.activation(out=L[:], in_=E[:], func=AF.Ln, bias=1.0, scale=1.0), [t_e])
            Et.release(ie, t_l)
            st2[bi] = (il, L, t_l)

        def stage_C(bi):
            h, qt, kb = blocks[bi]
            b = h % 2
            il, L, t_l = st2.pop(bi)
            first = kb == 4 * qt + 3
            last = kb == 0
            diag = kb >= 4 * qt
            i_c, Cb, cfr = pC.next()
            S.op("pe", lambda e: e.matmul(out=Cb[:], lhsT=tri[:], rhs=L[:], start=True, stop=False), [t_l, cfr])
            m = S.op("pe", lambda e: e.matmul(out=Cb[:], lhsT=nkT[b][:, kb * 128:(kb + 1) * 128], rhs=qT[b][:, qt * 512:(qt + 1) * 512],
                                             start=False, stop=(not diag and first)), [])
            if diag:
                m = S.op("pe", lambda e: e.matmul(out=Cb[:], lhsT=ident[:], rhs=mSp[:, kb - 4 * qt, :], start=False, stop=first), [])
            users = [m]
            if not first:
                ils, Lsum, t_ls = ls_cur[bi]
                m = S.op("pe", lambda e: e.matmul(out=Cb[:], lhsT=ones[:], rhs=Lsum[:], start=False, stop=True), [t_ls])
                users = [m]
            if not last:
                if first:
                    ls_cur[bi + 1] = (None, L, t_l)
                    users.append(("hold",))
                else:
                    iln, Lnew, lnfr = Ls.next()
                    t_n = S.op("dve", lambda e: e.tensor_tensor(out=Lnew[:], in0=Lsum[:], in1=L[:], op=ALU.add), [t_l, t_ls])
                    ls_cur[bi + 1] = (iln, Lnew, t_n)
                    users.append(t_n)
            if not first:
                if ils is not None:
                    Ls.release(ils, [u for u in users if u != ("hold",)])
                else:
                    Lt.release(first_il[0], [u for u in users if u != ("hold",)] + first_users[0])
                ls_cur.pop(bi)
            if ("hold",) in users:
                first_il[0] = il
                first_users[0] = [u for u in users if u != ("hold",)]
            else:
                Lt.release(il, list(users))
            st3[bi] = (i_c, Cb, m)

        def stage_A(bi):
            i_c, Cb, m = st3.pop(bi)
            ia, A, afr = At.next()
            t_a = S.op("act", lambda e: e.activation(out=A[:], in_=Cb[:], func=AF.Exp, scale=-1.0), [m, afr])
            pC.release(i_c, t_a)
            st3[("A", bi)] = (ia, A, t_a)

        def stage_PV(bi):
            h, qt, kb = blocks[bi]
            b = h % 2
            ia, A, t_a = st3.pop(("A", bi))
            first = kb == 4 * qt + 3
            last = kb == 0
            if first:
                io, O, ofr = pO.next()
                gi, gt, gfr = gts.next()
                t_gate = S.dma("sp", lambda e: e.dma_start(
                    out=gt[:], in_=scr["gate1_T"][h * 128:(h + 1) * 128, qt * 512:(qt + 1) * 512]), gsl[gi], [gfr])
                ocur.update(io=io, O=O, ofr=ofr, gi=gi, gt=gt, t_gate=t_gate)
            O = ocur["O"]
            m = S.op("pe", lambda e: e.matmul(out=O[:], lhsT=vv[b][:, kb, :], rhs=A[:], start=first, stop=last),
                     [t_a, ocur["ofr"] if first else None])
            At.release(ia, m)
            if last:
                gt = ocur["gt"]
                tc = stg.put("dve", lambda e, t: e.tensor_tensor(out=t[:], in0=O[:], in1=gt[:], op=ALU.mult),
                             [m, ocur["t_gate"]], scr["mix1_T"][h * 128:(h + 1) * 128, qt * 512:(qt + 1) * 512])
                pO.release(ocur["io"], tc)
                gts.release(ocur["gi"], tc)
                if qt == NQT - 1:
                    head_free[b] = m
                if qt == 0 and h + 1 < HS:
                    head_tok[h + 1] = load_head(h + 1)

        for step in range(-2, NBK):
            if 0 <= step + 2 < NBK:
                stage_S(step + 2)
                stage_EL(step + 2)
            if 0 <= step + 1 < NBK:
                stage_C(step + 1)
                stage_A(step + 1)
            if 0 <= step < NBK:
                stage_PV(step)
        cx.flush(extra_flush)


def phase_gather(nc, S, cfg, src, dst):
    groups = [[cfg.NR * i + j for j in range(cfg.NR)] for i in range(8 // cfg.NR)]
    sl = S.slot("pool")
    t = S.dma("pool", lambda e: e.collective_compute("AllGather", ALU.bypass, replica_groups=groups, ins=[src], outs=[dst]), sl)
    S.op("sp", lambda e: e.nop(), [t])
    S.emit()


def load_wo(S, wo, wo_dram):
    sw = S.slot("pool")
    t_w = None
    for cg in range(4):
        t_w = S.dma("pool", lambda e, cg=cg: e.dma_start(
            out=wo[:, :, cg * 512:(cg + 1) * 512],
            in_=wo_dram[:, cg * 512:(cg + 1) * 512].rearrange("(k p) n -> p k n", p=128)), sw)
    return t_w


def phase_out(nc, S, cfg, mixT_dram, wo_dram, xin_dram, xout_dram, final_gain=None, out_dram=None, wo_pre=None):
    SL, D = cfg.SL, cfg.D
    KC = D // 128
    with Ctx(nc, S) as cx:
        if wo_pre is not None:
            wo, t_w = wo_pre
        else:
            wo = cx.sb([128, KC, D], BF16, "wo")
            t_w = load_wo(S, wo, wo_dram)
        mts = Rot([cx.sb([128, KC, 128], BF16, "mt") for _ in range(3)])
        msl = [S.slot() for _ in range(3)]
        xts = Rot([cx.sb([128, D], F32, "xo") for _ in range(2)])
        xsl = [S.slot() for _ in range(2)]
        banks = [cx.ps([128, 512], F32, "po") for _ in range(8)]
        bfree = [None] * 8
        ystg = OutStage(cx, [128, D], F32, n=2, name="yst", dma_eng="sp")
        if final_gain is not None:
            gft, t_g = load_const(cx, final_gain.partition_broadcast(128), [128, D], F32, "gft")
            junk = cx.sb([128, D], BF16, "junk")
            ss = cx.sb([128, SL // 128], F32, "ss")
            rs = cx.sb([128, SL // 128], F32, "rs")
            ostg = OutStage(cx, [128, D], F32, n=2, name="ost", dma_eng="sp")
            ysb = Rot([cx.sb([128, D], F32, "ysb") for _ in range(2)])
        for tb in range(SL // 128):
            im, mt, mfr = mts.next()
            t_m = S.dma("sp", lambda e, mt=mt, tb=tb: e.dma_start(
                out=mt[:], in_=mixT_dram[:, tb * 128:(tb + 1) * 128].rearrange("(k p) t -> p k t", p=128)), msl[im], [mfr])
            ix, xt, xfr = xts.next()
            t_x = S.dma("sp", lambda e, xt=xt, tb=tb: e.dma_start(out=xt[:], in_=xin_dram[tb * 128:(tb + 1) * 128, :]), xsl[ix], [xfr])
            mms = []
            for cg in range(4):
                bi = (tb % 2) * 4 + cg
                bank = banks[bi]
                tm = None
                for kc in range(KC):
                    tm = S.op("pe", lambda e, bank=bank, mt=mt, kc=kc, cg=cg: e.matmul(
                        out=bank[:], lhsT=mt[:, kc, :], rhs=wo[:, kc, cg * 512:(cg + 1) * 512],
                        start=(kc == 0), stop=(kc == KC - 1)), [t_m, t_w, bfree[bi]])
                mms.append(tm)
            mts.release(im, mms[-1])
            if final_gain is None:
                comps = []
                for cg in range(4):
                    bank = banks[(tb % 2) * 4 + cg]
                    comps.append(("dve", lambda e, t, bank=bank, xt=xt, cg=cg: e.tensor_tensor(
                        out=t[:, cg * 512:(cg + 1) * 512], in0=bank[:], in1=xt[:, cg * 512:(cg + 1) * 512], op=ALU.add),
                        [mms[cg], t_x]))
                toks = ystg.put_multi(comps, xout_dram[tb * 128:(tb + 1) * 128, :])
                for cg in range(4):
                    bfree[(tb % 2) * 4 + cg] = toks[cg]
                xts.release(ix, toks[-1])
            else:
                iy, y, yfr = ysb.next()
                tl = yfr
                toks = []
                for cg in range(4):
                    bank = banks[(tb % 2) * 4 + cg]
                    tl = S.op("dve", lambda e, y=y, bank=bank, xt=xt, cg=cg: e.tensor_tensor(
                        out=y[:, cg * 512:(cg + 1) * 512], in0=bank[:], in1=xt[:, cg * 512:(cg + 1) * 512], op=ALU.add),
                        [mms[cg], t_x, tl])
                    bfree[(tb % 2) * 4 + cg] = tl
                    toks.append(tl)
                xts.release(ix, tl)
                t_ss = S.op("act", lambda e, y=y, tb=tb: e.activation(out=junk[:], in_=y[:], func=AF.Square,
                                                                      accum_out=ss[:, tb:tb + 1]), [tl])
                t_a = S.op("act", lambda e, tb=tb: e.activation(out=rs[:, tb:tb + 1], in_=ss[:, tb:tb + 1], func=AF.Sqrt,
                                                                bias=EPS, scale=1.0 / D), [t_ss])
                t_r = S.op("dve", lambda e, tb=tb: e.reciprocal(out=rs[:, tb:tb + 1], in_=rs[:, tb:tb + 1]), [t_a])
                tc = ostg.put("dve", lambda e, t, y=y, tb=tb: e.scalar_tensor_tensor(
                    out=t[:], in0=y[:], scalar=rs[:, tb:tb + 1], in1=gft[:], op0=ALU.mult, op1=ALU.mult),
                    [t_r, t_g], out_dram[tb * 128:(tb + 1) * 128, :])
                ysb.release(iy, tc)
        cx.flush()


def phase_proj1(nc, S, cfg, hT, w1, scr):
    SL, HS = cfg.SL, cfg.HS
    KC = cfg.D // 128
    NTT, NTB = SL // 512, SL // 128
    with Ctx(nc, S) as cx:
        pj = Proj(cx, cfg, hT, KC, None)
        ws = WStream(cx, KC)
        st16 = OutStage(cx, [128, 512], BF16, n=6, name="st16")
        groups = []
        c = 0
        for g in range(HS * 128 // 512):
            groups.append((c + g * 512, "T", [(scr["q1_T"], g * 512, 128.0 ** -0.5, None)]))
        c += HS * 128
        for g in range(HS * 128 // 512):
            groups.append((c + g * 512, "T", [(scr["k1_T"], g * 512, 1.0, None)]))
        c += HS * 128
        for g in range(HS * 128 // 512):
            groups.append((c + g * 512, "N", (scr["v1"], g * 4)))
        c += HS * 128
        for g in range(cfg.FM1 // 512):
            groups.append((c + g * 512, "T", [(scr["gate1_T"], g * 512, 1.0, AF.Silu)]))
        nxt = ws.load(w1[:, groups[0][0]:groups[0][0] + 512], 512)
        nev = 0
        for gi, (c0, kind, spec) in enumerate(groups):
            wi, wt, wtok = nxt
            if gi + 1 < len(groups):
                n0 = groups[gi + 1][0]
                nxt = ws.load(w1[:, n0:n0 + 512], 512)
            last = None
            if kind == "T":
                for cb in range(4):
                    for tt in range(NTT):
                        i, bank, tm = pj.mm_T(wt, cb * 128, 128, tt, wtok)
                        last = tm
                        tcs = []
                        for (dst, r0, scale, func) in spec:
                            eng = "act" if (func is not None or nev % 2 == 0) else "dve"
                            nev += 1
                            tcs.append(st16.put(eng, lambda e, t, eng=eng, bank=bank, scale=scale, func=func: evac(
                                eng, e, t[:], bank[:], scale, func), [tm],
                                dst[r0 + cb * 128:r0 + (cb + 1) * 128, tt * 512:(tt + 1) * 512]))
                        pj.banks.release(i, tcs)
            else:
                dst, h0 = spec
                for tb in range(NTB):
                    i, bank, tm = pj.mm_N(wt, 0, 512, tb, wtok)
                    last = tm
                    eng = "act" if nev % 2 == 0 else "dve"
                    nev += 1
                    tc = st16.put(eng, lambda e, t, eng=eng, bank=bank: evac(eng, e, t[:], bank[:]), [tm],
                                  dst[h0:h0 + 4, tb * 128:(tb + 1) * 128, :].rearrange("h p f -> p h f"),
                                  sub=lambda t: t[:].rearrange("p (h f) -> p h f", h=4))
                    pj.banks.release(i, tc)
            ws.release(wi, last)
        cx.flush()


def const_arrays(cfg):
    SL = cfg.SL
    bf = ml_dtypes.bfloat16
    c = {}
    c["ident"] = np.eye(128, dtype=np.float32).astype(bf)
    c["ones_bf"] = np.ones((128, 128), np.float32).astype(bf)
    c["ones32"] = np.ones((128, 128), np.float32)
    j = np.arange(128)[:, None]
    s = np.arange(128)[None, :]
    c["tri"] = (j >= s).astype(np.float32).astype(bf)
    t = np.arange(512)[None, None, :]
    i = np.arange(4)[None, :, None]
    jj = np.arange(128)[:, None, None]
    c["maskA"] = np.where(128 * i + jj <= t, 0.0, NEG).astype(np.float32).astype(bf)
    mneg = np.where(128 * i + jj < t, 0.0, NEG).astype(np.float32)
    c["maskSn"] = mneg.astype(bf)
    c["maskSp"] = (-mneg).astype(bf)
    qi = np.arange(128)[None, :]
    kj = np.arange(128)[:, None]
    mprev = np.where(kj >= qi, 0.0, NEG)
    mcur = np.where(kj <= qi, 0.0, NEG)
    dm = np.zeros((128, 6, 128), np.float32)
    for p in range(3):
        dm[:, 2 * p, :] = mprev
        dm[:, 2 * p + 1, :] = mcur
    c["dmask"] = dm
    half = 32
    inv = 1.0 / (10000.0 ** (np.arange(half, dtype=np.float32) / half))
    ang = np.arange(SL, dtype=np.float32)[None, :] * inv[:, None]
    ang = ang.astype(np.float32)
    cs = np.zeros((2, 64, SL), np.float32)
    cs[0, :32] = np.cos(ang)
    cs[0, 32:] = np.cos(ang)
    cs[1, :32] = np.sin(ang)
    cs[1, 32:] = np.sin(ang)
    c["cs"] = cs
    return c


def t5_bucket_np(dist):
    max_exact = 16
    d = np.maximum(dist.astype(np.float32), 1.0)
    large = max_exact + (np.log(d / max_exact) / math.log(2048 / max_exact) * (32 - max_exact)).astype(np.int32)
    large = np.minimum(large, 31)
    return np.where(dist < max_exact, dist, large)


def dil_bias_index():
    qi = np.arange(128)[None, :]
    kj = np.arange(128)[:, None]
    idx = np.zeros((6, 128, 128), np.int64)
    for p, d in enumerate((1, 4, 16)):
        rel_prev = np.maximum(128 + qi - kj, 0)
        rel_cur = np.maximum(qi - kj, 0)
        idx[2 * p] = t5_bucket_np((rel_prev * d).astype(np.int32))
        idx[2 * p + 1] = t5_bucket_np((rel_cur * d).astype(np.int32))
    return idx


def build_program(cfg, phases, debug=()):
    nc = bass.Bass("TRN2", target_bir_lowering=False)
    SL, D, HA, HB, HS = cfg.SL, cfg.D, cfg.HA, cfg.HB, cfg.HS

    def din(name, shape, dt=F32):
        return nc.dram_tensor(name, list(shape), dt, kind="ExternalInput").ap()

    def dscr(name, shape, dt):
        kind = "ExternalOutput" if name in debug else "Internal"
        return nc.dram_tensor(name, list(shape), dt, kind=kind).ap()

    I = {}
    I["x"] = din("x", [SL, D])
    I["g0"] = din("g0", [1, D])
    I["g1"] = din("g1", [1, D])
    I["gf"] = din("gf", [1, D])
    I["w0"] = din("w0", [D, cfg.W0C])
    I["qg"] = din("qg", [128, 4])
    I["kvg"] = din("kvg", [128, 4])
    I["wuq"] = din("wuq", [512, HA * 192])
    I["wukv"] = din("wukv", [512, HA * 256])
    I["wo0"] = din("wo0", [D, D])
    I["dbias"] = din("dbias", [HB, 128, 6, 128])
    I["w1"] = din("w1", [D, cfg.W1C])
    I["wo1"] = din("wo1", [D, D])
    I["ident"] = din("ident", [128, 128], BF16)
    I["ones_bf"] = din("ones_bf", [128, 128], BF16)
    I["ones32"] = din("ones32", [128, 128], F32)
    I["tri"] = din("tri", [128, 128], BF16)
    I["maskA"] = din("maskA", [128, 4, 512], BF16)
    I["maskSn"] = din("maskSn", [128, 4, 512], BF16)
    I["maskSp"] = din("maskSp", [128, 4, 512], BF16)
    I["dmask"] = din("dmask", [128, 6, 128], F32)
    I["cs"] = din("cs", [2, 64, SL], F32)
    out = nc.dram_tensor("out", [SL, D], F32, kind="ExternalOutput").ap()

    scr = {}
    scr["cq_T"] = dscr("cq_T", [512, SL], F32)
    scr["ckv_T"] = dscr("ckv_T", [512, SL], F32)
    rows0 = 64 + 3 * HB * 128 + cfg.FM0 + HA * 192 + 2 * HA * 128 + cfg.FM0
    rows1 = 3 * HS * 128 + 2 * cfg.FM1
    assert cfg.FM0 == cfg.FM1
    arena = dscr("arena16", [max(rows0, rows1), SL], BF16)
    pos = [0]

    def carve(nrows):
        v = arena[pos[0]:pos[0] + nrows, :]
        pos[0] += nrows
        return v

    def tokmajor(v, h):
        return v.rearrange("(h a) (b f) -> h (a b) f", h=h, f=128)

    scr["kr_T"] = carve(64)
    scr["qb_T"] = carve(HB * 128)
    scr["kb_T"] = carve(HB * 128)
    scr["vb"] = tokmajor(carve(HB * 128), HB)
    scr["gate_T"] = carve(cfg.FM0)
    scr["qa_T"] = carve(HA * 192).rearrange("(h r) s -> h r s", h=HA)
    scr["ka_T"] = carve(HA * 128).rearrange("(h r) s -> h r s", h=HA)
    scr["va"] = tokmajor(carve(HA * 128), HA)
    if cfg.NR == 1:
        scr["mix0_T"] = carve(cfg.FM0)
        scr["mixg0_T"] = scr["mix0_T"]
    else:
        cc_in = nc.dram_tensor("cc_in", [cfg.FM0, SL], BF16).ap()
        cc_out = nc.dram_tensor("cc_out", [cfg.NR * cfg.FM0, SL], BF16).ap()
        scr["mix0_T"] = cc_in
        scr["mixg0_T"] = cc_out
    pos[0] = 0
    scr["q1_T"] = carve(HS * 128)
    scr["k1_T"] = carve(HS * 128)
    scr["v1"] = tokmajor(carve(HS * 128), HS)
    scr["gate1_T"] = carve(cfg.FM1)
    if cfg.NR == 1:
        scr["mix1_T"] = carve(cfg.FM1)
        scr["mixg1_T"] = scr["mix1_T"]
    else:
        scr["mix1_T"] = cc_in
        scr["mixg1_T"] = cc_out
    scr["x1"] = out

    with contextlib.ExitStack() as es:
        S = Sched(nc, es)
        gcx = Ctx(nc, S)
        es.enter_context(gcx)
        ident = gcx.sb([128, 128], BF16, "ident")
        sl = S.slot()
        t_id = S.dma("sp", lambda e: e.dma_start(out=ident[:], in_=I["ident"]), sl)
        S.op("sp", lambda e: e.nop(), [t_id])
        S.emit()

        if "n0" in phases:
            hcx = Ctx(nc, S)
            hcx.__enter__()
            hT = hcx.sb([128, D // 128, SL], BF16, "hT")
            phase_norm(nc, S, cfg, I["x"], I["g0"], hT, ident)
            if "p0" in phases:
                phase_proj0(nc, S, cfg, hT, None, I["w0"], I["cs"], scr)
            hcx.__exit__(None, None, None)
        for ph in phases:
            if ph.startswith("dummy"):
                S.op("sp", lambda e: e.nop(), [])
                S.emit()
        if "u0" in phases:
            phase_up(nc, S, cfg, I["wuq"], I["wukv"], I["qg"], I["kvg"], I["cs"], I["ones32"], scr)
        if "a0" in phases:
            phase_mla(nc, S, cfg, I, scr)
        if "b0" in phases:
            phase_dil(nc, S, cfg, I, scr)
        if "o0" in phases:
            if cfg.NR > 1:
                phase_gather(nc, S, cfg, scr["mix0_T"], scr["mixg0_T"])
            phase_out(nc, S, cfg, scr["mixg0_T"], I["wo0"], I["x"], scr["x1"])
        if "n1" in phases:
            hcx = Ctx(nc, S)
            hcx.__enter__()
            hT = hcx.sb([128, D // 128, SL], BF16, "hT1")
            phase_norm(nc, S, cfg, I["x"] if "n1x" in phases else scr["x1"], I["g1"], hT, ident)
            if "p1" in phases:
                phase_proj1(nc, S, cfg, hT, I["w1"], scr)
            hcx.__exit__(None, None, None)
        wo_pre = None
        wcx = None
        if "s1" in phases and "o1" in phases:
            wcx = Ctx(nc, S)
            wcx.__enter__()
            wo1 = wcx.sb([128, D // 128, D], BF16, "wo1")
            wo_pre = (wo1, load_wo(S, wo1, I["wo1"]))
        if "s1" in phases:
            phase_sb(nc, S, cfg, I, scr, extra_flush=[wo_pre[1]] if wo_pre else ())
        if "o1" in phases:
            if cfg.NR > 1:
                phase_gather(nc, S, cfg, scr["mix1_T"], scr["mixg1_T"])
            phase_out(nc, S, cfg, scr["mixg1_T"], I["wo1"], I["x"] if "n1x" in phases else scr["x1"], None,
                      final_gain=I["gf"], out_dram=out, wo_pre=wo_pre)
        if wcx is not None:
            wcx.__exit__(None, None, None)
    return nc


def make_in_maps(cfg, inp, ncores=8):
    HA, HB, HS, NR = cfg.HA, cfg.HB, cfg.HS, cfg.NR
    consts = const_arrays(cfg)
    bidx = dil_bias_index()
    f32 = np.float32
    x = np.asarray(inp["x"], f32)
    wie = np.asarray(inp["w_in_even"], f32)[0]
    wio = np.asarray(inp["w_in_odd"], f32)[0]
    wuq = np.asarray(inp["w_uq"], f32)[0]
    wukv = np.asarray(inp["w_ukv"], f32)[0]
    woe = np.asarray(inp["w_out_even"], f32)[0]
    woo = np.asarray(inp["w_out_odd"], f32)[0]
    rb = np.asarray(inp["rel_bias"], f32)
    ng = np.asarray(inp["norm_gain"], f32)
    rows0 = []
    for r in range(NR):
        rows0.extend(range(r * HA * 128, (r + 1) * HA * 128))
        rows0.extend(range(1024 + r * HB * 128, 1024 + (r + 1) * HB * 128))
    rows0 = np.array(rows0)
    maps = []
    for c in range(ncores):
        b = (c // NR) % x.shape[0]
        p = c % NR
        cols = list(range(0, 1088))
        for base in (1088, 2112, 3136):
            cols.extend(range(base + p * HB * 128, base + (p + 1) * HB * 128))
        cols.extend(range(4160 + p * HA * 128, 4160 + (p + 1) * HA * 128))
        cols.extend(range(4160 + 1024 + p * HB * 128, 4160 + 1024 + (p + 1) * HB * 128))
        cols1 = []
        for base in (0, 2048, 4096, 6144):
            cols1.extend(range(base + p * HS * 128, base + (p + 1) * HS * 128))
        heads_b = np.arange(p * HB, (p + 1) * HB)
        db = rb[bidx][:, :, :, heads_b]
        db = np.ascontiguousarray(db.transpose(3, 1, 0, 2))
        m = {
            "x": np.ascontiguousarray(x[b]),
            "g0": np.ascontiguousarray(ng[0][None, :]),
            "g1": np.ascontiguousarray(ng[1][None, :]),
            "gf": np.ascontiguousarray(np.asarray(inp["final_norm_gain"], f32)[None, :]),
            "w0": np.ascontiguousarray(wie[:, cols]),
            "qg": np.ascontiguousarray(np.asarray(inp["q_norm_gain"], f32)[0].reshape(4, 128).T),
            "kvg": np.ascontiguousarray(np.asarray(inp["kv_norm_gain"], f32)[0].reshape(4, 128).T),
            "wuq": np.ascontiguousarray(wuq[:, p * HA * 192:(p + 1) * HA * 192]),
            "wukv": np.ascontiguousarray(wukv[:, p * HA * 256:(p + 1) * HA * 256]),
            "wo0": np.ascontiguousarray(woe[rows0, :]),
            "dbias": db,
            "w1": np.ascontiguousarray(wio[:, cols1]),
            "wo1": np.ascontiguousarray(woo),
        }
        m.update(consts)
        maps.append(m)
    return maps


ALL_PHASES = ("n0", "p0", "u0", "a0", "b0", "o0", "n1", "p1", "s1", "o1")
PH_A = ("n0", "p0", "u0", "a0", "b0", "o0")
PH_B = ("n1x", "n1", "p1", "s1", "o1")


def kernel(**inputs):
    cfg = Cfg(NR=1)
    nb = np.asarray(inputs["x"]).shape[0]
    ncores = nb * cfg.NR
    nc = build_program(cfg, ALL_PHASES)
    maps = make_in_maps(cfg, inputs, ncores=ncores)
    res = run_bass_kernel_spmd(nc, maps, core_ids=list(range(ncores)))
    outs = [np.asarray(res.results[c]["out"], dtype=np.float32) for c in range(0, ncores, cfg.NR)]
    return np.stack(outs, 0)
```

```python
import contextlib
import math
import numpy as np
import ml_dtypes
import concourse.bass as bass
import concourse.mybir as mybir
from concourse.bass_utils import run_bass_kernel_spmd

F32 = mybir.dt.float32
BF16 = mybir.dt.bfloat16
AF = mybir.ActivationFunctionType
ALU = mybir.AluOpType

ENGS = ["pe", "act", "dve", "pool", "sp"]
NEG = -30000.0
EMBED_WAIT = True
SB_PAIRED = True
EPS = 1e-6


class Slot:
    def __init__(self, sem):
        self.sem = sem
        self.count = 0


class Sched:
    def __init__(self, nc, es, same_engine_sync=("act", "dve", "pool")):
        self.nc = nc
        self.es = es
        self.same = set(same_engine_sync)
        self.ops = {e: [] for e in ENGS}
        self.phase_id = 0
        self._begin_phase()

    def _begin_phase(self):
        self.phase_id += 1
        if not hasattr(self, "pool"):
            self.pool = {"eng": [], "sp": [], "pool": []}
            self.semval = {}
        self.pool_pos = {k: 0 for k in self.pool}
        self.sem = {}
        self.sem_key = {}
        self.cnt = {}
        for e in ENGS:
            self.sem[e], self.sem_key[e] = self._alloc("eng")
            self.cnt[e] = self.semval[self.sem_key[e]]
        self.waited = {e: {} for e in ENGS}
        self.slots = []

    def _alloc(self, kind):
        pool = self.pool[kind]
        i = self.pool_pos[kind]
        if i >= len(pool):
            pool.append(self.es.enter_context(self.nc.semaphore("sem_%s_%d" % (kind, len(pool)))))
            self.semval[(kind, i)] = 0
        self.pool_pos[kind] += 1
        return pool[i], (kind, i)

    def slot(self, kind="sp"):
        h, key = self._alloc(kind)
        sl = Slot(h)
        sl.count = self.semval[key]
        sl.key = key
        sl.kind = kind
        self.slots.append(sl)
        return sl

    def op(self, eng, fn, deps=()):
        self.cnt[eng] += 1
        tok = ("e", eng, self.cnt[eng])
        self.ops[eng].append((fn, self._flat(deps), tok))
        return tok

    def dma(self, eng, fn, slot, deps=()):
        assert slot.kind == eng, (slot.kind, eng)
        slot.count += 16
        tok = ("d", slot, slot.count)
        self.ops[eng].append((fn, self._flat(deps), tok))
        return tok

    def _flat(self, deps):
        out = []
        for d in deps:
            if d is None:
                continue
            if isinstance(d, list):
                out.extend(self._flat(d))
            else:
                out.append(d)
        return out

    def _emit_engine(self, eng, e):
        waited = self.waited[eng]
        for fn, deps, tok in self.ops[eng]:
            need = {}
            for d in deps:
                if d[0] == "e":
                    if d[1] == eng and (eng not in self.same or len(d) > 3):
                        continue
                    key = ("e", d[1])
                    sem = self.sem[d[1]]
                else:
                    key = ("d", id(d[1]))
                    sem = d[1].sem
                if waited.get(key, 0) >= d[2]:
                    continue
                if key not in need or need[key][1] < d[2]:
                    need[key] = (sem, d[2])
            items = list(need.items())
            for key, (sem, val) in items:
                waited[key] = val
            for key, (sem, val) in items[:-1]:
                e.wait_ge(sem, val)
            inst = fn(e)
            if items:
                sem, val = items[-1][1]
                if EMBED_WAIT:
                    inst._wait_ge(sem, val)
                else:
                    raise RuntimeError("standalone wait must precede instruction")
            if tok[0] == "e":
                inst.then_inc(self.sem[eng], 1)
            else:
                inst.then_inc(tok[1].sem, 16)
        self.ops[eng] = []

    def emit(self):
        nc = self.nc
        with nc.Block() as block:
            @block.tensor
            def _(e):
                self._emit_engine("pe", e)

            @block.scalar
            def _(e):
                self._emit_engine("act", e)

            @block.vector
            def _(e):
                self._emit_engine("dve", e)

            @block.gpsimd
            def _(e):
                self._emit_engine("pool", e)

            @block.sync
            def _(e):
                self._emit_engine("sp", e)
        for e in ENGS:
            self.semval[self.sem_key[e]] = self.cnt[e]
        for sl in self.slots:
            self.semval[sl.key] = sl.count
        self._begin_phase()


def war(tok):
    if tok is None:
        return None
    if isinstance(tok, list):
        return [war(t) for t in tok]
    if tok[0] == "e" and len(tok) == 3:
        return tok + ("war",)
    return tok


class Rot:
    def __init__(self, bufs):
        self.bufs = bufs
        self.free = [None] * len(bufs)
        self.k = 0

    def next(self):
        i = self.k % len(self.bufs)
        self.k += 1
        return i, self.bufs[i], war(self.free[i])

    def release(self, i, tok):
        self.free[i] = tok


class Ctx:
    uid = 0

    def __init__(self, nc, S):
        self.nc = nc
        self.S = S
        self.es = contextlib.ExitStack()
        self.n = 0
        self.stages = []

    def __enter__(self):
        self.es.__enter__()
        return self

    def __exit__(self, *a):
        return self.es.__exit__(*a)

    def sb(self, shape, dt, name=None):
        Ctx.uid += 1
        return self.es.enter_context(self.nc.sbuf_tensor("%s_%d" % (name or "t", Ctx.uid), shape, dt))

    def ps(self, shape, dt, name=None):
        Ctx.uid += 1
        return self.es.enter_context(self.nc.psum_tensor("%s_%d" % (name or "p", Ctx.uid), shape, dt))

    def flush(self, extra=()):
        toks = list(extra)
        for st in self.stages:
            toks.extend([t for t in st.last if t is not None])
        self.S.op("sp", lambda e: e.nop(), toks)
        self.S.emit()


class OutStage:
    def __init__(self, cx, shape, dt, n=3, dma_eng="pool", name="stg"):
        self.S = cx.S
        self.tiles = [cx.sb(shape, dt, name) for _ in range(n)]
        self.slots = [cx.S.slot(dma_eng) for _ in range(n)]
        self.last = [None] * n
        self.k = 0
        self.dma_eng = dma_eng
        cx.stages.append(self)

    def put(self, eng, compute, deps, dram_ap, sub=None):
        i = self.k % len(self.tiles)
        self.k += 1
        t = self.tiles[i]
        tc = self.S.op(eng, lambda e: compute(e, t), list(deps) + [self.last[i]])
        src = t[:] if sub is None else sub(t)
        self.last[i] = self.S.dma(self.dma_eng, lambda e: e.dma_start(out=dram_ap, in_=src), self.slots[i], [tc])
        return tc

    def put_multi(self, computes, dram_ap, sub=None):
        i = self.k % len(self.tiles)
        self.k += 1
        t = self.tiles[i]
        toks = []
        prev = self.last[i]
        for eng, fn, deps in computes:
            prev = self.S.op(eng, lambda e, fn=fn: fn(e, t), list(deps) + [prev])
            toks.append(prev)
        src = t[:] if sub is None else sub(t)
        self.last[i] = self.S.dma(self.dma_eng, lambda e: e.dma_start(out=dram_ap, in_=src), self.slots[i], [prev])
        return toks


def evac(eng, e, out, in_, scale=1.0, func=None):
    if eng == "act":
        return e.activation(out=out, in_=in_, func=(func or AF.Copy), scale=scale)
    assert func is None
    if scale == 1.0:
        return e.tensor_copy(out=out, in_=in_)
    return e.tensor_scalar(out=out, in0=in_, scalar1=float(scale), scalar2=None, op0=ALU.mult)


class Cfg:
    SL = 4096
    D = 2048

    def __init__(self, NR=1):
        self.NR = NR
        self.HA = 8 // NR
        self.HB = 8 // NR
        self.HS = 16 // NR

    @property
    def FM0(self):
        return (self.HA + self.HB) * 128

    @property
    def FM1(self):
        return self.HS * 128

    @property
    def W0C(self):
        return 1088 + 3 * self.HB * 128 + self.FM0

    @property
    def W1C(self):
        return 4 * self.HS * 128


def phase_norm(nc, S, cfg, x_dram, g_dram, hT, ident):
    SL, D = cfg.SL, cfg.D
    KC = D // 128
    with Ctx(nc, S) as cx:
        xt = [cx.sb([128, D], F32, "xt") for _ in range(2)]
        hb = [cx.sb([128, D], BF16, "hb") for _ in range(2)]
        junk = cx.sb([128, D], BF16, "junk")
        gt = cx.sb([128, D], F32, "gt")
        ss = cx.sb([128, SL // 128], F32, "ss")
        rs = cx.sb([128, SL // 128], F32, "rs")
        pT = Rot([cx.ps([128, 4, 128], BF16, "pT") for _ in range(4)])
        sx = [S.slot() for _ in range(2)]
        sg = S.slot()
        t_g = S.dma("sp", lambda e: e.dma_start(out=gt[:], in_=g_dram.partition_broadcast(128)), sg)
        x_free = [None, None]
        h_free = [None, None]
        nev = 0
        for tb in range(SL // 128):
            b = tb % 2
            t_x = S.dma("sp", lambda e, b=b, tb=tb: e.dma_start(out=xt[b][:], in_=x_dram[tb * 128:(tb + 1) * 128, :]),
                        sx[b], [x_free[b]])
            t_ss = S.op("act", lambda e, b=b, tb=tb: e.activation(out=junk[:], in_=xt[b][:], func=AF.Square,
                                                                  accum_out=ss[:, tb:tb + 1]), [t_x])
            t_a = S.op("act", lambda e, tb=tb: e.activation(out=rs[:, tb:tb + 1], in_=ss[:, tb:tb + 1], func=AF.Sqrt,
                                                            bias=EPS, scale=1.0 / D), [t_ss])
            t_r = S.op("dve", lambda e, tb=tb: e.reciprocal(out=rs[:, tb:tb + 1], in_=rs[:, tb:tb + 1]), [t_a])
            t_h = S.op("dve", lambda e, b=b, tb=tb: e.scalar_tensor_tensor(
                out=hb[b][:], in0=xt[b][:], scalar=rs[:, tb:tb + 1], in1=gt[:], op0=ALU.mult, op1=ALU.mult),
                [t_r, t_g, h_free[b]])
            x_free[b] = t_h
            tp = None
            for grp in range(KC // 4):
                i, pt, fr = pT.next()
                for j in range(4):
                    kc = grp * 4 + j
                    tp = S.op("pe", lambda e, pt=pt, j=j, kc=kc, b=b: e.transpose(
                        out=pt[:, j, :], in_=hb[b][:, kc * 128:(kc + 1) * 128], identity=ident[:]), [t_h, fr])
                eng = "act" if nev % 2 == 0 else "dve"
                nev += 1
                t_e = S.op(eng, lambda e, eng=eng, pt=pt, grp=grp, tb=tb: evac(
                    eng, e, hT[:, grp * 4:(grp + 1) * 4, tb * 128:(tb + 1) * 128], pt[:]), [tp])
                pT.release(i, t_e)
            h_free[b] = tp
        S.emit()


class Proj:
    def __init__(self, cx, cfg, srcT, KC, src_tok, nbanks=4, banks=None):
        self.cx, self.S, self.cfg = cx, cx.S, cfg
        self.srcT, self.KC, self.src_tok = srcT, KC, src_tok
        self.banks = banks or Rot([cx.ps([128, 512], F32, "pp") for _ in range(nbanks)])

    def mm_T(self, w, c0, ncol, tt, wtok, M=None):
        S = self.S
        i, bank, fr = self.banks.next()
        tm = None
        for kc in range(self.KC):
            tm = S.op("pe", lambda e, kc=kc, bank=bank: e.matmul(
                out=bank[0:ncol, :], lhsT=w[:, kc, c0:c0 + ncol], rhs=self.srcT[:, kc, tt * 512:(tt + 1) * 512],
                start=(kc == 0), stop=(kc == self.KC - 1)), [fr, wtok, self.src_tok])
        return i, bank, tm

    def mm_N(self, w, c0, ncol, tb, wtok):
        S = self.S
        i, bank, fr = self.banks.next()
        tm = None
        for kc in range(self.KC):
            tm = S.op("pe", lambda e, kc=kc, bank=bank: e.matmul(
                out=bank[:, 0:ncol], lhsT=self.srcT[:, kc, tb * 128:(tb + 1) * 128], rhs=w[:, kc, c0:c0 + ncol],
                start=(kc == 0), stop=(kc == self.KC - 1)), [fr, wtok, self.src_tok])
        return i, bank, tm


class WStream:
    def __init__(self, cx, KC, width=512, n=2):
        self.cx, self.S, self.KC = cx, cx.S, KC
        self.tiles = [cx.sb([128, KC, width], BF16, "wt") for _ in range(n)]
        self.slots = [cx.S.slot("pool") for _ in range(n)]
        self.free = [None] * n
        self.k = 0

    def load(self, w_ap, ncol):
        i = self.k % len(self.tiles)
        self.k += 1
        t = self.tiles[i]
        src = w_ap.rearrange("(k p) n -> p k n", p=128)
        tok = self.S.dma("pool", lambda e: e.dma_start(out=t[:, :, 0:ncol], in_=src), self.slots[i], [self.free[i]])
        return i, t, tok

    def release(self, i, tok):
        self.free[i] = tok


def phase_proj0(nc, S, cfg, hT, h_tok, w0, cs_dram, scr):
    SL, HA, HB = cfg.SL, cfg.HA, cfg.HB
    KC = cfg.D // 128
    NTT, NTB = SL // 512, SL // 128
    with Ctx(nc, S) as cx:
        pj = Proj(cx, cfg, hT, KC, h_tok)
        ws = WStream(cx, KC)
        st32 = OutStage(cx, [128, 512], F32, n=2, name="st32")
        st16 = OutStage(cx, [128, 512], BF16, n=4, name="st16")
        groups = []
        groups.append((0, 512, "T", ("f32", scr["cq_T"], 0, 1.0, None)))
        groups.append((512, 512, "T", ("f32", scr["ckv_T"], 0, 1.0, None)))
        groups.append((1024, 64, "R", None))
        c = 1088
        for g in range(HB * 128 // 512):
            groups.append((c + g * 512, 512, "T", ("bf", scr["qb_T"], g * 512, 128.0 ** -0.5, None)))
        c += HB * 128
        for g in range(HB * 128 // 512):
            groups.append((c + g * 512, 512, "T", ("bf", scr["kb_T"], g * 512, 1.0, None)))
        c += HB * 128
        for g in range(HB * 128 // 512):
            groups.append((c + g * 512, 512, "N", (scr["vb"], g * 4)))
        c += HB * 128
        for g in range(cfg.FM0 // 512):
            groups.append((c + g * 512, 512, "T", ("bf", scr["gate_T"], g * 512, 1.0, AF.Silu)))
        nxt = ws.load(w0[:, groups[0][0]:groups[0][0] + groups[0][1]], groups[0][1])
        nev = 0
        for gi, (c0, ncol, kind, spec) in enumerate(groups):
            wi, wt, wtok = nxt
            if gi + 1 < len(groups):
                n0, nn = groups[gi + 1][0], groups[gi + 1][1]
                nxt = ws.load(w0[:, n0:n0 + nn], nn)
            last = None
            if kind == "T":
                typ, dst, r0, scale, func = spec
                for cb in range(ncol // 128):
                    for tt in range(NTT):
                        i, bank, tm = pj.mm_T(wt, cb * 128, 128, tt, wtok)
                        last = tm
                        eng = "act" if (func is not None or nev % 2 == 0) else "dve"
                        nev += 1
                        stg = st32 if typ == "f32" else st16
                        tc = stg.put(eng, lambda e, t, eng=eng, bank=bank, scale=scale, func=func: evac(
                            eng, e, t[:], bank[:], scale, func), [tm],
                            dst[r0 + cb * 128:r0 + (cb + 1) * 128, tt * 512:(tt + 1) * 512])
                        pj.banks.release(i, tc)
            elif kind == "N":
                dst, h0 = spec
                for tb in range(NTB):
                    i, bank, tm = pj.mm_N(wt, 0, ncol, tb, wtok)
                    last = tm
                    eng = "act" if nev % 2 == 0 else "dve"
                    nev += 1
                    tc = st16.put(eng, lambda e, t, eng=eng, bank=bank: evac(eng, e, t[:], bank[:]), [tm],
                                  dst[h0:h0 + 4, tb * 128:(tb + 1) * 128, :].rearrange("h p f -> p h f"),
                                  sub=lambda t: t[:].rearrange("p (h f) -> p h f", h=4))
                    pj.banks.release(i, tc)
            else:
                last = Rope(cx, pj, cfg, cs_dram, st16).run(wt, 0, wtok, 1.0, scr["kr_T"])
            ws.release(wi, last)
        cx.flush()


def latent_norm(cx, cfg, c_dram, gain_dram, cnT, ones32, pbanks, ones_tok):
    S = cx.S
    SL = cfg.SL
    cT = cx.sb([128, 4, SL], F32, "cT")
    gn = cx.sb([128, 4], F32, "gn")
    sq = [cx.sb([128, 4, 512], F32, "sq") for _ in range(2)]
    rt = [cx.sb([128, 512], F32, "rt") for _ in range(2)]
    sl = S.slot()
    sg = S.slot()
    t_g = S.dma("sp", lambda e: e.dma_start(out=gn[:], in_=gain_dram), sg)
    t_c = None
    for k in range(4):
        t_c = S.dma("sp", lambda e, k=k: e.dma_start(out=cT[:, k, :], in_=c_dram[k * 128:(k + 1) * 128, :]), sl)
    sq_free = [None, None]
    rt_free = [None, None]
    last = None
    for tt in range(SL // 512):
        b = tt % 2
        t_s = S.op("act", lambda e, b=b, tt=tt: e.activation(out=sq[b][:], in_=cT[:, :, tt * 512:(tt + 1) * 512], func=AF.Square),
                   [t_c, sq_free[b]])
        i, bank, fr = pbanks.next()
        tm = None
        for k in range(4):
            tm = S.op("pe", lambda e, k=k, b=b, bank=bank: e.matmul(out=bank[:], lhsT=ones32[:], rhs=sq[b][:, k, :],
                                                                   start=(k == 0), stop=(k == 3)), [t_s, fr, ones_tok])
        sq_free[b] = tm
        t_a = S.op("act", lambda e, b=b, bank=bank: e.activation(out=rt[b][:], in_=bank[:], func=AF.Sqrt, bias=EPS, scale=1.0 / 512),
                   [tm, rt_free[b]])
        pbanks.release(i, t_a)
        t_r = S.op("dve", lambda e, b=b: e.reciprocal(out=rt[b][:], in_=rt[b][:]), [t_a])
        tn = t_r
        for k in range(4):
            tn = S.op("dve", lambda e, k=k, b=b, tt=tt: e.scalar_tensor_tensor(
                out=cnT[:, k, tt * 512:(tt + 1) * 512], in0=cT[:, k, tt * 512:(tt + 1) * 512], scalar=gn[:, k:k + 1],
                in1=rt[b][:], op0=ALU.mult, op1=ALU.mult), [t_r, t_g, tn])
        rt_free[b] = tn
        last = tn
    return last


def phase_up(nc, S, cfg, wuq_d, wukv_d, qg_d, kvg_d, cs_dram, ones32_d, scr):
    SL, HA = cfg.SL, cfg.HA
    NTT, NTB = SL // 512, SL // 128
    for which in ("q", "kv"):
        with Ctx(nc, S) as cx:
            ones32 = cx.sb([128, 128], F32, "ones32")
            so = S.slot()
            t_o = S.dma("sp", lambda e: e.dma_start(out=ones32[:], in_=ones32_d), so)
            cnT = cx.sb([128, 4, SL], BF16, "cnT")
            banks = Rot([cx.ps([128, 512], F32, "pp") for _ in range(4)])
            st16 = OutStage(cx, [128, 512], BF16, n=4, name="st16")
            ncols = HA * 192 if which == "q" else HA * 256
            wt = cx.sb([128, 4, ncols], BF16, "wup")
            sw = S.slot("pool")
            wd = wuq_d if which == "q" else wukv_d
            t_w = S.dma("pool", lambda e: e.dma_start(out=wt[:], in_=wd.rearrange("(k p) n -> p k n", p=128)), sw)
            t_n = latent_norm(cx, cfg, scr["cq_T"] if which == "q" else scr["ckv_T"],
                              qg_d if which == "q" else kvg_d, cnT, ones32, banks, t_o)
            pj = Proj(cx, cfg, cnT, 4, t_n, banks=banks)
            nev = 0
            if which == "q":
                sc = 192.0 ** -0.5
                rp = Rope(cx, pj, cfg, cs_dram, st16)
                for h in range(HA):
                    for tt in range(NTT):
                        i, bank, tm = pj.mm_T(wt, h * 192, 128, tt, t_w)
                        eng = "act" if nev % 2 == 0 else "dve"
                        nev += 1
                        tc = st16.put(eng, lambda e, t, eng=eng, bank=bank: evac(eng, e, t[:], bank[:], sc), [tm],
                                      scr["qa_T"][h, 0:128, tt * 512:(tt + 1) * 512])
                        pj.banks.release(i, tc)
                    rp.run(wt, h * 192 + 128, t_w, sc, scr["qa_T"][h, 128:192, :])
            else:
                for h in range(HA):
                    for tt in range(NTT):
                        i, bank, tm = pj.mm_T(wt, h * 256, 128, tt, t_w)
                        eng = "act" if nev % 2 == 0 else "dve"
                        nev += 1
                        tc = st16.put(eng, lambda e, t, eng=eng, bank=bank: evac(eng, e, t[:], bank[:]), [tm],
                                      scr["ka_T"][h, :, tt * 512:(tt + 1) * 512])
                        pj.banks.release(i, tc)
                    for tb in range(NTB):
                        i, bank, tm = pj.mm_N(wt, h * 256 + 128, 128, tb, t_w)
                        eng = "act" if nev % 2 == 0 else "dve"
                        nev += 1
                        tc = st16.put(eng, lambda e, t, eng=eng, bank=bank: evac(eng, e, t[:, 0:128], bank[:, 0:128]), [tm],
                                      scr["va"][h, tb * 128:(tb + 1) * 128, :], sub=lambda t: t[:, 0:128])
                        pj.banks.release(i, tc)
            cx.flush()


class Rope:
    def __init__(self, cx, pj, cfg, cs_dram, out_stage):
        self.cx, self.pj, self.cfg, self.cs_dram, self.out_stage = cx, pj, cfg, cs_dram, out_stage
        S = cx.S
        self.wr = cx.sb([128, pj.KC, 64], BF16, "wrot")
        self.cst = [cx.sb([64, 2, 512], F32, "cs") for _ in range(2)]
        self.css = [S.slot() for _ in range(2)]
        self.csfree = [None, None]
        self.tmp = [cx.sb([64, 2, 512], F32, "rtmp") for _ in range(2)]
        self.last_mm = None
        self.k = 0

    def run(self, w, c0, wtok, scale, dst_rows):
        S = self.cx.S
        pj, wr, cst, tmp = self.pj, self.wr, self.cst, self.tmp
        t1 = S.op("act", lambda e: e.activation(out=wr[:, :, 0:32], in_=w[:, :, c0 + 32:c0 + 64], func=AF.Copy, scale=-1.0),
                  [wtok, self.last_mm])
        t2 = S.op("act", lambda e: e.activation(out=wr[:, :, 32:64], in_=w[:, :, c0:c0 + 32], func=AF.Copy, scale=1.0),
                  [wtok, t1])
        m2 = None
        for tt in range(self.cfg.SL // 512):
            b = self.k % 2
            self.k += 1
            t_cs = S.dma("sp", lambda e, b=b, tt=tt: e.dma_start(
                out=cst[b][:], in_=self.cs_dram[:, :, tt * 512:(tt + 1) * 512].rearrange("c p t -> p c t")),
                self.css[b], [self.csfree[b]])
            i1, b1, m1 = pj.mm_T(w, c0, 64, tt, wtok)
            i2, b2, m2 = pj.mm_T(wr, 0, 64, tt, t2)
            ta = S.op("dve", lambda e, b=b, b1=b1: e.scalar_tensor_tensor(
                out=tmp[b][:, 0, :], in0=b1[0:64, :], scalar=float(scale), in1=cst[b][:, 0, :], op0=ALU.mult, op1=ALU.mult),
                [m1, t_cs, self.csfree[b]])
            tb_ = S.op("dve", lambda e, b=b, b2=b2: e.scalar_tensor_tensor(
                out=tmp[b][:, 1, :], in0=b2[0:64, :], scalar=float(scale), in1=cst[b][:, 1, :], op0=ALU.mult, op1=ALU.mult),
                [m2, t_cs, ta])
            pj.banks.release(i1, ta)
            pj.banks.release(i2, tb_)
            tc = self.out_stage.put("dve", lambda e, t, b=b: e.tensor_tensor(
                out=t[0:64, :], in0=tmp[b][:, 0, :], in1=tmp[b][:, 1, :], op=ALU.add),
                [ta, tb_], dst_rows[:, tt * 512:(tt + 1) * 512], sub=lambda t: t[0:64, :])
            self.csfree[b] = tc
        self.last_mm = m2
        return m2


def load_const(cx, dram_ap, shape, dt, name):
    t = cx.sb(shape, dt, name)
    sl = cx.S.slot()
    tok = cx.S.dma("sp", lambda e: e.dma_start(out=t[:], in_=dram_ap), sl)
    return t, tok


def phase_mla(nc, S, cfg, I, scr):
    SL, HA = cfg.SL, cfg.HA
    NQT, NB = SL // 512, SL // 128
    with Ctx(nc, S) as cx:
        ident, t_c0 = load_const(cx, I["ident"], [128, 128], BF16, "ident")
        ones, t_c1 = load_const(cx, I["ones_bf"], [128, 128], BF16, "ones")
        maskA, t_c2 = load_const(cx, I["maskA"], [128, 4, 512], BF16, "maskA")
        kr, t_kr = load_const(cx, scr["kr_T"], [64, SL], BF16, "kr")
        t_const = [t_c0, t_c1, t_c2, t_kr]
        qn = [cx.sb([128, SL], BF16, "qn") for _ in range(2)]
        qr = [cx.sb([64, SL], BF16, "qr") for _ in range(2)]
        kn = [cx.sb([128, SL], BF16, "kn") for _ in range(2)]
        vv = [cx.sb([128, NB, 128], BF16, "vv") for _ in range(2)]
        hs = [S.slot() for _ in range(2)]
        head_free = [None, None]
        pS = Rot([cx.ps([128, 512], F32, "pS") for _ in range(2)])
        pO = Rot([cx.ps([128, 512], F32, "pO") for _ in range(2)])
        pD = Rot([cx.ps([128, 512], F32, "pD") for _ in range(2)])
        pts = Rot([cx.sb([128, 512], BF16, "pt") for _ in range(3)])
        gts = Rot([cx.sb([128, 512], BF16, "gt") for _ in range(2)])
        gsl = [S.slot() for _ in range(2)]
        rden = Rot([cx.sb([128, 512], F32, "rden") for _ in range(2)])
        of = Rot([cx.sb([128, 512], F32, "of") for _ in range(2)])
        stg = OutStage(cx, [128, 512], BF16, n=2, name="mixst")

        def load_head(h):
            b = h % 2
            fr = head_free[b]
            S.dma("sp", lambda e: e.dma_start(out=qn[b][:], in_=scr["qa_T"][h, 0:128, :]), hs[b], [fr])
            S.dma("sp", lambda e: e.dma_start(out=qr[b][:], in_=scr["qa_T"][h, 128:192, :]), hs[b], [fr])
            S.dma("sp", lambda e: e.dma_start(out=kn[b][:], in_=scr["ka_T"][h, :, :]), hs[b], [fr])
            tok = None
            for j in range(4):
                tok = S.dma("sp", lambda e, j=j: e.dma_start(
                    out=vv[b][:, j * 8:(j + 1) * 8, :],
                    in_=scr["va"][h, j * 1024:(j + 1) * 1024, :].rearrange("(n p) f -> p n f", p=128)), hs[b], [fr])
            return tok

        blocks = []
        for h in range(HA):
            for qt in range(NQT):
                nkb = 4 * qt + 4
                for kb in range(nkb):
                    blocks.append((h, qt, kb, nkb))
        head_tok = {0: load_head(0)}
        state = {}

        def issue_S(bi):
            h, qt, kb, nkb = blocks[bi]
            b = h % 2
            i_s, Sb, sfr = pS.next()
            diag = kb >= 4 * qt
            S.op("pe", lambda e: e.matmul(out=Sb[:], lhsT=kn[b][:, kb * 128:(kb + 1) * 128], rhs=qn[b][:, qt * 512:(qt + 1) * 512],
                                         start=True, stop=False), [head_tok[h], sfr] + t_const)
            m = S.op("pe", lambda e: e.matmul(out=Sb[:], lhsT=kr[:, kb * 128:(kb + 1) * 128], rhs=qr[b][:, qt * 512:(qt + 1) * 512],
                                             start=False, stop=(not diag)), [])
            if diag:
                m = S.op("pe", lambda e: e.matmul(out=Sb[:], lhsT=ident[:], rhs=maskA[:, kb - 4 * qt, :], start=False, stop=True), [])
            state[bi] = (i_s, Sb, m)

        issue_S(0)
        cur = {}
        for bi, (h, qt, kb, nkb) in enumerate(blocks):
            b = h % 2
            if bi + 1 < len(blocks):
                issue_S(bi + 1)
            i_s, Sb, m = state.pop(bi)
            if kb == 0:
                io, O, ofr = pO.next()
                idn, Dn, dfr = pD.next()
                gi, gt, gfr = gts.next()
                t_gate = S.dma("sp", lambda e, gt=gt, h=h, qt=qt: e.dma_start(
                    out=gt[:], in_=scr["gate_T"][h * 128:(h + 1) * 128, qt * 512:(qt + 1) * 512]), gsl[gi], [gfr])
                cur = dict(io=io, O=O, ofr=ofr, idn=idn, Dn=Dn, dfr=dfr, gi=gi, gt=gt, t_gate=t_gate)
            O, Dn = cur["O"], cur["Dn"]
            ip, pt, pfr = pts.next()
            t_e = S.op("act", lambda e, pt=pt, Sb=Sb: e.activation(out=pt[:], in_=Sb[:], func=AF.Exp), [m, pfr])
            pS.release(i_s, t_e)
            first, last = kb == 0, kb == nkb - 1
            S.op("pe", lambda e, O=O, pt=pt, b=b, kb=kb, first=first, last=last: e.matmul(
                out=O[:], lhsT=vv[b][:, kb, :], rhs=pt[:], start=first, stop=last), [t_e, cur["ofr"] if first else None])
            m5 = S.op("pe", lambda e, Dn=Dn, pt=pt, first=first, last=last: e.matmul(
                out=Dn[:], lhsT=ones[:], rhs=pt[:], start=first, stop=last), [t_e, cur["dfr"] if first else None])
            pts.release(ip, m5)
            if last:
                ir, rd, rfr = rden.next()
                io2, o32, o32fr = of.next()
                t_r = S.op("dve", lambda e, rd=rd, Dn=Dn: e.reciprocal(out=rd[:], in_=Dn[:]), [m5, rfr])
                pD.release(cur["idn"], t_r)
                t_o = S.op("dve", lambda e, o32=o32, O=O, rd=rd: e.tensor_tensor(out=o32[:], in0=O[:], in1=rd[:], op=ALU.mult),
                           [m5, t_r, o32fr])
                pO.release(cur["io"], t_o)
                rden.release(ir, t_o)
                gt = cur["gt"]
                tc = stg.put("dve", lambda e, t, o32=o32, gt=gt: e.tensor_tensor(out=t[:], in0=o32[:], in1=gt[:], op=ALU.mult),
                             [t_o, cur["t_gate"]], scr["mix0_T"][h * 128:(h + 1) * 128, qt * 512:(qt + 1) * 512])
                of.release(io2, tc)
                gts.release(cur["gi"], tc)
                if qt == NQT - 1:
                    head_free[b] = m5
                if qt == 0 and h + 1 < HA:
                    head_tok[h + 1] = load_head(h + 1)
        cx.flush()


DILS = (1, 4, 16)


def dil_cols(T, d, r, n):
    if d == 1:
        return T[:, n * 128:(n + 1) * 128]
    return T[:, :].rearrange("p (m d) -> p d m", d=d)[:, r, n * 128:(n + 1) * 128]


def phase_dil(nc, S, cfg, I, scr):
    SL, HA, HB = cfg.SL, cfg.HA, cfg.HB
    NB = SL // 128
    NST = SL // 2048
    with Ctx(nc, S) as cx:
        ident, t_c0 = load_const(cx, I["ident"], [128, 128], BF16, "ident")
        ones, t_c1 = load_const(cx, I["ones_bf"], [128, 128], BF16, "ones")
        dmask, t_c2 = load_const(cx, I["dmask"], [128, 6, 128], F32, "dmask")
        t_const = [t_c0, t_c1, t_c2]
        qT = [cx.sb([128, SL], BF16, "dq") for _ in range(2)]
        kT = [cx.sb([128, SL], BF16, "dk") for _ in range(2)]
        vd = [[cx.sb([128, NB, 128], BF16, "dv") for _ in range(3)] for _ in range(2)]
        qP = {4: cx.sb([128, SL], BF16, "dqp4"), 16: cx.sb([128, SL], BF16, "dqp16")}
        kP = {4: cx.sb([128, SL], BF16, "dkp4"), 16: cx.sb([128, SL], BF16, "dkp16")}
        perm_free = [None]
        perm_head = [None]
        bmf1 = cx.sb([128, 6, 128], F32, "bmf")
        bsm1 = cx.sb([128, 2, 6, 128], BF16, "bsm")
        blo1 = cx.sb([128, 6, 128], F32, "blo")
        bmf, bsm, blo = [bmf1, bmf1], [bsm1, bsm1], [blo1, blo1]
        prep_free = [None]
        bh = [cx.sb([128, 6, 512], BF16, "bh") for _ in range(2)]
        bl = [cx.sb([128, 6, 512], BF16, "bl") for _ in range(2)]
        hs = [S.slot() for _ in range(2)]
        head_free = [None, None]
        pS = Rot([cx.ps([128, 512], F32, "pS") for _ in range(3)])
        pO = Rot([cx.ps([128, 512], F32, "pO") for _ in range(2)])
        pD = Rot([cx.ps([128, 512], F32, "pD") for _ in range(2)])
        pts = Rot([cx.sb([128, 512], BF16, "pt") for _ in range(4)])
        nacc = [cx.sb([128, 2048], F32, "nacc") for _ in range(2)]
        dacc = [cx.sb([128, 2048], F32, "dacc") for _ in range(2)]
        acc_free = [None, None]
        gts = [cx.sb([128, 2048], BF16, "dgt") for _ in range(2)]
        gsl = [S.slot() for _ in range(2)]
        g_free = [None, None]
        stg = OutStage(cx, [128, 2048], BF16, n=2, name="dmix")

        def load_head(hb):
            b = hb % 2
            fr = head_free[b]
            S.dma("sp", lambda e: e.dma_start(out=qT[b][:], in_=scr["qb_T"][hb * 128:(hb + 1) * 128, :]), hs[b], [fr])
            S.dma("sp", lambda e: e.dma_start(out=kT[b][:], in_=scr["kb_T"][hb * 128:(hb + 1) * 128, :]), hs[b], [fr])
            S.dma("sp", lambda e: e.dma_start(out=bmf[b][:], in_=I["dbias"][hb]), hs[b], [fr, prep_free[0]])
            for j in range(4):
                S.dma("sp", lambda e, j=j: e.dma_start(
                    out=vd[b][0][:, j * 8:(j + 1) * 8, :],
                    in_=scr["vb"][hb, j * 1024:(j + 1) * 1024, :].rearrange("(n p) f -> p n f", p=128)), hs[b], [fr])
            tok = None
            for p, d in ((1, 4), (2, 16)):
                nper = NB // d
                src = scr["vb"][hb].rearrange("(n i r) f -> r i n f", i=128, r=d)
                for r in range(d):
                    tok = S.dma("sp", lambda e, p=p, r=r, nper=nper, src=src: e.dma_start(
                        out=vd[b][p][:, r * nper:(r + 1) * nper, :], in_=src[r]), hs[b], [fr])
            t1 = S.op("dve", lambda e: e.tensor_tensor(out=bmf[b][:], in0=bmf[b][:], in1=dmask[:], op=ALU.add), [tok, t_c2, fr])
            t2 = S.op("dve", lambda e: e.tensor_copy(out=bsm[b][:, 0], in_=bmf[b][:]), [t1])
            t3 = S.op("dve", lambda e: e.tensor_tensor(out=blo[b][:], in0=bmf[b][:], in1=bsm[b][:, 0], op=ALU.subtract), [t2])
            t4 = S.op("dve", lambda e: e.tensor_copy(out=bsm[b][:, 1], in_=blo[b][:]), [t3])
            tl = t4
            for q in range(4):
                tl = S.op("dve", lambda e, q=q: e.tensor_copy(out=bh[b][:, :, q * 128:(q + 1) * 128], in_=bsm[b][:, 0]), [t4, tl])
                tl = S.op("dve", lambda e, q=q: e.tensor_copy(out=bl[b][:, :, q * 128:(q + 1) * 128], in_=bsm[b][:, 1]), [t4, tl])
            prep_free[0] = tl
            return [tok, tl]

        def permute_head(hb):
            b = hb % 2
            toks = []
            for d in (4, 16):
                for src, dst in ((qT[b], qP[d]), (kT[b], kP[d])):
                    toks.append(S.op("pool", lambda e, src=src, dst=dst, d=d: e.tensor_copy(
                        out=dst[:, :].rearrange("p (d m) -> p d m", d=d), in_=src[:, :].rearrange("p (m d) -> p d m", d=d)),
                        [head_tok[hb], perm_free[0]]))
            perm_head[0] = hb
            return toks

        passes = []
        for hb in range(HB):
            for st in range(NST):
                for p, d in enumerate(DILS):
                    for g in range(4):
                        quarters = []
                        for q in range(4):
                            if p == 0:
                                r, n = 0, st * 16 + g * 4 + q
                            elif p == 1:
                                r, n = q, st * 4 + g
                            else:
                                r, n = g * 4 + q, st
                            quarters.append((q, r, n))
                        has_prev = [qq for qq in quarters if qq[2] >= 1]
                        passes.append(dict(hb=hb, st=st, p=p, d=d, g=g, kind=1, qs=quarters, last=(len(has_prev) == 0)))
                        if has_prev:
                            passes.append(dict(hb=hb, st=st, p=p, d=d, g=g, kind=0, qs=has_prev, last=True))
        head_tok = {0: load_head(0)}
        state = {}

        def vslot(p, d, r, n):
            return r * (NB // d) + n

        def pcols(T, TP, d, r, n):
            if d == 1:
                return T[:, n * 128:(n + 1) * 128]
            c = r * (SL // d) + n * 128
            return TP[d][:, c:c + 128]

        def issue_S(pi):
            P = passes[pi]
            hb, st, p, d, g, kind = P["hb"], P["st"], P["p"], P["d"], P["g"], P["kind"]
            b = hb % 2
            if perm_head[0] != hb:
                head_tok[hb] = [head_tok[hb], permute_head(hb)]
            i_s, Sb, sfr = pS.next()
            q0 = P["qs"][0][0]
            c0 = q0 * 128
            bi = 2 * p + kind
            S.op("pe", lambda e: e.matmul(out=Sb[:, c0:512], lhsT=ident[:], rhs=bh[b][:, bi, c0:512], start=True, stop=False,
                                         skip_group_check=True), [head_tok[hb], sfr] + t_const)
            m = S.op("pe", lambda e: e.matmul(out=Sb[:, c0:512], lhsT=ident[:], rhs=bl[b][:, bi, c0:512], start=False, stop=False,
                                             skip_group_check=True), [])
            nq = len(P["qs"])
            for j, (q, r, n) in enumerate(P["qs"]):
                nk = n if kind == 1 else n - 1
                m = S.op("pe", lambda e, q=q, r=r, n=n, nk=nk, j=j: e.matmul(
                    out=Sb[:, q * 128:(q + 1) * 128], lhsT=pcols(kT[b], kP, d, r, nk), rhs=pcols(qT[b], qP, d, r, n),
                    start=False, stop=(j == nq - 1), skip_group_check=True), [])
            state[pi] = (i_s, Sb, m, c0)

        LOOK = 2
        issued = [0]

        def issue_upto(k):
            while issued[0] <= k and issued[0] < len(passes):
                issue_S(issued[0])
                issued[0] += 1

        issue_upto(0)
        cur = {}
        acc_toks = []
        for pi, P in enumerate(passes):
            hb, st, p, d, g, kind = P["hb"], P["st"], P["p"], P["d"], P["g"], P["kind"]
            b = hb % 2
            ab = (hb * NST + st) % 2
            lim = pi
            while lim + 1 < len(passes) and lim + 1 <= pi + LOOK and passes[lim + 1]["hb"] == hb:
                lim += 1
            issue_upto(lim)
            defer = pi + 1 < len(passes) and passes[pi + 1]["hb"] != hb
            i_s, Sb, m, c0 = state.pop(pi)
            if kind == 1:
                io, O, ofr = pO.next()
                idn, Dn, dfr = pD.next()
                cur = dict(io=io, O=O, ofr=ofr, idn=idn, Dn=Dn, dfr=dfr)
                if p == 0 and g == 0:
                    acc_toks = []
                    t_gate = S.dma("sp", lambda e, ab=ab, hb=hb, st=st: e.dma_start(
                        out=gts[ab][:], in_=scr["gate_T"][(HA + hb) * 128:(HA + hb + 1) * 128, st * 2048:(st + 1) * 2048]),
                        gsl[ab], [g_free[ab]])
            O, Dn = cur["O"], cur["Dn"]
            ip, pt, pfr = pts.next()
            t_e = S.op("act", lambda e, pt=pt, Sb=Sb, c0=c0: e.activation(out=pt[:, c0:512], in_=Sb[:, c0:512], func=AF.Exp), [m, pfr])
            pS.release(i_s, t_e)
            nq = len(P["qs"])
            for j, (q, r, n) in enumerate(P["qs"]):
                nk = n if kind == 1 else n - 1
                first = (kind == 1 and j == 0)
                S.op("pe", lambda e, O=O, pt=pt, q=q, b=b, p=p, sl=vslot(p, d, r, nk), first=first, lastmm=(P["last"] and j == nq - 1): e.matmul(
                    out=O[:, q * 128:(q + 1) * 128], lhsT=vd[b][p][:, sl, :], rhs=pt[:, q * 128:(q + 1) * 128],
                    start=first, stop=lastmm, skip_group_check=True), [t_e, cur["ofr"] if first else None])
            m5 = S.op("pe", lambda e, Dn=Dn, pt=pt, c0=c0, kind=kind, lastp=P["last"]: e.matmul(
                out=Dn[:, c0:512], lhsT=ones[:], rhs=pt[:, c0:512], start=(kind == 1), stop=lastp, skip_group_check=True),
                [t_e, cur["dfr"] if kind == 1 else None])
            pts.release(ip, m5)
            if P["last"]:
                if p == 0:
                    nv = nacc[ab][:, g * 512:(g + 1) * 512]
                    dv = dacc[ab][:, g * 512:(g + 1) * 512]
                    Ov, Dv = O[:], Dn[:]
                elif p == 1:
                    nv = nacc[ab][:, g * 512:(g + 1) * 512].rearrange("p (i r) -> p r i", r=4)
                    dv = dacc[ab][:, g * 512:(g + 1) * 512].rearrange("p (i r) -> p r i", r=4)
                    Ov = O[:, :].rearrange("p (r i) -> p r i", r=4)
                    Dv = Dn[:, :].rearrange("p (r i) -> p r i", r=4)
                else:
                    nv = nacc[ab][:, :].rearrange("p (i g r) -> p g r i", g=4, r=4)[:, g]
                    dv = dacc[ab][:, :].rearrange("p (i g r) -> p g r i", g=4, r=4)[:, g]
                    Ov = O[:, :].rearrange("p (r i) -> p r i", r=4)
                    Dv = Dn[:, :].rearrange("p (r i) -> p r i", r=4)
                if p == 0:
                    ta = S.op("act", lambda e, nv=nv, Ov=Ov: e.activation(out=nv, in_=Ov, func=AF.Copy), [m5, acc_free[ab]])
                    tb_ = S.op("dve", lambda e, dv=dv, Dv=Dv: e.tensor_copy(out=dv, in_=Dv), [m5, acc_free[ab]])
                else:
                    ta = S.op("dve", lambda e, nv=nv, Ov=Ov: e.tensor_tensor(out=nv, in0=Ov, in1=nv, op=ALU.add), [m5] + acc_toks)
                    tb_ = S.op("dve", lambda e, dv=dv, Dv=Dv: e.tensor_tensor(out=dv, in0=Dv, in1=dv, op=ALU.add), [m5, ta] + acc_toks)
                acc_toks = acc_toks + [ta, tb_]
                pO.release(cur["io"], ta)
                pD.release(cur["idn"], tb_)
                if p == 2 and g == 3:
                    t_r = S.op("dve", lambda e, ab=ab: e.reciprocal(out=dacc[ab][:], in_=dacc[ab][:]), acc_toks)
                    t_o = S.op("dve", lambda e, ab=ab: e.tensor_tensor(out=nacc[ab][:], in0=nacc[ab][:], in1=dacc[ab][:], op=ALU.mult), [t_r])
                    tc = stg.put("dve", lambda e, t, ab=ab: e.tensor_tensor(out=t[:], in0=nacc[ab][:], in1=gts[ab][:], op=ALU.mult),
                                 [t_o, t_gate], scr["mix0_T"][(HA + hb) * 128:(HA + hb + 1) * 128, st * 2048:(st + 1) * 2048])
                    acc_free[ab] = tc
                    g_free[ab] = tc
                    if st == NST - 1:
                        head_free[b] = m5
                        perm_free[0] = m5
                    if st == 0 and hb + 1 < HB:
                        head_tok[hb + 1] = load_head(hb + 1)
            if defer:
                issue_upto(pi + 1)
        cx.flush()


def phase_sb(nc, S, cfg, I, scr, extra_flush=()):
    SL, HS = cfg.SL, cfg.HS
    NQT, NB = SL // 512, SL // 128
    with Ctx(nc, S) as cx:
        ident, t_c0 = load_const(cx, I["ident"], [128, 128], BF16, "ident")
        ones, t_c1 = load_const(cx, I["ones_bf"], [128, 128], BF16, "ones")
        tri, t_c2 = load_const(cx, I["tri"], [128, 128], BF16, "tri")
        mSn, t_c3 = load_const(cx, I["maskSn"], [128, 4, 512], BF16, "mSn")
        mSp, t_c4 = load_const(cx, I["maskSp"], [128, 4, 512], BF16, "mSp")
        t_const = [t_c0, t_c1, t_c2, t_c3, t_c4]
        qT = [cx.sb([128, SL], BF16, "sq") for _ in range(2)]
        kT = [cx.sb([128, SL], BF16, "sk") for _ in range(2)]
        nkT = [cx.sb([128, SL], BF16, "snk") for _ in range(2)]
        vv = [cx.sb([128, NB, 128], BF16, "sv") for _ in range(2)]
        hs = [S.slot() for _ in range(2)]
        head_free = [None, None]
        pS = Rot([cx.ps([128, 512], F32, "pS") for _ in range(2)])
        pC = Rot([cx.ps([128, 512], F32, "pC") for _ in range(2)])
        pO = Rot([cx.ps([128, 512], F32, "pO") for _ in range(2)])
        Et = Rot([cx.sb([128, 512], F32, "Et") for _ in range(2)])
        Lt = Rot([cx.sb([128, 512], BF16, "Lt") for _ in range(4)])
        Ls = Rot([cx.sb([128, 512], BF16, "Ls") for _ in range(3)])
        At = Rot([cx.sb([128, 512], BF16, "At") for _ in range(3)])
        gts = Rot([cx.sb([128, 512], BF16, "sgt") for _ in range(2)])
        gsl = [S.slot() for _ in range(2)]
        stg = OutStage(cx, [128, 512], BF16, n=2, name="smix")

        def load_head(h):
            b = h % 2
            fr = head_free[b]
            S.dma("sp", lambda e: e.dma_start(out=qT[b][:], in_=scr["q1_T"][h * 128:(h + 1) * 128, :]), hs[b], [fr])
            S.dma("sp", lambda e: e.dma_start(out=kT[b][:], in_=scr["k1_T"][h * 128:(h + 1) * 128, :]), hs[b], [fr])
            tok = None
            for j in range(4):
                tok = S.dma("sp", lambda e, j=j: e.dma_start(
                    out=vv[b][:, j * 8:(j + 1) * 8, :],
                    in_=scr["v1"][h, j * 1024:(j + 1) * 1024, :].rearrange("(n p) f -> p n f", p=128)), hs[b], [fr])
            t_nk = S.op("pool", lambda e: e.tensor_scalar(out=nkT[b][:], in0=kT[b][:], scalar1=-1.0, scalar2=None, op0=ALU.mult),
                        [tok, fr])
            return [tok, t_nk]

        blocks = []
        for h in range(HS):
            for qt in range(NQT):
                for kb in range(4 * qt + 3, -1, -1):
                    blocks.append((h, qt, kb))
        NBK = len(blocks)
        head_tok = {0: load_head(0)}
        st1, st2, st3 = {}, {}, {}
        first_il = [None]
        first_users = [[]]
        ls_cur = {}
        ocur = {}

        def stage_S(bi):
            h, qt, kb = blocks[bi]
            b = h % 2
            i_s, Sb, sfr = pS.next()
            diag = kb >= 4 * qt
            m = S.op("pe", lambda e: e.matmul(out=Sb[:], lhsT=kT[b][:, kb * 128:(kb + 1) * 128], rhs=qT[b][:, qt * 512:(qt + 1) * 512],
                                             start=True, stop=(not diag)), [head_tok[h], sfr] + t_const)
            if diag:
                m = S.op("pe", lambda e: e.matmul(out=Sb[:], lhsT=ident[:], rhs=mSn[:, kb - 4 * qt, :], start=False, stop=True), [])
            st1[bi] = (i_s, Sb, m)

        def stage_EL(bi):
            h, qt, kb = blocks[bi]
            i_s, Sb, m = st1.pop(bi)
            ie, E, efr = Et.next()
            il, L, lfr = Lt.next()
            t_e = S.op("act", lambda e: e.activation(out=E[:], in_=Sb[:], func=AF.Exp), [m, efr])
            pS.release(i_s, t_e)
            t_l = S.op("act", lambda e: e.activation(out=L[:], in_=E[:], func=AF.Ln, bias=1.0, scale=1.0), [t_e])
            Et.release(ie, t_l)
            st2[bi] = (il, L, t_l)

        def stage_C(bi):
            h, qt, kb = blocks[bi]
            b = h % 2
            il, L, t_l = st2.pop(bi)
            first = kb == 4 * qt + 3
            last = kb == 0
            diag = kb >= 4 * qt
            i_c, Cb, cfr = pC.next()
            S.op("pe", lambda e: e.matmul(out=Cb[:], lhsT=tri[:], rhs=L[:], start=True, stop=False), [t_l, cfr])
            m = S.op("pe", lambda e: e.matmul(out=Cb[:], lhsT=nkT[b][:, kb * 128:(kb + 1) * 128], rhs=qT[b][:, qt * 512:(qt + 1) * 512],
                                             start=False, stop=(not diag and first)), [])
            if diag:
                m = S.op("pe", lambda e: e.matmul(out=Cb[:], lhsT=ident[:], rhs=mSp[:, kb - 4 * qt, :], start=False, stop=first), [])
            users = [m]
            if not first:
                ils, Lsum, t_ls = ls_cur[bi]
                m = S.op("pe", lambda e: e.matmul(out=Cb[:], lhsT=ones[:], rhs=Lsum[:], start=False, stop=True), [t_ls])
                users = [m]
            if not last:
                if first:
                    ls_cur[bi + 1] = (None, L, t_l)
                    users.append(("hold",))
                else:
                    iln, Lnew, lnfr = Ls.next()
                    t_n = S.op("dve", lambda e: e.tensor_tensor(out=Lnew[:], in0=Lsum[:], in1=L[:], op=ALU.add), [t_l, t_ls])
                    ls_cur[bi + 1] = (iln, Lnew, t_n)
                    users.append(t_n)
            if not first:
                if ils is not None:
                    Ls.release(ils, [u for u in users if u != ("hold",)])
                else:
                    Lt.release(first_il[0], [u for u in users if u != ("hold",)] + first_users[0])
                ls_cur.pop(bi)
            if ("hold",) in users:
                first_il[0] = il
                first_users[0] = [u for u in users if u != ("hold",)]
            else:
                Lt.release(il, list(users))
            st3[bi] = (i_c, Cb, m)

        def stage_A(bi):
            i_c, Cb, m = st3.pop(bi)
            ia, A, afr = At.next()
            t_a = S.op("act", lambda e: e.activation(out=A[:], in_=Cb[:], func=AF.Exp, scale=-1.0), [m, afr])
            pC.release(i_c, t_a)
            st3[("A", bi)] = (ia, A, t_a)

        def stage_PV(bi):
            h, qt, kb = blocks[bi]
            b = h % 2
            ia, A, t_a = st3.pop(("A", bi))
            first = kb == 4 * qt + 3
            last = kb == 0
            if first:
                io, O, ofr = pO.next()
                gi, gt, gfr = gts.next()
                t_gate = S.dma("sp", lambda e: e.dma_start(
                    out=gt[:], in_=scr["gate1_T"][h * 128:(h + 1) * 128, qt * 512:(qt + 1) * 512]), gsl[gi], [gfr])
                ocur.update(io=io, O=O, ofr=ofr, gi=gi, gt=gt, t_gate=t_gate)
            O = ocur["O"]
            m = S.op("pe", lambda e: e.matmul(out=O[:], lhsT=vv[b][:, kb, :], rhs=A[:], start=first, stop=last),
                     [t_a, ocur["ofr"] if first else None])
            At.release(ia, m)
            if last:
                gt = ocur["gt"]
                tc = stg.put("dve", lambda e, t: e.tensor_tensor(out=t[:], in0=O[:], in1=gt[:], op=ALU.mult),
                             [m, ocur["t_gate"]], scr["mix1_T"][h * 128:(h + 1) * 128, qt * 512:(qt + 1) * 512])
                pO.release(ocur["io"], tc)
                gts.release(ocur["gi"], tc)
                if qt == NQT - 1:
                    head_free[b] = m
                if qt == 0 and h + 1 < HS:
                    head_tok[h + 1] = load_head(h + 1)

        for step in range(-2, NBK):
            if 0 <= step + 2 < NBK:
                stage_S(step + 2)
                stage_EL(step + 2)
            if 0 <= step + 1 < NBK:
                stage_C(step + 1)
                stage_A(step + 1)
            if 0 <= step < NBK:
                stage_PV(step)
        cx.flush(extra_flush)


def phase_sb2(nc, S, cfg, I, scr, extra_flush=()):
    SL, HS = cfg.SL, cfg.HS
    NQT, NB = SL // 512, SL // 128
    with Ctx(nc, S) as cx:
        ident, t_c0 = load_const(cx, I["ident"], [128, 128], BF16, "ident")
        ones, t_c1 = load_const(cx, I["ones_bf"], [128, 128], BF16, "ones")
        tri, t_c2 = load_const(cx, I["tri"], [128, 128], BF16, "tri")
        mSn, t_c3 = load_const(cx, I["maskSn"], [128, 4, 512], BF16, "mSn")
        mSp, t_c4 = load_const(cx, I["maskSp"], [128, 4, 512], BF16, "mSp")
        t_const = [t_c0, t_c1, t_c2, t_c3, t_c4]
        qT = [cx.sb([128, SL], BF16, "sq") for _ in range(2)]
        kT = [cx.sb([128, SL], BF16, "sk") for _ in range(2)]
        nkT = [cx.sb([128, SL], BF16, "snk") for _ in range(2)]
        vv = [cx.sb([128, NB, 128], BF16, "sv") for _ in range(2)]
        hs = [S.slot() for _ in range(2)]
        head_free = [None, None]
        pS = Rot([cx.ps([128, 2, 512], F32, "pS") for _ in range(2)])
        pC = Rot([cx.ps([128, 2, 512], F32, "pC") for _ in range(1)])
        pO = Rot([cx.ps([128, 512], F32, "pO") for _ in range(2)])
        Et = Rot([cx.sb([128, 2, 512], F32, "Et") for _ in range(2)])
        Lt = Rot([cx.sb([128, 2, 512], BF16, "Lt") for _ in range(3)])
        Ls = Rot([cx.sb([128, 512], BF16, "Ls") for _ in range(3)])
        At = Rot([cx.sb([128, 2, 512], BF16, "At") for _ in range(2)])
        gts = Rot([cx.sb([128, 512], BF16, "sgt") for _ in range(2)])
        gsl = [S.slot() for _ in range(2)]
        stg = OutStage(cx, [128, 512], BF16, n=2, name="smix")

        def load_head(h):
            b = h % 2
            fr = head_free[b]
            S.dma("sp", lambda e: e.dma_start(out=qT[b][:], in_=scr["q1_T"][h * 128:(h + 1) * 128, :]), hs[b], [fr])
            S.dma("sp", lambda e: e.dma_start(out=kT[b][:], in_=scr["k1_T"][h * 128:(h + 1) * 128, :]), hs[b], [fr])
            tok = None
            for j in range(4):
                tok = S.dma("sp", lambda e, j=j: e.dma_start(
                    out=vv[b][:, j * 8:(j + 1) * 8, :],
                    in_=scr["v1"][h, j * 1024:(j + 1) * 1024, :].rearrange("(n p) f -> p n f", p=128)), hs[b], [fr])
            t_nk = S.op("pool", lambda e: e.tensor_scalar(out=nkT[b][:], in0=kT[b][:], scalar1=-1.0, scalar2=None, op0=ALU.mult),
                        [tok, fr])
            return [tok, t_nk]

        pairs = []
        for h in range(HS):
            for qt in range(NQT):
                np_ = 2 * qt + 2
                for j in range(np_):
                    pairs.append((h, qt, j, np_))
        NP = len(pairs)
        head_tok = {0: load_head(0)}
        st1, st2, st3, st4 = {}, {}, {}, {}
        ls_cur = {}
        ocur = {}

        def kbs(qt, j):
            return (4 * qt + 3 - 2 * j, 4 * qt + 2 - 2 * j)

        def stage_S(i):
            h, qt, j, np_ = pairs[i]
            b = h % 2
            i_s, Sp, sfr = pS.next()
            diag = j < 2
            m = None
            for t, kb in enumerate(kbs(qt, j)):
                m = S.op("pe", lambda e, t=t, kb=kb: e.matmul(
                    out=Sp[:, t, :], lhsT=kT[b][:, kb * 128:(kb + 1) * 128], rhs=qT[b][:, qt * 512:(qt + 1) * 512],
                    start=True, stop=(not diag)), [head_tok[h], sfr] + t_const)
                if diag:
                    m = S.op("pe", lambda e, t=t, kb=kb: e.matmul(
                        out=Sp[:, t, :], lhsT=ident[:], rhs=mSn[:, kb - 4 * qt, :], start=False, stop=True), [])
            st1[i] = (i_s, Sp, m)

        def stage_EL(i):
            i_s, Sp, m = st1.pop(i)
            ie, E, efr = Et.next()
            il, L, lfr = Lt.next()
            t_e = S.op("act", lambda e: e.activation(out=E[:], in_=Sp[:], func=AF.Exp), [m, efr])
            pS.release(i_s, t_e)
            t_l = S.op("act", lambda e: e.activation(out=L[:], in_=E[:], func=AF.Ln, bias=1.0, scale=1.0), [t_e])
            Et.release(ie, t_l)
            st2[i] = (il, L, t_l)

        def stage_C(i):
            h, qt, j, np_ = pairs[i]
            b = h % 2
            il, L, t_l = st2.pop(i)
            first, last, diag = j == 0, j == np_ - 1, j < 2
            i_c, Cp, cfr = pC.next()
            if not first:
                ils, Lsum, t_ls = ls_cur.pop(i)
            m = None
            for t, kb in enumerate(kbs(qt, j)):
                seq = [(tri[:], L[:, t, :], [t_l, cfr])]
                seq.append((nkT[b][:, kb * 128:(kb + 1) * 128], qT[b][:, qt * 512:(qt + 1) * 512], []))
                if diag:
                    seq.append((ident[:], mSp[:, kb - 4 * qt, :], []))
                if t == 1:
                    seq.append((ones[:], L[:, 0, :], []))
                if not first:
                    seq.append((ones[:], Lsum[:], [t_ls]))
                for k, (lh, rh, deps) in enumerate(seq):
                    m = S.op("pe", lambda e, t=t, lh_ap=lh, rh=rh, k=k, n=len(seq): e.matmul(
                        out=Cp[:, t, :], lhsT=lh_ap, rhs=rh, start=(k == 0), stop=(k == n - 1)), deps)
            users = [m]
            if not last:
                iln, Lnew, lnfr = Ls.next()
                if first:
                    t_n = S.op("dve", lambda e: e.tensor_tensor(out=Lnew[:], in0=L[:, 0, :], in1=L[:, 1, :], op=ALU.add), [t_l])
                else:
                    t_n0 = S.op("dve", lambda e: e.tensor_tensor(out=Lnew[:], in0=Lsum[:], in1=L[:, 0, :], op=ALU.add), [t_l, t_ls])
                    t_n = S.op("dve", lambda e: e.tensor_tensor(out=Lnew[:], in0=Lnew[:], in1=L[:, 1, :], op=ALU.add), [t_n0])
                ls_cur[i + 1] = (iln, Lnew, t_n)
                users.append(t_n)
            if not first:
                Ls.release(ils, list(users))
            Lt.release(il, list(users))
            st3[i] = (i_c, Cp, m)

        def stage_A(i):
            i_c, Cp, m = st3.pop(i)
            ia, A, afr = At.next()
            t_a = S.op("act", lambda e: e.activation(out=A[:], in_=Cp[:], func=AF.Exp, scale=-1.0), [m, afr])
            pC.release(i_c, t_a)
            st4[i] = (ia, A, t_a)

        def stage_PV(i):
            h, qt, j, np_ = pairs[i]
            b = h % 2
            ia, A, t_a = st4.pop(i)
            first, last = j == 0, j == np_ - 1
            if first:
                io, O, ofr = pO.next()
                gi, gt, gfr = gts.next()
                t_gate = S.dma("sp", lambda e: e.dma_start(
                    out=gt[:], in_=scr["gate1_T"][h * 128:(h + 1) * 128, qt * 512:(qt + 1) * 512]), gsl[gi], [gfr])
                ocur.update(io=io, O=O, ofr=ofr, gi=gi, gt=gt, t_gate=t_gate)
            O = ocur["O"]
            m = None
            for t, kb in enumerate(kbs(qt, j)):
                m = S.op("pe", lambda e, t=t, kb=kb: e.matmul(
                    out=O[:], lhsT=vv[b][:, kb, :], rhs=A[:, t, :], start=(first and t == 0), stop=(last and t == 1)),
                    [t_a, ocur["ofr"] if (first and t == 0) else None])
            At.release(ia, m)
            if last:
                gt = ocur["gt"]
                tc = stg.put("dve", lambda e, tl: e.tensor_tensor(out=tl[:], in0=O[:], in1=gt[:], op=ALU.mult),
                             [m, ocur["t_gate"]], scr["mix1_T"][h * 128:(h + 1) * 128, qt * 512:(qt + 1) * 512])
                pO.release(ocur["io"], tc)
                gts.release(ocur["gi"], tc)
                if qt == NQT - 1:
                    head_free[b] = m
                if qt == 0 and h + 1 < HS:
                    head_tok[h + 1] = load_head(h + 1)

        for step in range(-2, NP):
            if 0 <= step + 2 < NP:
                stage_S(step + 2)
                stage_EL(step + 2)
            if 0 <= step + 1 < NP:
                stage_C(step + 1)
                stage_A(step + 1)
            if 0 <= step < NP:
                stage_PV(step)
        cx.flush(extra_flush)


def phase_gather(nc, S, cfg, src, dst):
    groups = [[cfg.NR * i + j for j in range(cfg.NR)] for i in range(8 // cfg.NR)]
    sl = S.slot("pool")
    t = S.dma("pool", lambda e: e.collective_compute("AllGather", ALU.bypass, replica_groups=groups, ins=[src], outs=[dst]), sl)
    S.op("sp", lambda e: e.nop(), [t])
    S.emit()


def load_wo(S, wo, wo_dram):
    sw = S.slot("pool")
    t_w = None
    for cg in range(4):
        t_w = S.dma("pool", lambda e, cg=cg: e.dma_start(
            out=wo[:, :, cg * 512:(cg + 1) * 512],
            in_=wo_dram[:, cg * 512:(cg + 1) * 512].rearrange("(k p) n -> p k n", p=128)), sw)
    return t_w


def phase_out(nc, S, cfg, mixT_dram, wo_dram, xin_dram, xout_dram, final_gain=None, out_dram=None, wo_pre=None):
    SL, D = cfg.SL, cfg.D
    KC = D // 128
    with Ctx(nc, S) as cx:
        if wo_pre is not None:
            wo, t_w = wo_pre
        else:
            wo = cx.sb([128, KC, D], BF16, "wo")
            t_w = load_wo(S, wo, wo_dram)
        mts = Rot([cx.sb([128, KC, 128], BF16, "mt") for _ in range(3)])
        msl = [S.slot() for _ in range(3)]
        xts = Rot([cx.sb([128, D], F32, "xo") for _ in range(2)])
        xsl = [S.slot() for _ in range(2)]
        banks = [cx.ps([128, 512], F32, "po") for _ in range(8)]
        bfree = [None] * 8
        ystg = OutStage(cx, [128, D], F32, n=2, name="yst", dma_eng="sp")
        if final_gain is not None:
            gft, t_g = load_const(cx, final_gain.partition_broadcast(128), [128, D], F32, "gft")
            junk = cx.sb([128, D], BF16, "junk")
            ss = cx.sb([128, SL // 128], F32, "ss")
            rs = cx.sb([128, SL // 128], F32, "rs")
            ostg = OutStage(cx, [128, D], F32, n=2, name="ost", dma_eng="sp")
            ysb = Rot([cx.sb([128, D], F32, "ysb") for _ in range(2)])
        for tb in range(SL // 128):
            im, mt, mfr = mts.next()
            t_m = S.dma("sp", lambda e, mt=mt, tb=tb: e.dma_start(
                out=mt[:], in_=mixT_dram[:, tb * 128:(tb + 1) * 128].rearrange("(k p) t -> p k t", p=128)), msl[im], [mfr])
            ix, xt, xfr = xts.next()
            t_x = S.dma("sp", lambda e, xt=xt, tb=tb: e.dma_start(out=xt[:], in_=xin_dram[tb * 128:(tb + 1) * 128, :]), xsl[ix], [xfr])
            mms = []
            for cg in range(4):
                bi = (tb % 2) * 4 + cg
                bank = banks[bi]
                tm = None
                for kc in range(KC):
                    tm = S.op("pe", lambda e, bank=bank, mt=mt, kc=kc, cg=cg: e.matmul(
                        out=bank[:], lhsT=mt[:, kc, :], rhs=wo[:, kc, cg * 512:(cg + 1) * 512],
                        start=(kc == 0), stop=(kc == KC - 1)), [t_m, t_w, bfree[bi]])
                mms.append(tm)
            mts.release(im, mms[-1])
            if final_gain is None:
                comps = []
                for cg in range(4):
                    bank = banks[(tb % 2) * 4 + cg]
                    comps.append(("dve", lambda e, t, bank=bank, xt=xt, cg=cg: e.tensor_tensor(
                        out=t[:, cg * 512:(cg + 1) * 512], in0=bank[:], in1=xt[:, cg * 512:(cg + 1) * 512], op=ALU.add),
                        [mms[cg], t_x]))
                toks = ystg.put_multi(comps, xout_dram[tb * 128:(tb + 1) * 128, :])
                for cg in range(4):
                    bfree[(tb % 2) * 4 + cg] = toks[cg]
                xts.release(ix, toks[-1])
            else:
                iy, y, yfr = ysb.next()
                tl = yfr
                toks = []
                for cg in range(4):
                    bank = banks[(tb % 2) * 4 + cg]
                    tl = S.op("dve", lambda e, y=y, bank=bank, xt=xt, cg=cg: e.tensor_tensor(
                        out=y[:, cg * 512:(cg + 1) * 512], in0=bank[:], in1=xt[:, cg * 512:(cg + 1) * 512], op=ALU.add),
                        [mms[cg], t_x, tl])
                    bfree[(tb % 2) * 4 + cg] = tl
                    toks.append(tl)
                xts.release(ix, tl)
                t_ss = S.op("act", lambda e, y=y, tb=tb: e.activation(out=junk[:], in_=y[:], func=AF.Square,
                                                                      accum_out=ss[:, tb:tb + 1]), [tl])
                t_a = S.op("act", lambda e, tb=tb: e.activation(out=rs[:, tb:tb + 1], in_=ss[:, tb:tb + 1], func=AF.Sqrt,
                                                                bias=EPS, scale=1.0 / D), [t_ss])
                t_r = S.op("dve", lambda e, tb=tb: e.reciprocal(out=rs[:, tb:tb + 1], in_=rs[:, tb:tb + 1]), [t_a])
                tc = ostg.put("dve", lambda e, t, y=y, tb=tb: e.scalar_tensor_tensor(
                    out=t[:], in0=y[:], scalar=rs[:, tb:tb + 1], in1=gft[:], op0=ALU.mult, op1=ALU.mult),
                    [t_r, t_g], out_dram[tb * 128:(tb + 1) * 128, :])
                ysb.release(iy, tc)
        cx.flush()


def phase_proj1(nc, S, cfg, hT, w1, scr):
    SL, HS = cfg.SL, cfg.HS
    KC = cfg.D // 128
    NTT, NTB = SL // 512, SL // 128
    with Ctx(nc, S) as cx:
        pj = Proj(cx, cfg, hT, KC, None)
        ws = WStream(cx, KC)
        st16 = OutStage(cx, [128, 512], BF16, n=6, name="st16")
        groups = []
        c = 0
        for g in range(HS * 128 // 512):
            groups.append((c + g * 512, "T", [(scr["q1_T"], g * 512, 128.0 ** -0.5, None)]))
        c += HS * 128
        for g in range(HS * 128 // 512):
            groups.append((c + g * 512, "T", [(scr["k1_T"], g * 512, 1.0, None)]))
        c += HS * 128
        for g in range(HS * 128 // 512):
            groups.append((c + g * 512, "N", (scr["v1"], g * 4)))
        c += HS * 128
        for g in range(cfg.FM1 // 512):
            groups.append((c + g * 512, "T", [(scr["gate1_T"], g * 512, 1.0, AF.Silu)]))
        nxt = ws.load(w1[:, groups[0][0]:groups[0][0] + 512], 512)
        nev = 0
        for gi, (c0, kind, spec) in enumerate(groups):
            wi, wt, wtok = nxt
            if gi + 1 < len(groups):
                n0 = groups[gi + 1][0]
                nxt = ws.load(w1[:, n0:n0 + 512], 512)
            last = None
            if kind == "T":
                for cb in range(4):
                    for tt in range(NTT):
                        i, bank, tm = pj.mm_T(wt, cb * 128, 128, tt, wtok)
                        last = tm
                        tcs = []
                        for (dst, r0, scale, func) in spec:
                            eng = "act" if (func is not None or nev % 2 == 0) else "dve"
                            nev += 1
                            tcs.append(st16.put(eng, lambda e, t, eng=eng, bank=bank, scale=scale, func=func: evac(
                                eng, e, t[:], bank[:], scale, func), [tm],
                                dst[r0 + cb * 128:r0 + (cb + 1) * 128, tt * 512:(tt + 1) * 512]))
                        pj.banks.release(i, tcs)
            else:
                dst, h0 = spec
                for tb in range(NTB):
                    i, bank, tm = pj.mm_N(wt, 0, 512, tb, wtok)
                    last = tm
                    eng = "act" if nev % 2 == 0 else "dve"
                    nev += 1
                    tc = st16.put(eng, lambda e, t, eng=eng, bank=bank: evac(eng, e, t[:], bank[:]), [tm],
                                  dst[h0:h0 + 4, tb * 128:(tb + 1) * 128, :].rearrange("h p f -> p h f"),
                                  sub=lambda t: t[:].rearrange("p (h f) -> p h f", h=4))
                    pj.banks.release(i, tc)
            ws.release(wi, last)
        cx.flush()


def const_arrays(cfg):
    SL = cfg.SL
    bf = ml_dtypes.bfloat16
    c = {}
    c["ident"] = np.eye(128, dtype=np.float32).astype(bf)
    c["ones_bf"] = np.ones((128, 128), np.float32).astype(bf)
    c["ones32"] = np.ones((128, 128), np.float32)
    j = np.arange(128)[:, None]
    s = np.arange(128)[None, :]
    c["tri"] = (j >= s).astype(np.float32).astype(bf)
    t = np.arange(512)[None, None, :]
    i = np.arange(4)[None, :, None]
    jj = np.arange(128)[:, None, None]
    c["maskA"] = np.where(128 * i + jj <= t, 0.0, NEG).astype(np.float32).astype(bf)
    mneg = np.where(128 * i + jj < t, 0.0, NEG).astype(np.float32)
    c["maskSn"] = mneg.astype(bf)
    c["maskSp"] = (-mneg).astype(bf)
    qi = np.arange(128)[None, :]
    kj = np.arange(128)[:, None]
    mprev = np.where(kj >= qi, 0.0, NEG)
    mcur = np.where(kj <= qi, 0.0, NEG)
    dm = np.zeros((128, 6, 128), np.float32)
    for p in range(3):
        dm[:, 2 * p, :] = mprev
        dm[:, 2 * p + 1, :] = mcur
    c["dmask"] = dm
    half = 32
    inv = 1.0 / (10000.0 ** (np.arange(half, dtype=np.float32) / half))
    ang = np.arange(SL, dtype=np.float32)[None, :] * inv[:, None]
    ang = ang.astype(np.float32)
    cs = np.zeros((2, 64, SL), np.float32)
    cs[0, :32] = np.cos(ang)
    cs[0, 32:] = np.cos(ang)
    cs[1, :32] = np.sin(ang)
    cs[1, 32:] = np.sin(ang)
    c["cs"] = cs
    return c


def t5_bucket_np(dist):
    max_exact = 16
    d = np.maximum(dist.astype(np.float32), 1.0)
    large = max_exact + (np.log(d / max_exact) / math.log(2048 / max_exact) * (32 - max_exact)).astype(np.int32)
    large = np.minimum(large, 31)
    return np.where(dist < max_exact, dist, large)


def dil_bias_index():
    qi = np.arange(128)[None, :]
    kj = np.arange(128)[:, None]
    idx = np.zeros((6, 128, 128), np.int64)
    for p, d in enumerate((1, 4, 16)):
        rel_prev = np.maximum(128 + qi - kj, 0)
        rel_cur = np.maximum(qi - kj, 0)
        idx[2 * p] = t5_bucket_np((rel_prev * d).astype(np.int32))
        idx[2 * p + 1] = t5_bucket_np((rel_cur * d).astype(np.int32))
    return idx


def build_program(cfg, phases, debug=()):
    nc = bass.Bass("TRN2", target_bir_lowering=False)
    SL, D, HA, HB, HS = cfg.SL, cfg.D, cfg.HA, cfg.HB, cfg.HS

    def din(name, shape, dt=F32):
        return nc.dram_tensor(name, list(shape), dt, kind="ExternalInput").ap()

    def dscr(name, shape, dt):
        kind = "ExternalOutput" if name in debug else "Internal"
        return nc.dram_tensor(name, list(shape), dt, kind=kind).ap()

    I = {}
    I["x"] = din("x", [SL, D])
    I["g0"] = din("g0", [1, D])
    I["g1"] = din("g1", [1, D])
    I["gf"] = din("gf", [1, D])
    I["w0"] = din("w0", [D, cfg.W0C])
    I["qg"] = din("qg", [128, 4])
    I["kvg"] = din("kvg", [128, 4])
    I["wuq"] = din("wuq", [512, HA * 192])
    I["wukv"] = din("wukv", [512, HA * 256])
    I["wo0"] = din("wo0", [D, D])
    I["dbias"] = din("dbias", [HB, 128, 6, 128])
    I["w1"] = din("w1", [D, cfg.W1C])
    I["wo1"] = din("wo1", [D, D])
    I["ident"] = din("ident", [128, 128], BF16)
    I["ones_bf"] = din("ones_bf", [128, 128], BF16)
    I["ones32"] = din("ones32", [128, 128], F32)
    I["tri"] = din("tri", [128, 128], BF16)
    I["maskA"] = din("maskA", [128, 4, 512], BF16)
    I["maskSn"] = din("maskSn", [128, 4, 512], BF16)
    I["maskSp"] = din("maskSp", [128, 4, 512], BF16)
    I["dmask"] = din("dmask", [128, 6, 128], F32)
    I["cs"] = din("cs", [2, 64, SL], F32)
    out = nc.dram_tensor("out", [SL, D], F32, kind="ExternalOutput").ap()

    scr = {}
    scr["cq_T"] = dscr("cq_T", [512, SL], F32)
    scr["ckv_T"] = dscr("ckv_T", [512, SL], F32)
    rows0 = 64 + 3 * HB * 128 + cfg.FM0 + HA * 192 + 2 * HA * 128 + cfg.FM0
    rows1 = 3 * HS * 128 + 2 * cfg.FM1
    assert cfg.FM0 == cfg.FM1
    arena = dscr("arena16", [max(rows0, rows1), SL], BF16)
    pos = [0]

    def carve(nrows):
        v = arena[pos[0]:pos[0] + nrows, :]
        pos[0] += nrows
        return v

    def tokmajor(v, h):
        return v.rearrange("(h a) (b f) -> h (a b) f", h=h, f=128)

    scr["kr_T"] = carve(64)
    scr["qb_T"] = carve(HB * 128)
    scr["kb_T"] = carve(HB * 128)
    scr["vb"] = tokmajor(carve(HB * 128), HB)
    scr["gate_T"] = carve(cfg.FM0)
    scr["qa_T"] = carve(HA * 192).rearrange("(h r) s -> h r s", h=HA)
    scr["ka_T"] = carve(HA * 128).rearrange("(h r) s -> h r s", h=HA)
    scr["va"] = tokmajor(carve(HA * 128), HA)
    if cfg.NR == 1:
        scr["mix0_T"] = carve(cfg.FM0)
        scr["mixg0_T"] = scr["mix0_T"]
    else:
        cc_in = nc.dram_tensor("cc_in", [cfg.FM0, SL], BF16).ap()
        cc_out = nc.dram_tensor("cc_out", [cfg.NR * cfg.FM0, SL], BF16).ap()
        scr["mix0_T"] = cc_in
        scr["mixg0_T"] = cc_out
    pos[0] = 0
    scr["q1_T"] = carve(HS * 128)
    scr["k1_T"] = carve(HS * 128)
    scr["v1"] = tokmajor(carve(HS * 128), HS)
    scr["gate1_T"] = carve(cfg.FM1)
    if cfg.NR == 1:
        scr["mix1_T"] = carve(cfg.FM1)
        scr["mixg1_T"] = scr["mix1_T"]
    else:
        scr["mix1_T"] = cc_in
        scr["mixg1_T"] = cc_out
    scr["x1"] = out

    with contextlib.ExitStack() as es:
        S = Sched(nc, es)
        gcx = Ctx(nc, S)
        es.enter_context(gcx)
        ident = gcx.sb([128, 128], BF16, "ident")
        sl = S.slot()
        t_id = S.dma("sp", lambda e: e.dma_start(out=ident[:], in_=I["ident"]), sl)
        S.op("sp", lambda e: e.nop(), [t_id])
        S.emit()

        if "n0" in phases:
            hcx = Ctx(nc, S)
            hcx.__enter__()
            hT = hcx.sb([128, D // 128, SL], BF16, "hT")
            phase_norm(nc, S, cfg, I["x"], I["g0"], hT, ident)
            if "p0" in phases:
                phase_proj0(nc, S, cfg, hT, None, I["w0"], I["cs"], scr)
            hcx.__exit__(None, None, None)
        for ph in phases:
            if ph.startswith("dummy"):
                S.op("sp", lambda e: e.nop(), [])
                S.emit()
        if "u0" in phases:
            phase_up(nc, S, cfg, I["wuq"], I["wukv"], I["qg"], I["kvg"], I["cs"], I["ones32"], scr)
        if "a0" in phases:
            phase_mla(nc, S, cfg, I, scr)
        if "b0" in phases:
            phase_dil(nc, S, cfg, I, scr)
        if "o0" in phases:
            if cfg.NR > 1:
                phase_gather(nc, S, cfg, scr["mix0_T"], scr["mixg0_T"])
            phase_out(nc, S, cfg, scr["mixg0_T"], I["wo0"], I["x"], scr["x1"])
        if "n1" in phases:
            hcx = Ctx(nc, S)
            hcx.__enter__()
            hT = hcx.sb([128, D // 128, SL], BF16, "hT1")
            phase_norm(nc, S, cfg, I["x"] if "n1x" in phases else scr["x1"], I["g1"], hT, ident)
            if "p1" in phases:
                phase_proj1(nc, S, cfg, hT, I["w1"], scr)
            hcx.__exit__(None, None, None)
        wo_pre = None
        wcx = None
        if "s1" in phases and "o1" in phases:
            wcx = Ctx(nc, S)
            wcx.__enter__()
            wo1 = wcx.sb([128, D // 128, D], BF16, "wo1")
            wo_pre = (wo1, load_wo(S, wo1, I["wo1"]))
        if "s1" in phases:
            (phase_sb2 if SB_PAIRED else phase_sb)(nc, S, cfg, I, scr, extra_flush=[wo_pre[1]] if wo_pre else ())
        if "o1" in phases:
            if cfg.NR > 1:
                phase_gather(nc, S, cfg, scr["mix1_T"], scr["mixg1_T"])
            phase_out(nc, S, cfg, scr["mixg1_T"], I["wo1"], I["x"] if "n1x" in phases else scr["x1"], None,
                      final_gain=I["gf"], out_dram=out, wo_pre=wo_pre)
        if wcx is not None:
            wcx.__exit__(None, None, None)
    return nc


def make_in_maps(cfg, inp, ncores=8):
    HA, HB, HS, NR = cfg.HA, cfg.HB, cfg.HS, cfg.NR
    consts = const_arrays(cfg)
    bidx = dil_bias_index()
    f32 = np.float32
    x = np.asarray(inp["x"], f32)
    wie = np.asarray(inp["w_in_even"], f32)[0]
    wio = np.asarray(inp["w_in_odd"], f32)[0]
    wuq = np.asarray(inp["w_uq"], f32)[0]
    wukv = np.asarray(inp["w_ukv"], f32)[0]
    woe = np.asarray(inp["w_out_even"], f32)[0]
    woo = np.asarray(inp["w_out_odd"], f32)[0]
    rb = np.asarray(inp["rel_bias"], f32)
    ng = np.asarray(inp["norm_gain"], f32)
    rows0 = []
    for r in range(NR):
        rows0.extend(range(r * HA * 128, (r + 1) * HA * 128))
        rows0.extend(range(1024 + r * HB * 128, 1024 + (r + 1) * HB * 128))
    rows0 = np.array(rows0)
    maps = []
    for c in range(ncores):
        b = (c // NR) % x.shape[0]
        p = c % NR
        cols = list(range(0, 1088))
        for base in (1088, 2112, 3136):
            cols.extend(range(base + p * HB * 128, base + (p + 1) * HB * 128))
        cols.extend(range(4160 + p * HA * 128, 4160 + (p + 1) * HA * 128))
        cols.extend(range(4160 + 1024 + p * HB * 128, 4160 + 1024 + (p + 1) * HB * 128))
        cols1 = []
        for base in (0, 2048, 4096, 6144):
            cols1.extend(range(base + p * HS * 128, base + (p + 1) * HS * 128))
        heads_b = np.arange(p * HB, (p + 1) * HB)
        db = rb[bidx][:, :, :, heads_b]
        db = np.ascontiguousarray(db.transpose(3, 1, 0, 2))
        m = {
            "x": np.ascontiguousarray(x[b]),
            "g0": np.ascontiguousarray(ng[0][None, :]),
            "g1": np.ascontiguousarray(ng[1][None, :]),
            "gf": np.ascontiguousarray(np.asarray(inp["final_norm_gain"], f32)[None, :]),
            "w0": np.ascontiguousarray(wie[:, cols]),
            "qg": np.ascontiguousarray(np.asarray(inp["q_norm_gain"], f32)[0].reshape(4, 128).T),
            "kvg": np.ascontiguousarray(np.asarray(inp["kv_norm_gain"], f32)[0].reshape(4, 128).T),
            "wuq": np.ascontiguousarray(wuq[:, p * HA * 192:(p + 1) * HA * 192]),
            "wukv": np.ascontiguousarray(wukv[:, p * HA * 256:(p + 1) * HA * 256]),
            "wo0": np.ascontiguousarray(woe[rows0, :]),
            "dbias": db,
            "w1": np.ascontiguousarray(wio[:, cols1]),
            "wo1": np.ascontiguousarray(woo),
        }
        m.update(consts)
        maps.append(m)
    return maps


ALL_PHASES = ("n0", "p0", "u0", "a0", "b0", "o0", "n1", "p1", "s1", "o1")
PH_A = ("n0", "p0", "u0", "a0", "b0", "o0")
PH_B = ("n1x", "n1", "p1", "s1", "o1")


def kernel(**inputs):
    cfg = Cfg(NR=1)
    nb = np.asarray(inputs["x"]).shape[0]
    ncores = nb * cfg.NR
    nc = build_program(cfg, ALL_PHASES)
    maps = make_in_maps(cfg, inputs, ncores=ncores)
    res = run_bass_kernel_spmd(nc, maps, core_ids=list(range(ncores)))
    outs = [np.asarray(res.results[c]["out"], dtype=np.float32) for c in range(0, ncores, cfg.NR)]
    return np.stack(outs, 0)
```

```python
import contextlib
import math
import numpy as np
import ml_dtypes
import concourse.bass as bass
import concourse.mybir as mybir
from concourse.bass_utils import run_bass_kernel_spmd

F32 = mybir.dt.float32
BF16 = mybir.dt.bfloat16
AF = mybir.ActivationFunctionType
ALU = mybir.AluOpType

ENGS = ["pe", "act", "dve", "pool", "sp"]
NEG = -30000.0
EMBED_WAIT = True
SB_PAIRED = True
EPS = 1e-6


class Slot:
    def __init__(self, sem):
        self.sem = sem
        self.count = 0


class Sched:
    def __init__(self, nc, es, same_engine_sync=("act", "dve", "pool")):
        self.nc = nc
        self.es = es
        self.same = set(same_engine_sync)
        self.ops = {e: [] for e in ENGS}
        self.phase_id = 0
        self._begin_phase()

    def _begin_phase(self):
        self.phase_id += 1
        if not hasattr(self, "pool"):
            self.pool = {"eng": [], "sp": [], "pool": []}
            self.semval = {}
        self.pool_pos = {k: 0 for k in self.pool}
        self.sem = {}
        self.sem_key = {}
        self.cnt = {}
        for e in ENGS:
            self.sem[e], self.sem_key[e] = self._alloc("eng")
            self.cnt[e] = self.semval[self.sem_key[e]]
        self.waited = {e: {} for e in ENGS}
        self.slots = []

    def _alloc(self, kind):
        pool = self.pool[kind]
        i = self.pool_pos[kind]
        if i >= len(pool):
            pool.append(self.es.enter_context(self.nc.semaphore("sem_%s_%d" % (kind, len(pool)))))
            self.semval[(kind, i)] = 0
        self.pool_pos[kind] += 1
        return pool[i], (kind, i)

    def slot(self, kind="sp"):
        h, key = self._alloc(kind)
        sl = Slot(h)
        sl.count = self.semval[key]
        sl.key = key
        sl.kind = kind
        self.slots.append(sl)
        return sl

    def op(self, eng, fn, deps=()):
        self.cnt[eng] += 1
        tok = ("e", eng, self.cnt[eng])
        self.ops[eng].append((fn, self._flat(deps), tok))
        return tok

    def dma(self, eng, fn, slot, deps=()):
        assert slot.kind == eng, (slot.kind, eng)
        slot.count += 16
        tok = ("d", slot, slot.count)
        self.ops[eng].append((fn, self._flat(deps), tok))
        return tok

    def _flat(self, deps):
        out = []
        for d in deps:
            if d is None:
                continue
            if isinstance(d, list):
                out.extend(self._flat(d))
            else:
                out.append(d)
        return out

    def _emit_engine(self, eng, e):
        waited = self.waited[eng]
        for fn, deps, tok in self.ops[eng]:
            need = {}
            for d in deps:
                if d[0] == "e":
                    if d[1] == eng and (eng not in self.same or len(d) > 3):
                        continue
                    key = ("e", d[1])
                    sem = self.sem[d[1]]
                else:
                    key = ("d", id(d[1]))
                    sem = d[1].sem
                if waited.get(key, 0) >= d[2]:
                    continue
                if key not in need or need[key][1] < d[2]:
                    need[key] = (sem, d[2])
            items = list(need.items())
            for key, (sem, val) in items:
                waited[key] = val
            for key, (sem, val) in items[:-1]:
                e.wait_ge(sem, val)
            inst = fn(e)
            if items:
                sem, val = items[-1][1]
                if EMBED_WAIT:
                    inst._wait_ge(sem, val)
                else:
                    raise RuntimeError("standalone wait must precede instruction")
            if tok[0] == "e":
                inst.then_inc(self.sem[eng], 1)
            else:
                inst.then_inc(tok[1].sem, 16)
        self.ops[eng] = []

    def emit(self):
        nc = self.nc
        with nc.Block() as block:
            @block.tensor
            def _(e):
                self._emit_engine("pe", e)

            @block.scalar
            def _(e):
                self._emit_engine("act", e)

            @block.vector
            def _(e):
                self._emit_engine("dve", e)

            @block.gpsimd
            def _(e):
                self._emit_engine("pool", e)

            @block.sync
            def _(e):
                self._emit_engine("sp", e)
        for e in ENGS:
            self.semval[self.sem_key[e]] = self.cnt[e]
        for sl in self.slots:
            self.semval[sl.key] = sl.count
        self._begin_phase()


def war(tok):
    if tok is None:
        return None
    if isinstance(tok, list):
        return [war(t) for t in tok]
    if tok[0] == "e" and len(tok) == 3:
        return tok + ("war",)
    return tok


class Rot:
    def __init__(self, bufs):
        self.bufs = bufs
        self.free = [None] * len(bufs)
        self.k = 0

    def next(self):
        i = self.k % len(self.bufs)
        self.k += 1
        return i, self.bufs[i], war(self.free[i])

    def release(self, i, tok):
        self.free[i] = tok


class Ctx:
    uid = 0

    def __init__(self, nc, S):
        self.nc = nc
        self.S = S
        self.es = contextlib.ExitStack()
        self.n = 0
        self.stages = []

    def __enter__(self):
        self.es.__enter__()
        return self

    def __exit__(self, *a):
        return self.es.__exit__(*a)

    def sb(self, shape, dt, name=None):
        Ctx.uid += 1
        return self.es.enter_context(self.nc.sbuf_tensor("%s_%d" % (name or "t", Ctx.uid), shape, dt))

    def ps(self, shape, dt, name=None):
        Ctx.uid += 1
        return self.es.enter_context(self.nc.psum_tensor("%s_%d" % (name or "p", Ctx.uid), shape, dt))

    def flush(self, extra=()):
        toks = list(extra)
        for st in self.stages:
            toks.extend([t for t in st.last if t is not None])
        self.S.op("sp", lambda e: e.nop(), toks)
        self.S.emit()


class OutStage:
    def __init__(self, cx, shape, dt, n=3, dma_eng="pool", name="stg"):
        self.S = cx.S
        self.tiles = [cx.sb(shape, dt, name) for _ in range(n)]
        self.slots = [cx.S.slot(dma_eng) for _ in range(n)]
        self.last = [None] * n
        self.k = 0
        self.dma_eng = dma_eng
        cx.stages.append(self)

    def put(self, eng, compute, deps, dram_ap, sub=None):
        i = self.k % len(self.tiles)
        self.k += 1
        t = self.tiles[i]
        tc = self.S.op(eng, lambda e: compute(e, t), list(deps) + [self.last[i]])
        src = t[:] if sub is None else sub(t)
        self.last[i] = self.S.dma(self.dma_eng, lambda e: e.dma_start(out=dram_ap, in_=src), self.slots[i], [tc])
        return tc

    def put_multi(self, computes, dram_ap, sub=None):
        i = self.k % len(self.tiles)
        self.k += 1
        t = self.tiles[i]
        toks = []
        prev = self.last[i]
        for eng, fn, deps in computes:
            prev = self.S.op(eng, lambda e, fn=fn: fn(e, t), list(deps) + [prev])
            toks.append(prev)
        src = t[:] if sub is None else sub(t)
        self.last[i] = self.S.dma(self.dma_eng, lambda e: e.dma_start(out=dram_ap, in_=src), self.slots[i], [prev])
        return toks


def evac(eng, e, out, in_, scale=1.0, func=None):
    if eng == "act":
        return e.activation(out=out, in_=in_, func=(func or AF.Copy), scale=scale)
    assert func is None
    if scale == 1.0:
        return e.tensor_copy(out=out, in_=in_)
    return e.tensor_scalar(out=out, in0=in_, scalar1=float(scale), scalar2=None, op0=ALU.mult)


class Cfg:
    SL = 4096
    D = 2048

    def __init__(self, NR=1):
        self.NR = NR
        self.HA = 8 // NR
        self.HB = 8 // NR
        self.HS = 16 // NR

    @property
    def FM0(self):
        return (self.HA + self.HB) * 128

    @property
    def FM1(self):
        return self.HS * 128

    @property
    def W0C(self):
        return 1088 + 3 * self.HB * 128 + self.FM0

    @property
    def W1C(self):
        return 4 * self.HS * 128


def phase_norm(nc, S, cfg, x_dram, g_dram, hT, ident):
    SL, D = cfg.SL, cfg.D
    KC = D // 128
    with Ctx(nc, S) as cx:
        xt = [cx.sb([128, D], F32, "xt") for _ in range(2)]
        hb = [cx.sb([128, D], BF16, "hb") for _ in range(2)]
        junk = cx.sb([128, D], BF16, "junk")
        gt = cx.sb([128, D], F32, "gt")
        ss = cx.sb([128, SL // 128], F32, "ss")
        rs = cx.sb([128, SL // 128], F32, "rs")
        pT = Rot([cx.ps([128, 4, 128], BF16, "pT") for _ in range(4)])
        sx = [S.slot() for _ in range(2)]
        sg = S.slot()
        t_g = S.dma("sp", lambda e: e.dma_start(out=gt[:], in_=g_dram.partition_broadcast(128)), sg)
        x_free = [None, None]
        h_free = [None, None]
        nev = 0
        for tb in range(SL // 128):
            b = tb % 2
            t_x = S.dma("sp", lambda e, b=b, tb=tb: e.dma_start(out=xt[b][:], in_=x_dram[tb * 128:(tb + 1) * 128, :]),
                        sx[b], [x_free[b]])
            t_ss = S.op("act", lambda e, b=b, tb=tb: e.activation(out=junk[:], in_=xt[b][:], func=AF.Square,
                                                                  accum_out=ss[:, tb:tb + 1]), [t_x])
            t_a = S.op("act", lambda e, tb=tb: e.activation(out=rs[:, tb:tb + 1], in_=ss[:, tb:tb + 1], func=AF.Sqrt,
                                                            bias=EPS, scale=1.0 / D), [t_ss])
            t_r = S.op("dve", lambda e, tb=tb: e.reciprocal(out=rs[:, tb:tb + 1], in_=rs[:, tb:tb + 1]), [t_a])
            t_h = S.op("dve", lambda e, b=b, tb=tb: e.scalar_tensor_tensor(
                out=hb[b][:], in0=xt[b][:], scalar=rs[:, tb:tb + 1], in1=gt[:], op0=ALU.mult, op1=ALU.mult),
                [t_r, t_g, h_free[b]])
            x_free[b] = t_h
            tp = None
            for grp in range(KC // 4):
                i, pt, fr = pT.next()
                for j in range(4):
                    kc = grp * 4 + j
                    tp = S.op("pe", lambda e, pt=pt, j=j, kc=kc, b=b: e.transpose(
                        out=pt[:, j, :], in_=hb[b][:, kc * 128:(kc + 1) * 128], identity=ident[:]), [t_h, fr])
                eng = "act" if nev % 2 == 0 else "dve"
                nev += 1
                t_e = S.op(eng, lambda e, eng=eng, pt=pt, grp=grp, tb=tb: evac(
                    eng, e, hT[:, grp * 4:(grp + 1) * 4, tb * 128:(tb + 1) * 128], pt[:]), [tp])
                pT.release(i, t_e)
            h_free[b] = tp
        S.emit()


class Proj:
    def __init__(self, cx, cfg, srcT, KC, src_tok, nbanks=4, banks=None):
        self.cx, self.S, self.cfg = cx, cx.S, cfg
        self.srcT, self.KC, self.src_tok = srcT, KC, src_tok
        self.banks = banks or Rot([cx.ps([128, 512], F32, "pp") for _ in range(nbanks)])

    def mm_T(self, w, c0, ncol, tt, wtok, M=None):
        S = self.S
        i, bank, fr = self.banks.next()
        tm = None
        for kc in range(self.KC):
            tm = S.op("pe", lambda e, kc=kc, bank=bank: e.matmul(
                out=bank[0:ncol, :], lhsT=w[:, kc, c0:c0 + ncol], rhs=self.srcT[:, kc, tt * 512:(tt + 1) * 512],
                start=(kc == 0), stop=(kc == self.KC - 1)), [fr, wtok, self.src_tok])
        return i, bank, tm

    def mm_N(self, w, c0, ncol, tb, wtok):
        S = self.S
        i, bank, fr = self.banks.next()
        tm = None
        for kc in range(self.KC):
            tm = S.op("pe", lambda e, kc=kc, bank=bank: e.matmul(
                out=bank[:, 0:ncol], lhsT=self.srcT[:, kc, tb * 128:(tb + 1) * 128], rhs=w[:, kc, c0:c0 + ncol],
                start=(kc == 0), stop=(kc == self.KC - 1)), [fr, wtok, self.src_tok])
        return i, bank, tm


class WStream:
    def __init__(self, cx, KC, width=512, n=2):
        self.cx, self.S, self.KC = cx, cx.S, KC
        self.tiles = [cx.sb([128, KC, width], BF16, "wt") for _ in range(n)]
        self.slots = [cx.S.slot("pool") for _ in range(n)]
        self.free = [None] * n
        self.k = 0

    def load(self, w_ap, ncol):
        i = self.k % len(self.tiles)
        self.k += 1
        t = self.tiles[i]
        src = w_ap.rearrange("(k p) n -> p k n", p=128)
        tok = self.S.dma("pool", lambda e: e.dma_start(out=t[:, :, 0:ncol], in_=src), self.slots[i], [self.free[i]])
        return i, t, tok

    def release(self, i, tok):
        self.free[i] = tok


def phase_proj0(nc, S, cfg, hT, h_tok, w0, cs_dram, scr):
    SL, HA, HB = cfg.SL, cfg.HA, cfg.HB
    KC = cfg.D // 128
    NTT, NTB = SL // 512, SL // 128
    with Ctx(nc, S) as cx:
        pj = Proj(cx, cfg, hT, KC, h_tok)
        ws = WStream(cx, KC)
        st32 = OutStage(cx, [128, 512], F32, n=2, name="st32")
        st16 = OutStage(cx, [128, 512], BF16, n=4, name="st16")
        groups = []
        groups.append((0, 512, "T", ("f32", scr["cq_T"], 0, 1.0, None)))
        groups.append((512, 512, "T", ("f32", scr["ckv_T"], 0, 1.0, None)))
        groups.append((1024, 64, "R", None))
        c = 1088
        for g in range(HB * 128 // 512):
            groups.append((c + g * 512, 512, "T", ("bf", scr["qb_T"], g * 512, 128.0 ** -0.5, None)))
        c += HB * 128
        for g in range(HB * 128 // 512):
            groups.append((c + g * 512, 512, "T", ("bf", scr["kb_T"], g * 512, 1.0, None)))
        c += HB * 128
        for g in range(HB * 128 // 512):
            groups.append((c + g * 512, 512, "N", (scr["vb"], g * 4)))
        c += HB * 128
        for g in range(cfg.FM0 // 512):
            groups.append((c + g * 512, 512, "T", ("bf", scr["gate_T"], g * 512, 1.0, AF.Silu)))
        nxt = ws.load(w0[:, groups[0][0]:groups[0][0] + groups[0][1]], groups[0][1])
        nev = 0
        for gi, (c0, ncol, kind, spec) in enumerate(groups):
            wi, wt, wtok = nxt
            if gi + 1 < len(groups):
                n0, nn = groups[gi + 1][0], groups[gi + 1][1]
                nxt = ws.load(w0[:, n0:n0 + nn], nn)
            last = None
            if kind == "T":
                typ, dst, r0, scale, func = spec
                for cb in range(ncol // 128):
                    for tt in range(NTT):
                        i, bank, tm = pj.mm_T(wt, cb * 128, 128, tt, wtok)
                        last = tm
                        eng = "act" if (func is not None or nev % 2 == 0) else "dve"
                        nev += 1
                        stg = st32 if typ == "f32" else st16
                        tc = stg.put(eng, lambda e, t, eng=eng, bank=bank, scale=scale, func=func: evac(
                            eng, e, t[:], bank[:], scale, func), [tm],
                            dst[r0 + cb * 128:r0 + (cb + 1) * 128, tt * 512:(tt + 1) * 512])
                        pj.banks.release(i, tc)
            elif kind == "N":
                dst, h0 = spec
                for tb in range(NTB):
                    i, bank, tm = pj.mm_N(wt, 0, ncol, tb, wtok)
                    last = tm
                    eng = "act" if nev % 2 == 0 else "dve"
                    nev += 1
                    tc = st16.put(eng, lambda e, t, eng=eng, bank=bank: evac(eng, e, t[:], bank[:]), [tm],
                                  dst[h0:h0 + 4, tb * 128:(tb + 1) * 128, :].rearrange("h p f -> p h f"),
                                  sub=lambda t: t[:].rearrange("p (h f) -> p h f", h=4))
                    pj.banks.release(i, tc)
            else:
                last = Rope(cx, pj, cfg, cs_dram, st16).run(wt, 0, wtok, 1.0, scr["kr_T"])
            ws.release(wi, last)
        cx.flush()


def latent_norm(cx, cfg, c_dram, gain_dram, cnT, ones32, pbanks, ones_tok):
    S = cx.S
    SL = cfg.SL
    cT = cx.sb([128, 4, SL], F32, "cT")
    gn = cx.sb([128, 4], F32, "gn")
    sq = [cx.sb([128, 4, 512], F32, "sq") for _ in range(2)]
    rt = [cx.sb([128, 512], F32, "rt") for _ in range(2)]
    sl = S.slot()
    sg = S.slot()
    t_g = S.dma("sp", lambda e: e.dma_start(out=gn[:], in_=gain_dram), sg)
    t_c = None
    for k in range(4):
        t_c = S.dma("sp", lambda e, k=k: e.dma_start(out=cT[:, k, :], in_=c_dram[k * 128:(k + 1) * 128, :]), sl)
    sq_free = [None, None]
    rt_free = [None, None]
    last = None
    for tt in range(SL // 512):
        b = tt % 2
        t_s = S.op("act", lambda e, b=b, tt=tt: e.activation(out=sq[b][:], in_=cT[:, :, tt * 512:(tt + 1) * 512], func=AF.Square),
                   [t_c, sq_free[b]])
        i, bank, fr = pbanks.next()
        tm = None
        for k in range(4):
            tm = S.op("pe", lambda e, k=k, b=b, bank=bank: e.matmul(out=bank[:], lhsT=ones32[:], rhs=sq[b][:, k, :],
                                                                   start=(k == 0), stop=(k == 3)), [t_s, fr, ones_tok])
        sq_free[b] = tm
        t_a = S.op("act", lambda e, b=b, bank=bank: e.activation(out=rt[b][:], in_=bank[:], func=AF.Sqrt, bias=EPS, scale=1.0 / 512),
                   [tm, rt_free[b]])
        pbanks.release(i, t_a)
        t_r = S.op("dve", lambda e, b=b: e.reciprocal(out=rt[b][:], in_=rt[b][:]), [t_a])
        tn = t_r
        for k in range(4):
            tn = S.op("dve", lambda e, k=k, b=b, tt=tt: e.scalar_tensor_tensor(
                out=cnT[:, k, tt * 512:(tt + 1) * 512], in0=cT[:, k, tt * 512:(tt + 1) * 512], scalar=gn[:, k:k + 1],
                in1=rt[b][:], op0=ALU.mult, op1=ALU.mult), [t_r, t_g, tn])
        rt_free[b] = tn
        last = tn
    return last


def phase_up(nc, S, cfg, wuq_d, wukv_d, qg_d, kvg_d, cs_dram, ones32_d, scr):
    SL, HA = cfg.SL, cfg.HA
    NTT, NTB = SL // 512, SL // 128
    for which in ("q", "kv"):
        with Ctx(nc, S) as cx:
            ones32 = cx.sb([128, 128], F32, "ones32")
            so = S.slot()
            t_o = S.dma("sp", lambda e: e.dma_start(out=ones32[:], in_=ones32_d), so)
            cnT = cx.sb([128, 4, SL], BF16, "cnT")
            banks = Rot([cx.ps([128, 512], F32, "pp") for _ in range(4)])
            st16 = OutStage(cx, [128, 512], BF16, n=4, name="st16")
            ncols = HA * 192 if which == "q" else HA * 256
            wt = cx.sb([128, 4, ncols], BF16, "wup")
            sw = S.slot("pool")
            wd = wuq_d if which == "q" else wukv_d
            t_w = S.dma("pool", lambda e: e.dma_start(out=wt[:], in_=wd.rearrange("(k p) n -> p k n", p=128)), sw)
            t_n = latent_norm(cx, cfg, scr["cq_T"] if which == "q" else scr["ckv_T"],
                              qg_d if which == "q" else kvg_d, cnT, ones32, banks, t_o)
            pj = Proj(cx, cfg, cnT, 4, t_n, banks=banks)
            nev = 0
            if which == "q":
                sc = 192.0 ** -0.5
                rp = Rope(cx, pj, cfg, cs_dram, st16)
                for h in range(HA):
                    for tt in range(NTT):
                        i, bank, tm = pj.mm_T(wt, h * 192, 128, tt, t_w)
                        eng = "act" if nev % 2 == 0 else "dve"
                        nev += 1
                        tc = st16.put(eng, lambda e, t, eng=eng, bank=bank: evac(eng, e, t[:], bank[:], sc), [tm],
                                      scr["qa_T"][h, 0:128, tt * 512:(tt + 1) * 512])
                        pj.banks.release(i, tc)
                    rp.run(wt, h * 192 + 128, t_w, sc, scr["qa_T"][h, 128:192, :])
            else:
                for h in range(HA):
                    for tt in range(NTT):
                        i, bank, tm = pj.mm_T(wt, h * 256, 128, tt, t_w)
                        eng = "act" if nev % 2 == 0 else "dve"
                        nev += 1
                        tc = st16.put(eng, lambda e, t, eng=eng, bank=bank: evac(eng, e, t[:], bank[:]), [tm],
                                      scr["ka_T"][h, :, tt * 512:(tt + 1) * 512])
                        pj.banks.release(i, tc)
                    for tb in range(NTB):
                        i, bank, tm = pj.mm_N(wt, h * 256 + 128, 128, tb, t_w)
                        eng = "act" if nev % 2 == 0 else "dve"
                        nev += 1
                        tc = st16.put(eng, lambda e, t, eng=eng, bank=bank: evac(eng, e, t[:, 0:128], bank[:, 0:128]), [tm],
                                      scr["va"][h, tb * 128:(tb + 1) * 128, :], sub=lambda t: t[:, 0:128])
                        pj.banks.release(i, tc)
            cx.flush()


class Rope:
    def __init__(self, cx, pj, cfg, cs_dram, out_stage):
        self.cx, self.pj, self.cfg, self.cs_dram, self.out_stage = cx, pj, cfg, cs_dram, out_stage
        S = cx.S
        self.wr = cx.sb([128, pj.KC, 64], BF16, "wrot")
        self.cst = [cx.sb([64, 2, 512], F32, "cs") for _ in range(2)]
        self.css = [S.slot() for _ in range(2)]
        self.csfree = [None, None]
        self.tmp = [cx.sb([64, 2, 512], F32, "rtmp") for _ in range(2)]
        self.last_mm = None
        self.k = 0

    def run(self, w, c0, wtok, scale, dst_rows):
        S = self.cx.S
        pj, wr, cst, tmp = self.pj, self.wr, self.cst, self.tmp
        t1 = S.op("act", lambda e: e.activation(out=wr[:, :, 0:32], in_=w[:, :, c0 + 32:c0 + 64], func=AF.Copy, scale=-1.0),
                  [wtok, self.last_mm])
        t2 = S.op("act", lambda e: e.activation(out=wr[:, :, 32:64], in_=w[:, :, c0:c0 + 32], func=AF.Copy, scale=1.0),
                  [wtok, t1])
        m2 = None
        for tt in range(self.cfg.SL // 512):
            b = self.k % 2
            self.k += 1
            t_cs = S.dma("sp", lambda e, b=b, tt=tt: e.dma_start(
                out=cst[b][:], in_=self.cs_dram[:, :, tt * 512:(tt + 1) * 512].rearrange("c p t -> p c t")),
                self.css[b], [self.csfree[b]])
            i1, b1, m1 = pj.mm_T(w, c0, 64, tt, wtok)
            i2, b2, m2 = pj.mm_T(wr, 0, 64, tt, t2)
            ta = S.op("dve", lambda e, b=b, b1=b1: e.scalar_tensor_tensor(
                out=tmp[b][:, 0, :], in0=b1[0:64, :], scalar=float(scale), in1=cst[b][:, 0, :], op0=ALU.mult, op1=ALU.mult),
                [m1, t_cs, self.csfree[b]])
            tb_ = S.op("dve", lambda e, b=b, b2=b2: e.scalar_tensor_tensor(
                out=tmp[b][:, 1, :], in0=b2[0:64, :], scalar=float(scale), in1=cst[b][:, 1, :], op0=ALU.mult, op1=ALU.mult),
                [m2, t_cs, ta])
            pj.banks.release(i1, ta)
            pj.banks.release(i2, tb_)
            tc = self.out_stage.put("dve", lambda e, t, b=b: e.tensor_tensor(
                out=t[0:64, :], in0=tmp[b][:, 0, :], in1=tmp[b][:, 1, :], op=ALU.add),
                [ta, tb_], dst_rows[:, tt * 512:(tt + 1) * 512], sub=lambda t: t[0:64, :])
            self.csfree[b] = tc
        self.last_mm = m2
        return m2


def load_const(cx, dram_ap, shape, dt, name):
    t = cx.sb(shape, dt, name)
    sl = cx.S.slot()
    tok = cx.S.dma("sp", lambda e: e.dma_start(out=t[:], in_=dram_ap), sl)
    return t, tok


def phase_mla(nc, S, cfg, I, scr):
    SL, HA = cfg.SL, cfg.HA
    NQT, NB = SL // 512, SL // 128
    with Ctx(nc, S) as cx:
        ident, t_c0 = load_const(cx, I["ident"], [128, 128], BF16, "ident")
        ones, t_c1 = load_const(cx, I["ones_bf"], [128, 128], BF16, "ones")
        maskA, t_c2 = load_const(cx, I["maskA"], [128, 4, 512], BF16, "maskA")
        kr, t_kr = load_const(cx, scr["kr_T"], [64, SL], BF16, "kr")
        t_const = [t_c0, t_c1, t_c2, t_kr]
        qn = [cx.sb([128, SL], BF16, "qn") for _ in range(2)]
        qr = [cx.sb([64, SL], BF16, "qr") for _ in range(2)]
        kn = [cx.sb([128, SL], BF16, "kn") for _ in range(2)]
        vv = [cx.sb([128, NB, 128], BF16, "vv") for _ in range(2)]
        hs = [S.slot() for _ in range(2)]
        head_free = [None, None]
        pS = Rot([cx.ps([128, 512], F32, "pS") for _ in range(2)])
        pO = Rot([cx.ps([128, 512], F32, "pO") for _ in range(2)])
        pD = Rot([cx.ps([128, 512], F32, "pD") for _ in range(2)])
        pts = Rot([cx.sb([128, 512], BF16, "pt") for _ in range(3)])
        gts = Rot([cx.sb([128, 512], BF16, "gt") for _ in range(2)])
        gsl = [S.slot() for _ in range(2)]
        rden = Rot([cx.sb([128, 512], F32, "rden") for _ in range(2)])
        of = Rot([cx.sb([128, 512], F32, "of") for _ in range(2)])
        stg = OutStage(cx, [128, 512], BF16, n=2, name="mixst")

        def load_head(h):
            b = h % 2
            fr = head_free[b]
            S.dma("sp", lambda e: e.dma_start(out=qn[b][:], in_=scr["qa_T"][h, 0:128, :]), hs[b], [fr])
            S.dma("sp", lambda e: e.dma_start(out=qr[b][:], in_=scr["qa_T"][h, 128:192, :]), hs[b], [fr])
            S.dma("sp", lambda e: e.dma_start(out=kn[b][:], in_=scr["ka_T"][h, :, :]), hs[b], [fr])
            tok = None
            for j in range(4):
                tok = S.dma("sp", lambda e, j=j: e.dma_start(
                    out=vv[b][:, j * 8:(j + 1) * 8, :],
                    in_=scr["va"][h, j * 1024:(j + 1) * 1024, :].rearrange("(n p) f -> p n f", p=128)), hs[b], [fr])
            return tok

        blocks = []
        for h in range(HA):
            for qt in range(NQT):
                nkb = 4 * qt + 4
                for kb in range(nkb):
                    blocks.append((h, qt, kb, nkb))
        head_tok = {0: load_head(0)}
        state = {}

        def issue_S(bi):
            h, qt, kb, nkb = blocks[bi]
            b = h % 2
            i_s, Sb, sfr = pS.next()
            diag = kb >= 4 * qt
            S.op("pe", lambda e: e.matmul(out=Sb[:], lhsT=kn[b][:, kb * 128:(kb + 1) * 128], rhs=qn[b][:, qt * 512:(qt + 1) * 512],
                                         start=True, stop=False), [head_tok[h], sfr] + t_const)
            m = S.op("pe", lambda e: e.matmul(out=Sb[:], lhsT=kr[:, kb * 128:(kb + 1) * 128], rhs=qr[b][:, qt * 512:(qt + 1) * 512],
                                             start=False, stop=(not diag)), [])
            if diag:
                m = S.op("pe", lambda e: e.matmul(out=Sb[:], lhsT=ident[:], rhs=maskA[:, kb - 4 * qt, :], start=False, stop=True), [])
            state[bi] = (i_s, Sb, m)

        issue_S(0)
        cur = {}
        for bi, (h, qt, kb, nkb) in enumerate(blocks):
            b = h % 2
            if bi + 1 < len(blocks):
                issue_S(bi + 1)
            i_s, Sb, m = state.pop(bi)
            if kb == 0:
                io, O, ofr = pO.next()
                idn, Dn, dfr = pD.next()
                gi, gt, gfr = gts.next()
                t_gate = S.dma("sp", lambda e, gt=gt, h=h, qt=qt: e.dma_start(
                    out=gt[:], in_=scr["gate_T"][h * 128:(h + 1) * 128, qt * 512:(qt + 1) * 512]), gsl[gi], [gfr])
                cur = dict(io=io, O=O, ofr=ofr, idn=idn, Dn=Dn, dfr=dfr, gi=gi, gt=gt, t_gate=t_gate)
            O, Dn = cur["O"], cur["Dn"]
            ip, pt, pfr = pts.next()
            t_e = S.op("act", lambda e, pt=pt, Sb=Sb: e.activation(out=pt[:], in_=Sb[:], func=AF.Exp), [m, pfr])
            pS.release(i_s, t_e)
            first, last = kb == 0, kb == nkb - 1
            S.op("pe", lambda e, O=O, pt=pt, b=b, kb=kb, first=first, last=last: e.matmul(
                out=O[:], lhsT=vv[b][:, kb, :], rhs=pt[:], start=first, stop=last), [t_e, cur["ofr"] if first else None])
            m5 = S.op("pe", lambda e, Dn=Dn, pt=pt, first=first, last=last: e.matmul(
                out=Dn[:], lhsT=ones[:], rhs=pt[:], start=first, stop=last), [t_e, cur["dfr"] if first else None])
            pts.release(ip, m5)
            if last:
                ir, rd, rfr = rden.next()
                io2, o32, o32fr = of.next()
                t_r = S.op("dve", lambda e, rd=rd, Dn=Dn: e.reciprocal(out=rd[:], in_=Dn[:]), [m5, rfr])
                pD.release(cur["idn"], t_r)
                t_o = S.op("dve", lambda e, o32=o32, O=O, rd=rd: e.tensor_tensor(out=o32[:], in0=O[:], in1=rd[:], op=ALU.mult),
                           [m5, t_r, o32fr])
                pO.release(cur["io"], t_o)
                rden.release(ir, t_o)
                gt = cur["gt"]
                tc = stg.put("dve", lambda e, t, o32=o32, gt=gt: e.tensor_tensor(out=t[:], in0=o32[:], in1=gt[:], op=ALU.mult),
                             [t_o, cur["t_gate"]], scr["mix0_T"][h * 128:(h + 1) * 128, qt * 512:(qt + 1) * 512])
                of.release(io2, tc)
                gts.release(cur["gi"], tc)
                if qt == NQT - 1:
                    head_free[b] = m5
                if qt == 0 and h + 1 < HA:
                    head_tok[h + 1] = load_head(h + 1)
        cx.flush()


DILS = (1, 4, 16)


def dil_cols(T, d, r, n):
    if d == 1:
        return T[:, n * 128:(n + 1) * 128]
    return T[:, :].rearrange("p (m d) -> p d m", d=d)[:, r, n * 128:(n + 1) * 128]


def phase_dil(nc, S, cfg, I, scr):
    SL, HA, HB = cfg.SL, cfg.HA, cfg.HB
    NB = SL // 128
    NST = SL // 2048
    with Ctx(nc, S) as cx:
        ident, t_c0 = load_const(cx, I["ident"], [128, 128], BF16, "ident")
        ones, t_c1 = load_const(cx, I["ones_bf"], [128, 128], BF16, "ones")
        dmask, t_c2 = load_const(cx, I["dmask"], [128, 6, 128], F32, "dmask")
        t_const = [t_c0, t_c1, t_c2]
        qT = [cx.sb([128, SL], BF16, "dq") for _ in range(2)]
        kT = [cx.sb([128, SL], BF16, "dk") for _ in range(2)]
        vd = [[cx.sb([128, NB, 128], BF16, "dv") for _ in range(3)] for _ in range(2)]
        qP = {4: cx.sb([128, SL], BF16, "dqp4"), 16: cx.sb([128, SL], BF16, "dqp16")}
        kP = {4: cx.sb([128, SL], BF16, "dkp4"), 16: cx.sb([128, SL], BF16, "dkp16")}
        perm_free = [None]
        perm_head = [None]
        bmf1 = cx.sb([128, 6, 128], F32, "bmf")
        bsm1 = cx.sb([128, 2, 6, 128], BF16, "bsm")
        blo1 = cx.sb([128, 6, 128], F32, "blo")
        bmf, bsm, blo = [bmf1, bmf1], [bsm1, bsm1], [blo1, blo1]
        prep_free = [None]
        bh = [cx.sb([128, 6, 512], BF16, "bh") for _ in range(2)]
        bl = [cx.sb([128, 6, 512], BF16, "bl") for _ in range(2)]
        hs = [S.slot() for _ in range(2)]
        head_free = [None, None]
        pS = Rot([cx.ps([128, 512], F32, "pS") for _ in range(3)])
        pO = Rot([cx.ps([128, 512], F32, "pO") for _ in range(2)])
        pD = Rot([cx.ps([128, 512], F32, "pD") for _ in range(2)])
        pts = Rot([cx.sb([128, 512], BF16, "pt") for _ in range(4)])
        nacc = [cx.sb([128, 2048], F32, "nacc") for _ in range(2)]
        dacc = [cx.sb([128, 2048], F32, "dacc") for _ in range(2)]
        acc_free = [None, None]
        gts = [cx.sb([128, 2048], BF16, "dgt") for _ in range(2)]
        gsl = [S.slot() for _ in range(2)]
        g_free = [None, None]
        stg = OutStage(cx, [128, 2048], BF16, n=2, name="dmix")

        def load_head(hb):
            b = hb % 2
            fr = head_free[b]
            S.dma("sp", lambda e: e.dma_start(out=qT[b][:], in_=scr["qb_T"][hb * 128:(hb + 1) * 128, :]), hs[b], [fr])
            S.dma("sp", lambda e: e.dma_start(out=kT[b][:], in_=scr["kb_T"][hb * 128:(hb + 1) * 128, :]), hs[b], [fr])
            S.dma("sp", lambda e: e.dma_start(out=bmf[b][:], in_=I["dbias"][hb]), hs[b], [fr, prep_free[0]])
            for j in range(4):
                S.dma("sp", lambda e, j=j: e.dma_start(
                    out=vd[b][0][:, j * 8:(j + 1) * 8, :],
                    in_=scr["vb"][hb, j * 1024:(j + 1) * 1024, :].rearrange("(n p) f -> p n f", p=128)), hs[b], [fr])
            tok = None
            for p, d in ((1, 4), (2, 16)):
                nper = NB // d
                src = scr["vb"][hb].rearrange("(n i r) f -> r i n f", i=128, r=d)
                for r in range(d):
                    tok = S.dma("sp", lambda e, p=p, r=r, nper=nper, src=src: e.dma_start(
                        out=vd[b][p][:, r * nper:(r + 1) * nper, :], in_=src[r]), hs[b], [fr])
            t1 = S.op("dve", lambda e: e.tensor_tensor(out=bmf[b][:], in0=bmf[b][:], in1=dmask[:], op=ALU.add), [tok, t_c2, fr])
            t2 = S.op("dve", lambda e: e.tensor_copy(out=bsm[b][:, 0], in_=bmf[b][:]), [t1])
            t3 = S.op("dve", lambda e: e.tensor_tensor(out=blo[b][:], in0=bmf[b][:], in1=bsm[b][:, 0], op=ALU.subtract), [t2])
            t4 = S.op("dve", lambda e: e.tensor_copy(out=bsm[b][:, 1], in_=blo[b][:]), [t3])
            tl = t4
            for q in range(4):
                tl = S.op("dve", lambda e, q=q: e.tensor_copy(out=bh[b][:, :, q * 128:(q + 1) * 128], in_=bsm[b][:, 0]), [t4, tl])
                tl = S.op("dve", lambda e, q=q: e.tensor_copy(out=bl[b][:, :, q * 128:(q + 1) * 128], in_=bsm[b][:, 1]), [t4, tl])
            prep_free[0] = tl
            return [tok, tl]

        def permute_head(hb):
            b = hb % 2
            toks = []
            for d in (4, 16):
                for src, dst in ((qT[b], qP[d]), (kT[b], kP[d])):
                    toks.append(S.op("pool", lambda e, src=src, dst=dst, d=d: e.tensor_copy(
                        out=dst[:, :].rearrange("p (d m) -> p d m", d=d), in_=src[:, :].rearrange("p (m d) -> p d m", d=d)),
                        [head_tok[hb], perm_free[0]]))
            perm_head[0] = hb
            return toks

        passes = []
        for hb in range(HB):
            for st in range(NST):
                for p, d in enumerate(DILS):
                    for g in range(4):
                        quarters = []
                        for q in range(4):
                            if p == 0:
                                r, n = 0, st * 16 + g * 4 + q
                            elif p == 1:
                                r, n = q, st * 4 + g
                            else:
                                r, n = g * 4 + q, st
                            quarters.append((q, r, n))
                        has_prev = [qq for qq in quarters if qq[2] >= 1]
                        passes.append(dict(hb=hb, st=st, p=p, d=d, g=g, kind=1, qs=quarters, last=(len(has_prev) == 0)))
                        if has_prev:
                            passes.append(dict(hb=hb, st=st, p=p, d=d, g=g, kind=0, qs=has_prev, last=True))
        head_tok = {0: load_head(0)}
        state = {}

        def vslot(p, d, r, n):
            return r * (NB // d) + n

        def pcols(T, TP, d, r, n):
            if d == 1:
                return T[:, n * 128:(n + 1) * 128]
            c = r * (SL // d) + n * 128
            return TP[d][:, c:c + 128]

        def issue_S(pi):
            P = passes[pi]
            hb, st, p, d, g, kind = P["hb"], P["st"], P["p"], P["d"], P["g"], P["kind"]
            b = hb % 2
            if perm_head[0] != hb:
                head_tok[hb] = [head_tok[hb], permute_head(hb)]
            i_s, Sb, sfr = pS.next()
            q0 = P["qs"][0][0]
            c0 = q0 * 128
            bi = 2 * p + kind
            S.op("pe", lambda e: e.matmul(out=Sb[:, c0:512], lhsT=ident[:], rhs=bh[b][:, bi, c0:512], start=True, stop=False,
                                         skip_group_check=True), [head_tok[hb], sfr] + t_const)
            m = S.op("pe", lambda e: e.matmul(out=Sb[:, c0:512], lhsT=ident[:], rhs=bl[b][:, bi, c0:512], start=False, stop=False,
                                             skip_group_check=True), [])
            nq = len(P["qs"])
            for j, (q, r, n) in enumerate(P["qs"]):
                nk = n if kind == 1 else n - 1
                m = S.op("pe", lambda e, q=q, r=r, n=n, nk=nk, j=j: e.matmul(
                    out=Sb[:, q * 128:(q + 1) * 128], lhsT=pcols(kT[b], kP, d, r, nk), rhs=pcols(qT[b], qP, d, r, n),
                    start=False, stop=(j == nq - 1), skip_group_check=True), [])
            state[pi] = (i_s, Sb, m, c0)

        LOOK = 2
        issued = [0]

        def issue_upto(k):
            while issued[0] <= k and issued[0] < len(passes):
                issue_S(issued[0])
                issued[0] += 1

        issue_upto(0)
        cur = {}
        acc_toks = []
        for pi, P in enumerate(passes):
            hb, st, p, d, g, kind = P["hb"], P["st"], P["p"], P["d"], P["g"], P["kind"]
            b = hb % 2
            ab = (hb * NST + st) % 2
            lim = pi
            while lim + 1 < len(passes) and lim + 1 <= pi + LOOK and passes[lim + 1]["hb"] == hb:
                lim += 1
            issue_upto(lim)
            defer = pi + 1 < len(passes) and passes[pi + 1]["hb"] != hb
            i_s, Sb, m, c0 = state.pop(pi)
            if kind == 1:
                io, O, ofr = pO.next()
                idn, Dn, dfr = pD.next()
                cur = dict(io=io, O=O, ofr=ofr, idn=idn, Dn=Dn, dfr=dfr)
                if p == 0 and g == 0:
                    acc_toks = []
                    t_gate = S.dma("sp", lambda e, ab=ab, hb=hb, st=st: e.dma_start(
                        out=gts[ab][:], in_=scr["gate_T"][(HA + hb) * 128:(HA + hb + 1) * 128, st * 2048:(st + 1) * 2048]),
                        gsl[ab], [g_free[ab]])
            O, Dn = cur["O"], cur["Dn"]
            ip, pt, pfr = pts.next()
            t_e = S.op("act", lambda e, pt=pt, Sb=Sb, c0=c0: e.activation(out=pt[:, c0:512], in_=Sb[:, c0:512], func=AF.Exp), [m, pfr])
            pS.release(i_s, t_e)
            nq = len(P["qs"])
            for j, (q, r, n) in enumerate(P["qs"]):
                nk = n if kind == 1 else n - 1
                first = (kind == 1 and j == 0)
                S.op("pe", lambda e, O=O, pt=pt, q=q, b=b, p=p, sl=vslot(p, d, r, nk), first=first, lastmm=(P["last"] and j == nq - 1): e.matmul(
                    out=O[:, q * 128:(q + 1) * 128], lhsT=vd[b][p][:, sl, :], rhs=pt[:, q * 128:(q + 1) * 128],
                    start=first, stop=lastmm, skip_group_check=True), [t_e, cur["ofr"] if first else None])
            m5 = S.op("pe", lambda e, Dn=Dn, pt=pt, c0=c0, kind=kind, lastp=P["last"]: e.matmul(
                out=Dn[:, c0:512], lhsT=ones[:], rhs=pt[:, c0:512], start=(kind == 1), stop=lastp, skip_group_check=True),
                [t_e, cur["dfr"] if kind == 1 else None])
            pts.release(ip, m5)
            if P["last"]:
                if p == 0:
                    nv = nacc[ab][:, g * 512:(g + 1) * 512]
                    dv = dacc[ab][:, g * 512:(g + 1) * 512]
                    Ov, Dv = O[:], Dn[:]
                elif p == 1:
                    nv = nacc[ab][:, g * 512:(g + 1) * 512].rearrange("p (i r) -> p r i", r=4)
                    dv = dacc[ab][:, g * 512:(g + 1) * 512].rearrange("p (i r) -> p r i", r=4)
                    Ov = O[:, :].rearrange("p (r i) -> p r i", r=4)
                    Dv = Dn[:, :].rearrange("p (r i) -> p r i", r=4)
                else:
                    nv = nacc[ab][:, :].rearrange("p (i g r) -> p g r i", g=4, r=4)[:, g]
                    dv = dacc[ab][:, :].rearrange("p (i g r) -> p g r i", g=4, r=4)[:, g]
                    Ov = O[:, :].rearrange("p (r i) -> p r i", r=4)
                    Dv = Dn[:, :].rearrange("p (r i) -> p r i", r=4)
                if p == 0:
                    ta = S.op("act", lambda e, nv=nv, Ov=Ov: e.activation(out=nv, in_=Ov, func=AF.Copy), [m5, acc_free[ab]])
                    tb_ = S.op("dve", lambda e, dv=dv, Dv=Dv: e.tensor_copy(out=dv, in_=Dv), [m5, acc_free[ab]])
                else:
                    ta = S.op("dve", lambda e, nv=nv, Ov=Ov: e.tensor_tensor(out=nv, in0=Ov, in1=nv, op=ALU.add), [m5] + acc_toks)
                    tb_ = S.op("dve", lambda e, dv=dv, Dv=Dv: e.tensor_tensor(out=dv, in0=Dv, in1=dv, op=ALU.add), [m5, ta] + acc_toks)
                acc_toks = acc_toks + [ta, tb_]
                pO.release(cur["io"], ta)
                pD.release(cur["idn"], tb_)
                if p == 2 and g == 3:
                    t_r = S.op("dve", lambda e, ab=ab: e.reciprocal(out=dacc[ab][:], in_=dacc[ab][:]), acc_toks)
                    t_o = S.op("dve", lambda e, ab=ab: e.tensor_tensor(out=nacc[ab][:], in0=nacc[ab][:], in1=dacc[ab][:], op=ALU.mult), [t_r])
                    tc = stg.put("dve", lambda e, t, ab=ab: e.tensor_tensor(out=t[:], in0=nacc[ab][:], in1=gts[ab][:], op=ALU.mult),
                                 [t_o, t_gate], scr["mix0_T"][(HA + hb) * 128:(HA + hb + 1) * 128, st * 2048:(st + 1) * 2048])
                    acc_free[ab] = tc
                    g_free[ab] = tc
                    if st == NST - 1:
                        head_free[b] = m5
                        perm_free[0] = m5
                    if st == 0 and hb + 1 < HB:
                        head_tok[hb + 1] = load_head(hb + 1)
            if defer:
                issue_upto(pi + 1)
        cx.flush()


def phase_sb(nc, S, cfg, I, scr, extra_flush=()):
    SL, HS = cfg.SL, cfg.HS
    NQT, NB = SL // 512, SL // 128
    with Ctx(nc, S) as cx:
        ident, t_c0 = load_const(cx, I["ident"], [128, 128], BF16, "ident")
        ones, t_c1 = load_const(cx, I["ones_bf"], [128, 128], BF16, "ones")
        tri, t_c2 = load_const(cx, I["tri"], [128, 128], BF16, "tri")
        mSn, t_c3 = load_const(cx, I["maskSn"], [128, 4, 512], BF16, "mSn")
        mSp, t_c4 = load_const(cx, I["maskSp"], [128, 4, 512], BF16, "mSp")
        t_const = [t_c0, t_c1, t_c2, t_c3, t_c4]
        qT = [cx.sb([128, SL], BF16, "sq") for _ in range(2)]
        kT = [cx.sb([128, SL], BF16, "sk") for _ in range(2)]
        nkT = [cx.sb([128, SL], BF16, "snk") for _ in range(2)]
        vv = [cx.sb([128, NB, 128], BF16, "sv") for _ in range(2)]
        hs = [S.slot() for _ in range(2)]
        head_free = [None, None]
        pS = Rot([cx.ps([128, 512], F32, "pS") for _ in range(2)])
        pC = Rot([cx.ps([128, 512], F32, "pC") for _ in range(2)])
        pO = Rot([cx.ps([128, 512], F32, "pO") for _ in range(2)])
        Et = Rot([cx.sb([128, 512], F32, "Et") for _ in range(2)])
        Lt = Rot([cx.sb([128, 512], BF16, "Lt") for _ in range(4)])
        Ls = Rot([cx.sb([128, 512], BF16, "Ls") for _ in range(3)])
        At = Rot([cx.sb([128, 512], BF16, "At") for _ in range(3)])
        gts = Rot([cx.sb([128, 512], BF16, "sgt") for _ in range(2)])
        gsl = [S.slot() for _ in range(2)]
        stg = OutStage(cx, [128, 512], BF16, n=2, name="smix")

        def load_head(h):
            b = h % 2
            fr = head_free[b]
            S.dma("sp", lambda e: e.dma_start(out=qT[b][:], in_=scr["q1_T"][h * 128:(h + 1) * 128, :]), hs[b], [fr])
            S.dma("sp", lambda e: e.dma_start(out=kT[b][:], in_=scr["k1_T"][h * 128:(h + 1) * 128, :]), hs[b], [fr])
            tok = None
            for j in range(4):
                tok = S.dma("sp", lambda e, j=j: e.dma_start(
                    out=vv[b][:, j * 8:(j + 1) * 8, :],
                    in_=scr["v1"][h, j * 1024:(j + 1) * 1024, :].rearrange("(n p) f -> p n f", p=128)), hs[b], [fr])
            t_nk = S.op("pool", lambda e: e.tensor_scalar(out=nkT[b][:], in0=kT[b][:], scalar1=-1.0, scalar2=None, op0=ALU.mult),
                        [tok, fr])
            return [tok, t_nk]

        blocks = []
        for h in range(HS):
            for qt in range(NQT):
                for kb in range(4 * qt + 3, -1, -1):
                    blocks.append((h, qt, kb))
        NBK = len(blocks)
        head_tok = {0: load_head(0)}
        st1, st2, st3 = {}, {}, {}
        first_il = [None]
        first_users = [[]]
        ls_cur = {}
        ocur = {}

        def stage_S(bi):
            h, qt, kb = blocks[bi]
            b = h % 2
            i_s, Sb, sfr = pS.next()
            diag = kb >= 4 * qt
            m = S.op("pe", lambda e: e.matmul(out=Sb[:], lhsT=kT[b][:, kb * 128:(kb + 1) * 128], rhs=qT[b][:, qt * 512:(qt + 1) * 512],
                                             start=True, stop=(not diag)), [head_tok[h], sfr] + t_const)
            if diag:
                m = S.op("pe", lambda e: e.matmul(out=Sb[:], lhsT=ident[:], rhs=mSn[:, kb - 4 * qt, :], start=False, stop=True), [])
            st1[bi] = (i_s, Sb, m)

        def stage_EL(bi):
            h, qt, kb = blocks[bi]
            i_s, Sb, m = st1.pop(bi)
            ie, E, efr = Et.next()
            il, L, lfr = Lt.next()
            t_e = S.op("act", lambda e: e.activation(out=E[:], in_=Sb[:], func=AF.Exp), [m, efr])
            pS.release(i_s, t_e)
            t_l = S.op("act", lambda e: e.activation(out=L[:], in_=E[:], func=AF.Ln, bias=1.0, scale=1.0), [t_e])
            Et.release(ie, t_l)
            st2[bi] = (il, L, t_l)

        def stage_C(bi):
            h, qt, kb = blocks[bi]
            b = h % 2
            il, L, t_l = st2.pop(bi)
            first = kb == 4 * qt + 3
            last = kb == 0
            diag = kb >= 4 * qt
            i_c, Cb, cfr = pC.next()
            S.op("pe", lambda e: e.matmul(out=Cb[:], lhsT=tri[:], rhs=L[:], start=True, stop=False), [t_l, cfr])
            m = S.op("pe", lambda e: e.matmul(out=Cb[:], lhsT=nkT[b][:, kb * 128:(kb + 1) * 128], rhs=qT[b][:, qt * 512:(qt + 1) * 512],
                                             start=False, stop=(not diag and first)), [])
            if diag:
                m = S.op("pe", lambda e: e.matmul(out=Cb[:], lhsT=ident[:], rhs=mSp[:, kb - 4 * qt, :], start=False, stop=first), [])
            users = [m]
            if not first:
                ils, Lsum, t_ls = ls_cur[bi]
                m = S.op("pe", lambda e: e.matmul(out=Cb[:], lhsT=ones[:], rhs=Lsum[:], start=False, stop=True), [t_ls])
                users = [m]
            if not last:
                if first:
                    ls_cur[bi + 1] = (None, L, t_l)
                    users.append(("hold",))
                else:
                    iln, Lnew, lnfr = Ls.next()
                    t_n = S.op("dve", lambda e: e.tensor_tensor(out=Lnew[:], in0=Lsum[:], in1=L[:], op=ALU.add), [t_l, t_ls])
                    ls_cur[bi + 1] = (iln, Lnew, t_n)
                    users.append(t_n)
            if not first:
                if ils is not None:
                    Ls.release(ils, [u for u in users if u != ("hold",)])
                else:
                    Lt.release(first_il[0], [u for u in users if u != ("hold",)] + first_users[0])
                ls_cur.pop(bi)
            if ("hold",) in users:
                first_il[0] = il
                first_users[0] = [u for u in users if u != ("hold",)]
            else:
                Lt.release(il, list(users))
            st3[bi] = (i_c, Cb, m)

        def stage_A(bi):
            i_c, Cb, m = st3.pop(bi)
            ia, A, afr = At.next()
            t_a = S.op("act", lambda e: e.activation(out=A[:], in_=Cb[:], func=AF.Exp, scale=-1.0), [m, afr])
            pC.release(i_c, t_a)
            st3[("A", bi)] = (ia, A, t_a)

        def stage_PV(bi):
            h, qt, kb = blocks[bi]
            b = h % 2
            ia, A, t_a = st3.pop(("A", bi))
            first = kb == 4 * qt + 3
            last = kb == 0
            if first:
                io, O, ofr = pO.next()
                gi, gt, gfr = gts.next()
                t_gate = S.dma("sp", lambda e: e.dma_start(
                    out=gt[:], in_=scr["gate1_T"][h * 128:(h + 1) * 128, qt * 512:(qt + 1) * 512]), gsl[gi], [gfr])
                ocur.update(io=io, O=O, ofr=ofr, gi=gi, gt=gt, t_gate=t_gate)
            O = ocur["O"]
            m = S.op("pe", lambda e: e.matmul(out=O[:], lhsT=vv[b][:, kb, :], rhs=A[:], start=first, stop=last),
                     [t_a, ocur["ofr"] if first else None])
            At.release(ia, m)
            if last:
                gt = ocur["gt"]
                tc = stg.put("dve", lambda e, t: e.tensor_tensor(out=t[:], in0=O[:], in1=gt[:], op=ALU.mult),
                             [m, ocur["t_gate"]], scr["mix1_T"][h * 128:(h + 1) * 128, qt * 512:(qt + 1) * 512])
                pO.release(ocur["io"], tc)
                gts.release(ocur["gi"], tc)
                if qt == NQT - 1:
                    head_free[b] = m
                if qt == 0 and h + 1 < HS:
                    head_tok[h + 1] = load_head(h + 1)

        for step in range(-2, NBK):
            if 0 <= step + 2 < NBK:
                stage_S(step + 2)
                stage_EL(step + 2)
            if 0 <= step + 1 < NBK:
                stage_C(step + 1)
                stage_A(step + 1)
            if 0 <= step < NBK:
                stage_PV(step)
        cx.flush(extra_flush)


def phase_sb2(nc, S, cfg, I, scr, extra_flush=()):
    SL, HS = cfg.SL, cfg.HS
    NQT, NB = SL // 512, SL // 128
    with Ctx(nc, S) as cx:
        ident, t_c0 = load_const(cx, I["ident"], [128, 128], BF16, "ident")
        ones, t_c1 = load_const(cx, I["ones_bf"], [128, 128], BF16, "ones")
        tri, t_c2 = load_const(cx, I["tri"], [128, 128], BF16, "tri")
        mSn, t_c3 = load_const(cx, I["maskSn"], [128, 4, 512], BF16, "mSn")
        mSp, t_c4 = load_const(cx, I["maskSp"], [128, 4, 512], BF16, "mSp")
        t_const = [t_c0, t_c1, t_c2, t_c3, t_c4]
        qT = [cx.sb([128, SL], BF16, "sq") for _ in range(2)]
        kT = [cx.sb([128, SL], BF16, "sk") for _ in range(2)]
        nkT = [cx.sb([128, SL], BF16, "snk") for _ in range(2)]
        vv = [cx.sb([128, NB, 128], BF16, "sv") for _ in range(2)]
        hs = [S.slot() for _ in range(2)]
        head_free = [None, None]
        pS = Rot([cx.ps([128, 2, 512], F32, "pS") for _ in range(2)])
        pC = Rot([cx.ps([128, 2, 512], F32, "pC") for _ in range(1)])
        pO = Rot([cx.ps([128, 512], F32, "pO") for _ in range(2)])
        Et = Rot([cx.sb([128, 2, 512], F32, "Et") for _ in range(2)])
        Lt = Rot([cx.sb([128, 2, 512], BF16, "Lt") for _ in range(3)])
        Ls = Rot([cx.sb([128, 512], BF16, "Ls") for _ in range(3)])
        At = Rot([cx.sb([128, 2, 512], BF16, "At") for _ in range(2)])
        gts = Rot([cx.sb([128, 512], BF16, "sgt") for _ in range(2)])
        gsl = [S.slot() for _ in range(2)]
        stg = OutStage(cx, [128, 512], BF16, n=2, name="smix")

        def load_head(h):
            b = h % 2
            fr = head_free[b]
            S.dma("sp", lambda e: e.dma_start(out=qT[b][:], in_=scr["q1_T"][h * 128:(h + 1) * 128, :]), hs[b], [fr])
            S.dma("sp", lambda e: e.dma_start(out=kT[b][:], in_=scr["k1_T"][h * 128:(h + 1) * 128, :]), hs[b], [fr])
            tok = None
            for j in range(4):
                tok = S.dma("sp", lambda e, j=j: e.dma_start(
                    out=vv[b][:, j * 8:(j + 1) * 8, :],
                    in_=scr["v1"][h, j * 1024:(j + 1) * 1024, :].rearrange("(n p) f -> p n f", p=128)), hs[b], [fr])
            t_nk = S.op("pool", lambda e: e.tensor_scalar(out=nkT[b][:], in0=kT[b][:], scalar1=-1.0, scalar2=None, op0=ALU.mult),
                        [tok, fr])
            return [tok, t_nk]

        pairs = []
        for h in range(HS):
            for qt in range(NQT):
                np_ = 2 * qt + 2
                for j in range(np_):
                    pairs.append((h, qt, j, np_))
        NP = len(pairs)
        head_tok = {0: load_head(0)}
        st1, st2, st3, st4 = {}, {}, {}, {}
        ls_cur = {}
        ocur = {}

        def kbs(qt, j):
            return (4 * qt + 3 - 2 * j, 4 * qt + 2 - 2 * j)

        def stage_S(i):
            h, qt, j, np_ = pairs[i]
            b = h % 2
            i_s, Sp, sfr = pS.next()
            diag = j < 2
            m = None
            for t, kb in enumerate(kbs(qt, j)):
                m = S.op("pe", lambda e, t=t, kb=kb: e.matmul(
                    out=Sp[:, t, :], lhsT=kT[b][:, kb * 128:(kb + 1) * 128], rhs=qT[b][:, qt * 512:(qt + 1) * 512],
                    start=True, stop=(not diag)), [head_tok[h], sfr] + t_const)
                if diag:
                    m = S.op("pe", lambda e, t=t, kb=kb: e.matmul(
                        out=Sp[:, t, :], lhsT=ident[:], rhs=mSn[:, kb - 4 * qt, :], start=False, stop=True), [])
            st1[i] = (i_s, Sp, m)

        def stage_EL(i):
            i_s, Sp, m = st1.pop(i)
            ie, E, efr = Et.next()
            il, L, lfr = Lt.next()
            t_e = S.op("act", lambda e: e.activation(out=E[:], in_=Sp[:], func=AF.Exp), [m, efr])
            pS.release(i_s, t_e)
            t_l = S.op("act", lambda e: e.activation(out=L[:], in_=E[:], func=AF.Ln, bias=1.0, scale=1.0), [t_e])
            Et.release(ie, t_l)
            st2[i] = (il, L, t_l)

        def stage_C(i):
            h, qt, j, np_ = pairs[i]
            b = h % 2
            il, L, t_l = st2.pop(i)
            first, last, diag = j == 0, j == np_ - 1, j < 2
            i_c, Cp, cfr = pC.next()
            if not first:
                ils, Lsum, t_ls = ls_cur.pop(i)
            m = None
            for t, kb in enumerate(kbs(qt, j)):
                seq = [(tri[:], L[:, t, :], [t_l, cfr])]
                seq.append((nkT[b][:, kb * 128:(kb + 1) * 128], qT[b][:, qt * 512:(qt + 1) * 512], []))
                if diag:
                    seq.append((ident[:], mSp[:, kb - 4 * qt, :], []))
                if t == 1:
                    seq.append((ones[:], L[:, 0, :], []))
                if not first:
                    seq.append((ones[:], Lsum[:], [t_ls]))
                for k, (lh, rh, deps) in enumerate(seq):
                    m = S.op("pe", lambda e, t=t, lh_ap=lh, rh=rh, k=k, n=len(seq): e.matmul(
                        out=Cp[:, t, :], lhsT=lh_ap, rhs=rh, start=(k == 0), stop=(k == n - 1)), deps)
            users = [m]
            if not last:
                iln, Lnew, lnfr = Ls.next()
                if first:
                    t_n = S.op("dve", lambda e: e.tensor_tensor(out=Lnew[:], in0=L[:, 0, :], in1=L[:, 1, :], op=ALU.add), [t_l])
                else:
                    t_n0 = S.op("dve", lambda e: e.tensor_tensor(out=Lnew[:], in0=Lsum[:], in1=L[:, 0, :], op=ALU.add), [t_l, t_ls])
                    t_n = S.op("dve", lambda e: e.tensor_tensor(out=Lnew[:], in0=Lnew[:], in1=L[:, 1, :], op=ALU.add), [t_n0])
                ls_cur[i + 1] = (iln, Lnew, t_n)
                users.append(t_n)
            if not first:
                Ls.release(ils, list(users))
            Lt.release(il, list(users))
            st3[i] = (i_c, Cp, m)

        def stage_A(i):
            i_c, Cp, m = st3.pop(i)
            ia, A, afr = At.next()
            t_a = S.op("act", lambda e: e.activation(out=A[:], in_=Cp[:], func=AF.Exp, scale=-1.0), [m, afr])
            pC.release(i_c, t_a)
            st4[i] = (ia, A, t_a)

        def stage_PV(i):
            h, qt, j, np_ = pairs[i]
            b = h % 2
            ia, A, t_a = st4.pop(i)
            first, last = j == 0, j == np_ - 1
            if first:
                io, O, ofr = pO.next()
                gi, gt, gfr = gts.next()
                t_gate = S.dma("sp", lambda e: e.dma_start(
                    out=gt[:], in_=scr["gate1_T"][h * 128:(h + 1) * 128, qt * 512:(qt + 1) * 512]), gsl[gi], [gfr])
                ocur.update(io=io, O=O, ofr=ofr, gi=gi, gt=gt, t_gate=t_gate)
            O = ocur["O"]
            m = None
            for t, kb in enumerate(kbs(qt, j)):
                m = S.op("pe", lambda e, t=t, kb=kb: e.matmul(
                    out=O[:], lhsT=vv[b][:, kb, :], rhs=A[:, t, :], start=(first and t == 0), stop=(last and t == 1)),
                    [t_a, ocur["ofr"] if (first and t == 0) else None])
            At.release(ia, m)
            if last:
                gt = ocur["gt"]
                tc = stg.put("dve", lambda e, tl: e.tensor_tensor(out=tl[:], in0=O[:], in1=gt[:], op=ALU.mult),
                             [m, ocur["t_gate"]], scr["mix1_T"][h * 128:(h + 1) * 128, qt * 512:(qt + 1) * 512])
                pO.release(ocur["io"], tc)
                gts.release(ocur["gi"], tc)
                if qt == NQT - 1:
                    head_free[b] = m
                if qt == 0 and h + 1 < HS:
                    head_tok[h + 1] = load_head(h + 1)

        for step in range(-2, NP):
            if 0 <= step + 2 < NP:
                stage_S(step + 2)
                stage_EL(step + 2)
            if 0 <= step + 1 < NP:
                stage_C(step + 1)
                stage_A(step + 1)
            if 0 <= step < NP:
                stage_PV(step)
        cx.flush(extra_flush)


def phase_gather(nc, S, cfg, src, dst):
    groups = [[cfg.NR * i + j for j in range(cfg.NR)] for i in range(8 // cfg.NR)]
    sl = S.slot("pool")
    t = S.dma("pool", lambda e: e.collective_compute("AllGather", ALU.bypass, replica_groups=groups, ins=[src], outs=[dst]), sl)
    S.op("sp", lambda e: e.nop(), [t])
    S.emit()


def load_wo(S, wo, wo_dram):
    sw = S.slot("pool")
    t_w = None
    for cg in range(4):
        t_w = S.dma("pool", lambda e, cg=cg: e.dma_start(
            out=wo[:, :, cg * 512:(cg + 1) * 512],
            in_=wo_dram[:, cg * 512:(cg + 1) * 512].rearrange("(k p) n -> p k n", p=128)), sw)
    return t_w


def phase_out(nc, S, cfg, mixT_dram, wo_dram, xin_dram, xout_dram, final_gain=None, out_dram=None, wo_pre=None):
    SL, D = cfg.SL, cfg.D
    KC = D // 128
    with Ctx(nc, S) as cx:
        if wo_pre is not None:
            wo, t_w = wo_pre
        else:
            wo = cx.sb([128, KC, D], BF16, "wo")
            t_w = load_wo(S, wo, wo_dram)
        mts = Rot([cx.sb([128, KC, 512], BF16, "mt") for _ in range(2)])
        msl = [S.slot() for _ in range(2)]
        xts = Rot([cx.sb([128, D], F32, "xo") for _ in range(2)])
        xsl = [S.slot() for _ in range(2)]
        banks = [cx.ps([128, 512], F32, "po") for _ in range(8)]
        bfree = [None] * 8
        ystg = OutStage(cx, [128, D], F32, n=2, name="yst", dma_eng="sp")
        if final_gain is not None:
            gft, t_g = load_const(cx, final_gain.partition_broadcast(128), [128, D], F32, "gft")
            junk = cx.sb([128, D], BF16, "junk")
            ss = cx.sb([128, SL // 128], F32, "ss")
            rs = cx.sb([128, SL // 128], F32, "rs")
            ostg = OutStage(cx, [128, D], F32, n=2, name="ost", dma_eng="sp")
            ysb = Rot([cx.sb([128, D], F32, "ysb") for _ in range(2)])
        for tb in range(SL // 128):
            if tb % 4 == 0:
                im, mt, mfr = mts.next()
                t_m = S.dma("sp", lambda e, mt=mt, tb=tb: e.dma_start(
                    out=mt[:], in_=mixT_dram[:, tb * 128:(tb + 4) * 128].rearrange("(k p) t -> p k t", p=128)), msl[im], [mfr])
            tq = tb % 4
            ix, xt, xfr = xts.next()
            t_x = S.dma("sp", lambda e, xt=xt, tb=tb: e.dma_start(out=xt[:], in_=xin_dram[tb * 128:(tb + 1) * 128, :]), xsl[ix], [xfr])
            mms = []
            for cg in range(4):
                bi = (tb % 2) * 4 + cg
                bank = banks[bi]
                tm = None
                for kc in range(KC):
                    tm = S.op("pe", lambda e, bank=bank, mt=mt, kc=kc, cg=cg, tq=tq: e.matmul(
                        out=bank[:], lhsT=mt[:, kc, tq * 128:(tq + 1) * 128], rhs=wo[:, kc, cg * 512:(cg + 1) * 512],
                        start=(kc == 0), stop=(kc == KC - 1)), [t_m, t_w, bfree[bi]])
                mms.append(tm)
            if tq == 3:
                mts.release(im, mms[-1])
            if final_gain is None:
                comps = []
                for cg in range(4):
                    bank = banks[(tb % 2) * 4 + cg]
                    comps.append(("dve", lambda e, t, bank=bank, xt=xt, cg=cg: e.tensor_tensor(
                        out=t[:, cg * 512:(cg + 1) * 512], in0=bank[:], in1=xt[:, cg * 512:(cg + 1) * 512], op=ALU.add),
                        [mms[cg], t_x]))
                toks = ystg.put_multi(comps, xout_dram[tb * 128:(tb + 1) * 128, :])
                for cg in range(4):
                    bfree[(tb % 2) * 4 + cg] = toks[cg]
                xts.release(ix, toks[-1])
            else:
                iy, y, yfr = ysb.next()
                tl = yfr
                toks = []
                for cg in range(4):
                    bank = banks[(tb % 2) * 4 + cg]
                    tl = S.op("dve", lambda e, y=y, bank=bank, xt=xt, cg=cg: e.tensor_tensor(
                        out=y[:, cg * 512:(cg + 1) * 512], in0=bank[:], in1=xt[:, cg * 512:(cg + 1) * 512], op=ALU.add),
                        [mms[cg], t_x, tl])
                    bfree[(tb % 2) * 4 + cg] = tl
                    toks.append(tl)
                xts.release(ix, tl)
                t_ss = S.op("act", lambda e, y=y, tb=tb: e.activation(out=junk[:], in_=y[:], func=AF.Square,
                                                                      accum_out=ss[:, tb:tb + 1]), [tl])
                t_a = S.op("act", lambda e, tb=tb: e.activation(out=rs[:, tb:tb + 1], in_=ss[:, tb:tb + 1], func=AF.Sqrt,
                                                                bias=EPS, scale=1.0 / D), [t_ss])
                t_r = S.op("dve", lambda e, tb=tb: e.reciprocal(out=rs[:, tb:tb + 1], in_=rs[:, tb:tb + 1]), [t_a])
                tc = ostg.put("dve", lambda e, t, y=y, tb=tb: e.scalar_tensor_tensor(
                    out=t[:], in0=y[:], scalar=rs[:, tb:tb + 1], in1=gft[:], op0=ALU.mult, op1=ALU.mult),
                    [t_r, t_g], out_dram[tb * 128:(tb + 1) * 128, :])
                ysb.release(iy, tc)
        cx.flush()


def phase_proj1(nc, S, cfg, hT, w1, scr):
    SL, HS = cfg.SL, cfg.HS
    KC = cfg.D // 128
    NTT, NTB = SL // 512, SL // 128
    with Ctx(nc, S) as cx:
        pj = Proj(cx, cfg, hT, KC, None)
        ws = WStream(cx, KC)
        st16 = OutStage(cx, [128, 512], BF16, n=6, name="st16")
        groups = []
        c = 0
        for g in range(HS * 128 // 512):
            groups.append((c + g * 512, "T", [(scr["q1_T"], g * 512, 128.0 ** -0.5, None)]))
        c += HS * 128
        for g in range(HS * 128 // 512):
            groups.append((c + g * 512, "T", [(scr["k1_T"], g * 512, 1.0, None)]))
        c += HS * 128
        for g in range(HS * 128 // 512):
            groups.append((c + g * 512, "N", (scr["v1"], g * 4)))
        c += HS * 128
        for g in range(cfg.FM1 // 512):
            groups.append((c + g * 512, "T", [(scr["gate1_T"], g * 512, 1.0, AF.Silu)]))
        nxt = ws.load(w1[:, groups[0][0]:groups[0][0] + 512], 512)
        nev = 0
        for gi, (c0, kind, spec) in enumerate(groups):
            wi, wt, wtok = nxt
            if gi + 1 < len(groups):
                n0 = groups[gi + 1][0]
                nxt = ws.load(w1[:, n0:n0 + 512], 512)
            last = None
            if kind == "T":
                for cb in range(4):
                    for tt in range(NTT):
                        i, bank, tm = pj.mm_T(wt, cb * 128, 128, tt, wtok)
                        last = tm
                        tcs = []
                        for (dst, r0, scale, func) in spec:
                            eng = "act" if (func is not None or nev % 2 == 0) else "dve"
                            nev += 1
                            tcs.append(st16.put(eng, lambda e, t, eng=eng, bank=bank, scale=scale, func=func: evac(
                                eng, e, t[:], bank[:], scale, func), [tm],
                                dst[r0 + cb * 128:r0 + (cb + 1) * 128, tt * 512:(tt + 1) * 512]))
                        pj.banks.release(i, tcs)
            else:
                dst, h0 = spec
                for tb in range(NTB):
                    i, bank, tm = pj.mm_N(wt, 0, 512, tb, wtok)
                    last = tm
                    eng = "act" if nev % 2 == 0 else "dve"
                    nev += 1
                    tc = st16.put(eng, lambda e, t, eng=eng, bank=bank: evac(eng, e, t[:], bank[:]), [tm],
                                  dst[h0:h0 + 4, tb * 128:(tb + 1) * 128, :].rearrange("h p f -> p h f"),
                                  sub=lambda t: t[:].rearrange("p (h f) -> p h f", h=4))
                    pj.banks.release(i, tc)
            ws.release(wi, last)
        cx.flush()


def const_arrays(cfg):
    SL = cfg.SL
    bf = ml_dtypes.bfloat16
    c = {}
    c["ident"] = np.eye(128, dtype=np.float32).astype(bf)
    c["ones_bf"] = np.ones((128, 128), np.float32).astype(bf)
    c["ones32"] = np.ones((128, 128), np.float32)
    j = np.arange(128)[:, None]
    s = np.arange(128)[None, :]
    c["tri"] = (j >= s).astype(np.float32).astype(bf)
    t = np.arange(512)[None, None, :]
    i = np.arange(4)[None, :, None]
    jj = np.arange(128)[:, None, None]
    c["maskA"] = np.where(128 * i + jj <= t, 0.0, NEG).astype(np.float32).astype(bf)
    mneg = np.where(128 * i + jj < t, 0.0, NEG).astype(np.float32)
    c["maskSn"] = mneg.astype(bf)
    c["maskSp"] = (-mneg).astype(bf)
    qi = np.arange(128)[None, :]
    kj = np.arange(128)[:, None]
    mprev = np.where(kj >= qi, 0.0, NEG)
    mcur = np.where(kj <= qi, 0.0, NEG)
    dm = np.zeros((128, 6, 128), np.float32)
    for p in range(3):
        dm[:, 2 * p, :] = mprev
        dm[:, 2 * p + 1, :] = mcur
    c["dmask"] = dm
    half = 32
    inv = 1.0 / (10000.0 ** (np.arange(half, dtype=np.float32) / half))
    ang = np.arange(SL, dtype=np.float32)[None, :] * inv[:, None]
    ang = ang.astype(np.float32)
    cs = np.zeros((2, 64, SL), np.float32)
    cs[0, :32] = np.cos(ang)
    cs[0, 32:] = np.cos(ang)
    cs[1, :32] = np.sin(ang)
    cs[1, 32:] = np.sin(ang)
    c["cs"] = cs
    return c


def t5_bucket_np(dist):
    max_exact = 16
    d = np.maximum(dist.astype(np.float32), 1.0)
    large = max_exact + (np.log(d / max_exact) / math.log(2048 / max_exact) * (32 - max_exact)).astype(np.int32)
    large = np.minimum(large, 31)
    return np.where(dist < max_exact, dist, large)


def dil_bias_index():
    qi = np.arange(128)[None, :]
    kj = np.arange(128)[:, None]
    idx = np.zeros((6, 128, 128), np.int64)
    for p, d in enumerate((1, 4, 16)):
        rel_prev = np.maximum(128 + qi - kj, 0)
        rel_cur = np.maximum(qi - kj, 0)
        idx[2 * p] = t5_bucket_np((rel_prev * d).astype(np.int32))
        idx[2 * p + 1] = t5_bucket_np((rel_cur * d).astype(np.int32))
    return idx


def build_program(cfg, phases, debug=()):
    nc = bass.Bass("TRN2", target_bir_lowering=False)
    SL, D, HA, HB, HS = cfg.SL, cfg.D, cfg.HA, cfg.HB, cfg.HS

    def din(name, shape, dt=F32):
        return nc.dram_tensor(name, list(shape), dt, kind="ExternalInput").ap()

    def dscr(name, shape, dt):
        kind = "ExternalOutput" if name in debug else "Internal"
        return nc.dram_tensor(name, list(shape), dt, kind=kind).ap()

    I = {}
    I["x"] = din("x", [SL, D])
    I["g0"] = din("g0", [1, D])
    I["g1"] = din("g1", [1, D])
    I["gf"] = din("gf", [1, D])
    I["w0"] = din("w0", [D, cfg.W0C])
    I["qg"] = din("qg", [128, 4])
    I["kvg"] = din("kvg", [128, 4])
    I["wuq"] = din("wuq", [512, HA * 192])
    I["wukv"] = din("wukv", [512, HA * 256])
    I["wo0"] = din("wo0", [D, D])
    I["dbias"] = din("dbias", [HB, 128, 6, 128])
    I["w1"] = din("w1", [D, cfg.W1C])
    I["wo1"] = din("wo1", [D, D])
    I["ident"] = din("ident", [128, 128], BF16)
    I["ones_bf"] = din("ones_bf", [128, 128], BF16)
    I["ones32"] = din("ones32", [128, 128], F32)
    I["tri"] = din("tri", [128, 128], BF16)
    I["maskA"] = din("maskA", [128, 4, 512], BF16)
    I["maskSn"] = din("maskSn", [128, 4, 512], BF16)
    I["maskSp"] = din("maskSp", [128, 4, 512], BF16)
    I["dmask"] = din("dmask", [128, 6, 128], F32)
    I["cs"] = din("cs", [2, 64, SL], F32)
    out = nc.dram_tensor("out", [SL, D], F32, kind="ExternalOutput").ap()

    scr = {}
    scr["cq_T"] = dscr("cq_T", [512, SL], F32)
    scr["ckv_T"] = dscr("ckv_T", [512, SL], F32)
    rows0 = 64 + 3 * HB * 128 + cfg.FM0 + HA * 192 + 2 * HA * 128 + cfg.FM0
    rows1 = 3 * HS * 128 + 2 * cfg.FM1
    assert cfg.FM0 == cfg.FM1
    arena = dscr("arena16", [max(rows0, rows1), SL], BF16)
    pos = [0]

    def carve(nrows):
        v = arena[pos[0]:pos[0] + nrows, :]
        pos[0] += nrows
        return v

    def tokmajor(v, h):
        return v.rearrange("(h a) (b f) -> h (a b) f", h=h, f=128)

    scr["kr_T"] = carve(64)
    scr["qb_T"] = carve(HB * 128)
    scr["kb_T"] = carve(HB * 128)
    scr["vb"] = tokmajor(carve(HB * 128), HB)
    scr["gate_T"] = carve(cfg.FM0)
    scr["qa_T"] = carve(HA * 192).rearrange("(h r) s -> h r s", h=HA)
    scr["ka_T"] = carve(HA * 128).rearrange("(h r) s -> h r s", h=HA)
    scr["va"] = tokmajor(carve(HA * 128), HA)
    if cfg.NR == 1:
        scr["mix0_T"] = carve(cfg.FM0)
        scr["mixg0_T"] = scr["mix0_T"]
    else:
        cc_in = nc.dram_tensor("cc_in", [cfg.FM0, SL], BF16).ap()
        cc_out = nc.dram_tensor("cc_out", [cfg.NR * cfg.FM0, SL], BF16).ap()
        scr["mix0_T"] = cc_in
        scr["mixg0_T"] = cc_out
    pos[0] = 0
    scr["q1_T"] = carve(HS * 128)
    scr["k1_T"] = carve(HS * 128)
    scr["v1"] = tokmajor(carve(HS * 128), HS)
    scr["gate1_T"] = carve(cfg.FM1)
    if cfg.NR == 1:
        scr["mix1_T"] = carve(cfg.FM1)
        scr["mixg1_T"] = scr["mix1_T"]
    else:
        scr["mix1_T"] = cc_in
        scr["mixg1_T"] = cc_out
    scr["x1"] = out

    with contextlib.ExitStack() as es:
        S = Sched(nc, es)
        gcx = Ctx(nc, S)
        es.enter_context(gcx)
        ident = gcx.sb([128, 128], BF16, "ident")
        sl = S.slot()
        t_id = S.dma("sp", lambda e: e.dma_start(out=ident[:], in_=I["ident"]), sl)
        S.op("sp", lambda e: e.nop(), [t_id])
        S.emit()

        if "n0" in phases:
            hcx = Ctx(nc, S)
            hcx.__enter__()
            hT = hcx.sb([128, D // 128, SL], BF16, "hT")
            phase_norm(nc, S, cfg, I["x"], I["g0"], hT, ident)
            if "p0" in phases:
                phase_proj0(nc, S, cfg, hT, None, I["w0"], I["cs"], scr)
            hcx.__exit__(None, None, None)
        for ph in phases:
            if ph.startswith("dummy"):
                S.op("sp", lambda e: e.nop(), [])
                S.emit()
        if "u0" in phases:
            phase_up(nc, S, cfg, I["wuq"], I["wukv"], I["qg"], I["kvg"], I["cs"], I["ones32"], scr)
        if "a0" in phases:
            phase_mla(nc, S, cfg, I, scr)
        if "b0" in phases:
            phase_dil(nc, S, cfg, I, scr)
        if "o0" in phases:
            if cfg.NR > 1:
                phase_gather(nc, S, cfg, scr["mix0_T"], scr["mixg0_T"])
            phase_out(nc, S, cfg, scr["mixg0_T"], I["wo0"], I["x"], scr["x1"])
        if "n1" in phases:
            hcx = Ctx(nc, S)
            hcx.__enter__()
            hT = hcx.sb([128, D // 128, SL], BF16, "hT1")
            phase_norm(nc, S, cfg, I["x"] if "n1x" in phases else scr["x1"], I["g1"], hT, ident)
            if "p1" in phases:
                phase_proj1(nc, S, cfg, hT, I["w1"], scr)
            hcx.__exit__(None, None, None)
        wo_pre = None
        wcx = None
        if "s1" in phases and "o1" in phases:
            wcx = Ctx(nc, S)
            wcx.__enter__()
            wo1 = wcx.sb([128, D // 128, D], BF16, "wo1")
            wo_pre = (wo1, load_wo(S, wo1, I["wo1"]))
        if "s1" in phases:
            (phase_sb2 if SB_PAIRED else phase_sb)(nc, S, cfg, I, scr, extra_flush=[wo_pre[1]] if wo_pre else ())
        if "o1" in phases:
            if cfg.NR > 1:
                phase_gather(nc, S, cfg, scr["mix1_T"], scr["mixg1_T"])
            phase_out(nc, S, cfg, scr["mixg1_T"], I["wo1"], I["x"] if "n1x" in phases else scr["x1"], None,
                      final_gain=I["gf"], out_dram=out, wo_pre=wo_pre)
        if wcx is not None:
            wcx.__exit__(None, None, None)
    return nc


def make_in_maps(cfg, inp, ncores=8):
    HA, HB, HS, NR = cfg.HA, cfg.HB, cfg.HS, cfg.NR
    consts = const_arrays(cfg)
    bidx = dil_bias_index()
    f32 = np.float32
    x = np.asarray(inp["x"], f32)
    wie = np.asarray(inp["w_in_even"], f32)[0]
    wio = np.asarray(inp["w_in_odd"], f32)[0]
    wuq = np.asarray(inp["w_uq"], f32)[0]
    wukv = np.asarray(inp["w_ukv"], f32)[0]
    woe = np.asarray(inp["w_out_even"], f32)[0]
    woo = np.asarray(inp["w_out_odd"], f32)[0]
    rb = np.asarray(inp["rel_bias"], f32)
    ng = np.asarray(inp["norm_gain"], f32)
    rows0 = []
    for r in range(NR):
        rows0.extend(range(r * HA * 128, (r + 1) * HA * 128))
        rows0.extend(range(1024 + r * HB * 128, 1024 + (r + 1) * HB * 128))
    rows0 = np.array(rows0)
    maps = []
    for c in range(ncores):
        b = (c // NR) % x.shape[0]
        p = c % NR
        cols = list(range(0, 1088))
        for base in (1088, 2112, 3136):
            cols.extend(range(base + p * HB * 128, base + (p + 1) * HB * 128))
        cols.extend(range(4160 + p * HA * 128, 4160 + (p + 1) * HA * 128))
        cols.extend(range(4160 + 1024 + p * HB * 128, 4160 + 1024 + (p + 1) * HB * 128))
        cols1 = []
        for base in (0, 2048, 4096, 6144):
            cols1.extend(range(base + p * HS * 128, base + (p + 1) * HS * 128))
        heads_b = np.arange(p * HB, (p + 1) * HB)
        db = rb[bidx][:, :, :, heads_b]
        db = np.ascontiguousarray(db.transpose(3, 1, 0, 2))
        m = {
            "x": np.ascontiguousarray(x[b]),
            "g0": np.ascontiguousarray(ng[0][None, :]),
            "g1": np.ascontiguousarray(ng[1][None, :]),
            "gf": np.ascontiguousarray(np.asarray(inp["final_norm_gain"], f32)[None, :]),
            "w0": np.ascontiguousarray(wie[:, cols]),
            "qg": np.ascontiguousarray(np.asarray(inp["q_norm_gain"], f32)[0].reshape(4, 128).T),
            "kvg": np.ascontiguousarray(np.asarray(inp["kv_norm_gain"], f32)[0].reshape(4, 128).T),
            "wuq": np.ascontiguousarray(wuq[:, p * HA * 192:(p + 1) * HA * 192]),
            "wukv": np.ascontiguousarray(wukv[:, p * HA * 256:(p + 1) * HA * 256]),
            "wo0": np.ascontiguousarray(woe[rows0, :]),
            "dbias": db,
            "w1": np.ascontiguousarray(wio[:, cols1]),
            "wo1": np.ascontiguousarray(woo),
        }
        m.update(consts)
        maps.append(m)
    return maps


ALL_PHASES = ("n0", "p0", "u0", "a0", "b0", "o0", "n1", "p1", "s1", "o1")
PH_A = ("n0", "p0", "u0", "a0", "b0", "o0")
PH_B = ("n1x", "n1", "p1", "s1", "o1")


def kernel(**inputs):
    cfg = Cfg(NR=1)
    nb = np.asarray(inputs["x"]).shape[0]
    ncores = nb * cfg.NR
    nc = build_program(cfg, ALL_PHASES)
    maps = make_in_maps(cfg, inputs, ncores=ncores)
    res = run_bass_kernel_spmd(nc, maps, core_ids=list(range(ncores)))
    outs = [np.asarray(res.results[c]["out"], dtype=np.float32) for c in range(0, ncores, cfg.NR)]
    return np.stack(outs, 0)
```

```python
import contextlib
import math
import numpy as np
import ml_dtypes
import concourse.bass as bass
import concourse.mybir as mybir
from concourse.bass_utils import run_bass_kernel_spmd

F32 = mybir.dt.float32
BF16 = mybir.dt.bfloat16
AF = mybir.ActivationFunctionType
ALU = mybir.AluOpType

ENGS = ["pe", "act", "dve", "pool", "sp"]
NEG = -30000.0
EMBED_WAIT = True
SB_PAIRED = True
EPS = 1e-6


class Slot:
    def __init__(self, sem):
        self.sem = sem
        self.count = 0


class Sched:
    def __init__(self, nc, es, same_engine_sync=("act", "dve", "pool")):
        self.nc = nc
        self.es = es
        self.same = set(same_engine_sync)
        self.ops = {e: [] for e in ENGS}
        self.phase_id = 0
        self._begin_phase()

    def _begin_phase(self):
        self.phase_id += 1
        if not hasattr(self, "pool"):
            self.pool = {"eng": [], "sp": [], "pool": []}
            self.semval = {}
        self.pool_pos = {k: 0 for k in self.pool}
        self.sem = {}
        self.sem_key = {}
        self.cnt = {}
        for e in ENGS:
            self.sem[e], self.sem_key[e] = self._alloc("eng")
            self.cnt[e] = self.semval[self.sem_key[e]]
        self.waited = {e: {} for e in ENGS}
        self.slots = []

    def _alloc(self, kind):
        pool = self.pool[kind]
        i = self.pool_pos[kind]
        if i >= len(pool):
            pool.append(self.es.enter_context(self.nc.semaphore("sem_%s_%d" % (kind, len(pool)))))
            self.semval[(kind, i)] = 0
        self.pool_pos[kind] += 1
        return pool[i], (kind, i)

    def slot(self, kind="sp"):
        h, key = self._alloc(kind)
        sl = Slot(h)
        sl.count = self.semval[key]
        sl.key = key
        sl.kind = kind
        self.slots.append(sl)
        return sl

    def op(self, eng, fn, deps=()):
        self.cnt[eng] += 1
        tok = ("e", eng, self.cnt[eng])
        self.ops[eng].append((fn, self._flat(deps), tok))
        return tok

    def dma(self, eng, fn, slot, deps=()):
        assert slot.kind == eng, (slot.kind, eng)
        slot.count += 16
        tok = ("d", slot, slot.count)
        self.ops[eng].append((fn, self._flat(deps), tok))
        return tok

    def _flat(self, deps):
        out = []
        for d in deps:
            if d is None:
                continue
            if isinstance(d, list):
                out.extend(self._flat(d))
            else:
                out.append(d)
        return out

    def _emit_engine(self, eng, e):
        waited = self.waited[eng]
        for fn, deps, tok in self.ops[eng]:
            need = {}
            for d in deps:
                if d[0] == "e":
                    if d[1] == eng and (eng not in self.same or len(d) > 3):
                        continue
                    key = ("e", d[1])
                    sem = self.sem[d[1]]
                else:
                    key = ("d", id(d[1]))
                    sem = d[1].sem
                if waited.get(key, 0) >= d[2]:
                    continue
                if key not in need or need[key][1] < d[2]:
                    need[key] = (sem, d[2])
            items = list(need.items())
            for key, (sem, val) in items:
                waited[key] = val
            for key, (sem, val) in items[:-1]:
                e.wait_ge(sem, val)
            inst = fn(e)
            if items:
                sem, val = items[-1][1]
                if EMBED_WAIT:
                    inst._wait_ge(sem, val)
                else:
                    raise RuntimeError("standalone wait must precede instruction")
            if tok[0] == "e":
                inst.then_inc(self.sem[eng], 1)
            else:
                inst.then_inc(tok[1].sem, 16)
        self.ops[eng] = []

    def emit(self):
        nc = self.nc
        with nc.Block() as block:
            @block.tensor
            def _(e):
                self._emit_engine("pe", e)

            @block.scalar
            def _(e):
                self._emit_engine("act", e)

            @block.vector
            def _(e):
                self._emit_engine("dve", e)

            @block.gpsimd
            def _(e):
                self._emit_engine("pool", e)

            @block.sync
            def _(e):
                self._emit_engine("sp", e)
        for e in ENGS:
            self.semval[self.sem_key[e]] = self.cnt[e]
        for sl in self.slots:
            self.semval[sl.key] = sl.count
        self._begin_phase()


def war(tok):
    if tok is None:
        return None
    if isinstance(tok, list):
        return [war(t) for t in tok]
    if tok[0] == "e" and len(tok) == 3:
        return tok + ("war",)
    return tok


class Rot:
    def __init__(self, bufs):
        self.bufs = bufs
        self.free = [None] * len(bufs)
        self.k = 0

    def next(self):
        i = self.k % len(self.bufs)
        self.k += 1
        return i, self.bufs[i], war(self.free[i])

    def release(self, i, tok):
        self.free[i] = tok


class Ctx:
    uid = 0

    def __init__(self, nc, S):
        self.nc = nc
        self.S = S
        self.es = contextlib.ExitStack()
        self.n = 0
        self.stages = []

    def __enter__(self):
        self.es.__enter__()
        return self

    def __exit__(self, *a):
        return self.es.__exit__(*a)

    def sb(self, shape, dt, name=None):
        Ctx.uid += 1
        return self.es.enter_context(self.nc.sbuf_tensor("%s_%d" % (name or "t", Ctx.uid), shape, dt))

    def ps(self, shape, dt, name=None):
        Ctx.uid += 1
        return self.es.enter_context(self.nc.psum_tensor("%s_%d" % (name or "p", Ctx.uid), shape, dt))

    def flush(self, extra=()):
        toks = list(extra)
        for st in self.stages:
            toks.extend([t for t in st.last if t is not None])
        self.S.op("sp", lambda e: e.nop(), toks)
        self.S.emit()


class OutStage:
    def __init__(self, cx, shape, dt, n=3, dma_eng="pool", name="stg"):
        self.S = cx.S
        self.tiles = [cx.sb(shape, dt, name) for _ in range(n)]
        self.slots = [cx.S.slot(dma_eng) for _ in range(n)]
        self.last = [None] * n
        self.k = 0
        self.dma_eng = dma_eng
        cx.stages.append(self)

    def put(self, eng, compute, deps, dram_ap, sub=None):
        i = self.k % len(self.tiles)
        self.k += 1
        t = self.tiles[i]
        tc = self.S.op(eng, lambda e: compute(e, t), list(deps) + [self.last[i]])
        src = t[:] if sub is None else sub(t)
        self.last[i] = self.S.dma(self.dma_eng, lambda e: e.dma_start(out=dram_ap, in_=src), self.slots[i], [tc])
        return tc

    def put_multi(self, computes, dram_ap, sub=None):
        i = self.k % len(self.tiles)
        self.k += 1
        t = self.tiles[i]
        toks = []
        prev = self.last[i]
        for eng, fn, deps in computes:
            prev = self.S.op(eng, lambda e, fn=fn: fn(e, t), list(deps) + [prev])
            toks.append(prev)
        src = t[:] if sub is None else sub(t)
        self.last[i] = self.S.dma(self.dma_eng, lambda e: e.dma_start(out=dram_ap, in_=src), self.slots[i], [prev])
        return toks


def evac(eng, e, out, in_, scale=1.0, func=None):
    if eng == "act":
        return e.activation(out=out, in_=in_, func=(func or AF.Copy), scale=scale)
    assert func is None
    if scale == 1.0:
        return e.tensor_copy(out=out, in_=in_)
    return e.tensor_scalar(out=out, in0=in_, scalar1=float(scale), scalar2=None, op0=ALU.mult)


class Cfg:
    SL = 4096
    D = 2048

    def __init__(self, NR=1):
        self.NR = NR
        self.HA = 8 // NR
        self.HB = 8 // NR
        self.HS = 16 // NR

    @property
    def FM0(self):
        return (self.HA + self.HB) * 128

    @property
    def FM1(self):
        return self.HS * 128

    @property
    def W0C(self):
        return 1088 + 3 * self.HB * 128 + self.FM0

    @property
    def W1C(self):
        return 4 * self.HS * 128


def phase_norm(nc, S, cfg, x_dram, g_dram, hT, ident):
    SL, D = cfg.SL, cfg.D
    KC = D // 128
    with Ctx(nc, S) as cx:
        xt = [cx.sb([128, D], F32, "xt") for _ in range(2)]
        hb = [cx.sb([128, D], BF16, "hb") for _ in range(2)]
        junk = cx.sb([128, D], BF16, "junk")
        gt = cx.sb([128, D], F32, "gt")
        ss = cx.sb([128, SL // 128], F32, "ss")
        rs = cx.sb([128, SL // 128], F32, "rs")
        pT = Rot([cx.ps([128, 4, 128], BF16, "pT") for _ in range(4)])
        sx = [S.slot() for _ in range(2)]
        sg = S.slot()
        t_g = S.dma("sp", lambda e: e.dma_start(out=gt[:], in_=g_dram.partition_broadcast(128)), sg)
        x_free = [None, None]
        h_free = [None, None]
        nev = 0
        for tb in range(SL // 128):
            b = tb % 2
            t_x = S.dma("sp", lambda e, b=b, tb=tb: e.dma_start(out=xt[b][:], in_=x_dram[tb * 128:(tb + 1) * 128, :]),
                        sx[b], [x_free[b]])
            t_ss = S.op("act", lambda e, b=b, tb=tb: e.activation(out=junk[:], in_=xt[b][:], func=AF.Square,
                                                                  accum_out=ss[:, tb:tb + 1]), [t_x])
            t_a = S.op("act", lambda e, tb=tb: e.activation(out=rs[:, tb:tb + 1], in_=ss[:, tb:tb + 1], func=AF.Sqrt,
                                                            bias=EPS, scale=1.0 / D), [t_ss])
            t_r = S.op("dve", lambda e, tb=tb: e.reciprocal(out=rs[:, tb:tb + 1], in_=rs[:, tb:tb + 1]), [t_a])
            t_h = S.op("dve", lambda e, b=b, tb=tb: e.scalar_tensor_tensor(
                out=hb[b][:], in0=xt[b][:], scalar=rs[:, tb:tb + 1], in1=gt[:], op0=ALU.mult, op1=ALU.mult),
                [t_r, t_g, h_free[b]])
            x_free[b] = t_h
            tp = None
            for grp in range(KC // 4):
                i, pt, fr = pT.next()
                for j in range(4):
                    kc = grp * 4 + j
                    tp = S.op("pe", lambda e, pt=pt, j=j, kc=kc, b=b: e.transpose(
                        out=pt[:, j, :], in_=hb[b][:, kc * 128:(kc + 1) * 128], identity=ident[:]), [t_h, fr])
                eng = "act" if nev % 2 == 0 else "dve"
                nev += 1
                t_e = S.op(eng, lambda e, eng=eng, pt=pt, grp=grp, tb=tb: evac(
                    eng, e, hT[:, grp * 4:(grp + 1) * 4, tb * 128:(tb + 1) * 128], pt[:]), [tp])
                pT.release(i, t_e)
            h_free[b] = tp
        S.emit()


class Proj:
    def __init__(self, cx, cfg, srcT, KC, src_tok, nbanks=4, banks=None):
        self.cx, self.S, self.cfg = cx, cx.S, cfg
        self.srcT, self.KC, self.src_tok = srcT, KC, src_tok
        self.banks = banks or Rot([cx.ps([128, 512], F32, "pp") for _ in range(nbanks)])

    def mm_T(self, w, c0, ncol, tt, wtok, M=None):
        S = self.S
        i, bank, fr = self.banks.next()
        tm = None
        for kc in range(self.KC):
            tm = S.op("pe", lambda e, kc=kc, bank=bank: e.matmul(
                out=bank[0:ncol, :], lhsT=w[:, kc, c0:c0 + ncol], rhs=self.srcT[:, kc, tt * 512:(tt + 1) * 512],
                start=(kc == 0), stop=(kc == self.KC - 1)), [fr, wtok, self.src_tok])
        return i, bank, tm

    def mm_N(self, w, c0, ncol, tb, wtok):
        S = self.S
        i, bank, fr = self.banks.next()
        tm = None
        for kc in range(self.KC):
            tm = S.op("pe", lambda e, kc=kc, bank=bank: e.matmul(
                out=bank[:, 0:ncol], lhsT=self.srcT[:, kc, tb * 128:(tb + 1) * 128], rhs=w[:, kc, c0:c0 + ncol],
                start=(kc == 0), stop=(kc == self.KC - 1)), [fr, wtok, self.src_tok])
        return i, bank, tm


class WStream:
    def __init__(self, cx, KC, width=512, n=2):
        self.cx, self.S, self.KC = cx, cx.S, KC
        self.tiles = [cx.sb([128, KC, width], BF16, "wt") for _ in range(n)]
        self.slots = [cx.S.slot("pool") for _ in range(n)]
        self.free = [None] * n
        self.k = 0

    def load(self, w_ap, ncol):
        i = self.k % len(self.tiles)
        self.k += 1
        t = self.tiles[i]
        src = w_ap.rearrange("(k p) n -> p k n", p=128)
        tok = self.S.dma("pool", lambda e: e.dma_start(out=t[:, :, 0:ncol], in_=src), self.slots[i], [self.free[i]])
        return i, t, tok

    def release(self, i, tok):
        self.free[i] = tok


def phase_proj0(nc, S, cfg, hT, h_tok, w0, cs_dram, scr):
    SL, HA, HB = cfg.SL, cfg.HA, cfg.HB
    KC = cfg.D // 128
    NTT, NTB = SL // 512, SL // 128
    with Ctx(nc, S) as cx:
        pj = Proj(cx, cfg, hT, KC, h_tok)
        ws = WStream(cx, KC)
        st32 = OutStage(cx, [128, 512], F32, n=2, name="st32")
        st16 = OutStage(cx, [128, 512], BF16, n=4, name="st16")
        groups = []
        groups.append((0, 512, "T", ("f32", scr["cq_T"], 0, 1.0, None)))
        groups.append((512, 512, "T", ("f32", scr["ckv_T"], 0, 1.0, None)))
        groups.append((1024, 64, "R", None))
        c = 1088
        for g in range(HB * 128 // 512):
            groups.append((c + g * 512, 512, "T", ("bf", scr["qb_T"], g * 512, 128.0 ** -0.5, None)))
        c += HB * 128
        for g in range(HB * 128 // 512):
            groups.append((c + g * 512, 512, "T", ("bf", scr["kb_T"], g * 512, 1.0, None)))
        c += HB * 128
        for g in range(HB * 128 // 512):
            groups.append((c + g * 512, 512, "N", (scr["vb"], g * 4)))
        c += HB * 128
        for g in range(cfg.FM0 // 512):
            groups.append((c + g * 512, 512, "T", ("bf", scr["gate_T"], g * 512, 1.0, AF.Silu)))
        nxt = ws.load(w0[:, groups[0][0]:groups[0][0] + groups[0][1]], groups[0][1])
        nev = 0
        for gi, (c0, ncol, kind, spec) in enumerate(groups):
            wi, wt, wtok = nxt
            if gi + 1 < len(groups):
                n0, nn = groups[gi + 1][0], groups[gi + 1][1]
                nxt = ws.load(w0[:, n0:n0 + nn], nn)
            last = None
            if kind == "T":
                typ, dst, r0, scale, func = spec
                for cb in range(ncol // 128):
                    for tt in range(NTT):
                        i, bank, tm = pj.mm_T(wt, cb * 128, 128, tt, wtok)
                        last = tm
                        eng = "act" if (func is not None or nev % 2 == 0) else "dve"
                        nev += 1
                        stg = st32 if typ == "f32" else st16
                        tc = stg.put(eng, lambda e, t, eng=eng, bank=bank, scale=scale, func=func: evac(
                            eng, e, t[:], bank[:], scale, func), [tm],
                            dst[r0 + cb * 128:r0 + (cb + 1) * 128, tt * 512:(tt + 1) * 512])
                        pj.banks.release(i, tc)
            elif kind == "N":
                dst, h0 = spec
                for tb in range(NTB):
                    i, bank, tm = pj.mm_N(wt, 0, ncol, tb, wtok)
                    last = tm
                    eng = "act" if nev % 2 == 0 else "dve"
                    nev += 1
                    tc = st16.put(eng, lambda e, t, eng=eng, bank=bank: evac(eng, e, t[:], bank[:]), [tm],
                                  dst[h0:h0 + 4, tb * 128:(tb + 1) * 128, :].rearrange("h p f -> p h f"),
                                  sub=lambda t: t[:].rearrange("p (h f) -> p h f", h=4))
                    pj.banks.release(i, tc)
            else:
                last = Rope(cx, pj, cfg, cs_dram, st16).run(wt, 0, wtok, 1.0, scr["kr_T"])
            ws.release(wi, last)
        cx.flush()


def latent_norm(cx, cfg, c_dram, gain_dram, cnT, ones32, pbanks, ones_tok):
    S = cx.S
    SL = cfg.SL
    cT = cx.sb([128, 4, SL], F32, "cT")
    gn = cx.sb([128, 4], F32, "gn")
    sq = [cx.sb([128, 4, 512], F32, "sq") for _ in range(2)]
    rt = [cx.sb([128, 512], F32, "rt") for _ in range(2)]
    sl = S.slot()
    sg = S.slot()
    t_g = S.dma("sp", lambda e: e.dma_start(out=gn[:], in_=gain_dram), sg)
    t_c = None
    for k in range(4):
        t_c = S.dma("sp", lambda e, k=k: e.dma_start(out=cT[:, k, :], in_=c_dram[k * 128:(k + 1) * 128, :]), sl)
    sq_free = [None, None]
    rt_free = [None, None]
    last = None
    for tt in range(SL // 512):
        b = tt % 2
        t_s = S.op("act", lambda e, b=b, tt=tt: e.activation(out=sq[b][:], in_=cT[:, :, tt * 512:(tt + 1) * 512], func=AF.Square),
                   [t_c, sq_free[b]])
        i, bank, fr = pbanks.next()
        tm = None
        for k in range(4):
            tm = S.op("pe", lambda e, k=k, b=b, bank=bank: e.matmul(out=bank[:], lhsT=ones32[:], rhs=sq[b][:, k, :],
                                                                   start=(k == 0), stop=(k == 3)), [t_s, fr, ones_tok])
        sq_free[b] = tm
        t_a = S.op("act", lambda e, b=b, bank=bank: e.activation(out=rt[b][:], in_=bank[:], func=AF.Sqrt, bias=EPS, scale=1.0 / 512),
                   [tm, rt_free[b]])
        pbanks.release(i, t_a)
        t_r = S.op("dve", lambda e, b=b: e.reciprocal(out=rt[b][:], in_=rt[b][:]), [t_a])
        tn = t_r
        for k in range(4):
            tn = S.op("dve", lambda e, k=k, b=b, tt=tt: e.scalar_tensor_tensor(
                out=cnT[:, k, tt * 512:(tt + 1) * 512], in0=cT[:, k, tt * 512:(tt + 1) * 512], scalar=gn[:, k:k + 1],
                in1=rt[b][:], op0=ALU.mult, op1=ALU.mult), [t_r, t_g, tn])
        rt_free[b] = tn
        last = tn
    return last


def phase_up(nc, S, cfg, wuq_d, wukv_d, qg_d, kvg_d, cs_dram, ones32_d, scr):
    SL, HA = cfg.SL, cfg.HA
    NTT, NTB = SL // 512, SL // 128
    for which in ("q", "kv"):
        with Ctx(nc, S) as cx:
            ones32 = cx.sb([128, 128], F32, "ones32")
            so = S.slot()
            t_o = S.dma("sp", lambda e: e.dma_start(out=ones32[:], in_=ones32_d), so)
            cnT = cx.sb([128, 4, SL], BF16, "cnT")
            banks = Rot([cx.ps([128, 512], F32, "pp") for _ in range(4)])
            st16 = OutStage(cx, [128, 512], BF16, n=4, name="st16")
            ncols = HA * 192 if which == "q" else HA * 256
            wt = cx.sb([128, 4, ncols], BF16, "wup")
            sw = S.slot("pool")
            wd = wuq_d if which == "q" else wukv_d
            t_w = S.dma("pool", lambda e: e.dma_start(out=wt[:], in_=wd.rearrange("(k p) n -> p k n", p=128)), sw)
            t_n = latent_norm(cx, cfg, scr["cq_T"] if which == "q" else scr["ckv_T"],
                              qg_d if which == "q" else kvg_d, cnT, ones32, banks, t_o)
            pj = Proj(cx, cfg, cnT, 4, t_n, banks=banks)
            nev = 0
            if which == "q":
                sc = 192.0 ** -0.5
                rp = Rope(cx, pj, cfg, cs_dram, st16)
                for h in range(HA):
                    for tt in range(NTT):
                        i, bank, tm = pj.mm_T(wt, h * 192, 128, tt, t_w)
                        eng = "act" if nev % 2 == 0 else "dve"
                        nev += 1
                        tc = st16.put(eng, lambda e, t, eng=eng, bank=bank: evac(eng, e, t[:], bank[:], sc), [tm],
                                      scr["qa_T"][h, 0:128, tt * 512:(tt + 1) * 512])
                        pj.banks.release(i, tc)
                    rp.run(wt, h * 192 + 128, t_w, sc, scr["qa_T"][h, 128:192, :])
            else:
                for h in range(HA):
                    for tt in range(NTT):
                        i, bank, tm = pj.mm_T(wt, h * 256, 128, tt, t_w)
                        eng = "act" if nev % 2 == 0 else "dve"
                        nev += 1
                        tc = st16.put(eng, lambda e, t, eng=eng, bank=bank: evac(eng, e, t[:], bank[:]), [tm],
                                      scr["ka_T"][h, :, tt * 512:(tt + 1) * 512])
                        pj.banks.release(i, tc)
                    for tb in range(NTB):
                        i, bank, tm = pj.mm_N(wt, h * 256 + 128, 128, tb, t_w)
                        eng = "act" if nev % 2 == 0 else "dve"
                        nev += 1
                        tc = st16.put(eng, lambda e, t, eng=eng, bank=bank: evac(eng, e, t[:, 0:128], bank[:, 0:128]), [tm],
                                      scr["va"][h, tb * 128:(tb + 1) * 128, :], sub=lambda t: t[:, 0:128])
                        pj.banks.release(i, tc)
            cx.flush()


class Rope:
    def __init__(self, cx, pj, cfg, cs_dram, out_stage):
        self.cx, self.pj, self.cfg, self.cs_dram, self.out_stage = cx, pj, cfg, cs_dram, out_stage
        S = cx.S
        self.wr = cx.sb([128, pj.KC, 64], BF16, "wrot")
        self.cst = [cx.sb([64, 2, 512], F32, "cs") for _ in range(2)]
        self.css = [S.slot() for _ in range(2)]
        self.csfree = [None, None]
        self.tmp = [cx.sb([64, 2, 512], F32, "rtmp") for _ in range(2)]
        self.last_mm = None
        self.k = 0

    def run(self, w, c0, wtok, scale, dst_rows):
        S = self.cx.S
        pj, wr, cst, tmp = self.pj, self.wr, self.cst, self.tmp
        t1 = S.op("act", lambda e: e.activation(out=wr[:, :, 0:32], in_=w[:, :, c0 + 32:c0 + 64], func=AF.Copy, scale=-1.0),
                  [wtok, self.last_mm])
        t2 = S.op("act", lambda e: e.activation(out=wr[:, :, 32:64], in_=w[:, :, c0:c0 + 32], func=AF.Copy, scale=1.0),
                  [wtok, t1])
        m2 = None
        for tt in range(self.cfg.SL // 512):
            b = self.k % 2
            self.k += 1
            t_cs = S.dma("sp", lambda e, b=b, tt=tt: e.dma_start(
                out=cst[b][:], in_=self.cs_dram[:, :, tt * 512:(tt + 1) * 512].rearrange("c p t -> p c t")),
                self.css[b], [self.csfree[b]])
            i1, b1, m1 = pj.mm_T(w, c0, 64, tt, wtok)
            i2, b2, m2 = pj.mm_T(wr, 0, 64, tt, t2)
            ta = S.op("dve", lambda e, b=b, b1=b1: e.scalar_tensor_tensor(
                out=tmp[b][:, 0, :], in0=b1[0:64, :], scalar=float(scale), in1=cst[b][:, 0, :], op0=ALU.mult, op1=ALU.mult),
                [m1, t_cs, self.csfree[b]])
            tb_ = S.op("dve", lambda e, b=b, b2=b2: e.scalar_tensor_tensor(
                out=tmp[b][:, 1, :], in0=b2[0:64, :], scalar=float(scale), in1=cst[b][:, 1, :], op0=ALU.mult, op1=ALU.mult),
                [m2, t_cs, ta])
            pj.banks.release(i1, ta)
            pj.banks.release(i2, tb_)
            tc = self.out_stage.put("dve", lambda e, t, b=b: e.tensor_tensor(
                out=t[0:64, :], in0=tmp[b][:, 0, :], in1=tmp[b][:, 1, :], op=ALU.add),
                [ta, tb_], dst_rows[:, tt * 512:(tt + 1) * 512], sub=lambda t: t[0:64, :])
            self.csfree[b] = tc
        self.last_mm = m2
        return m2


def load_const(cx, dram_ap, shape, dt, name):
    t = cx.sb(shape, dt, name)
    sl = cx.S.slot()
    tok = cx.S.dma("sp", lambda e: e.dma_start(out=t[:], in_=dram_ap), sl)
    return t, tok


def phase_mla(nc, S, cfg, I, scr):
    SL, HA = cfg.SL, cfg.HA
    NQT, NB = SL // 512, SL // 128
    with Ctx(nc, S) as cx:
        ident, t_c0 = load_const(cx, I["ident"], [128, 128], BF16, "ident")
        ones, t_c1 = load_const(cx, I["ones_bf"], [128, 128], BF16, "ones")
        maskA, t_c2 = load_const(cx, I["maskA"], [128, 4, 512], BF16, "maskA")
        kr, t_kr = load_const(cx, scr["kr_T"], [64, SL], BF16, "kr")
        t_const = [t_c0, t_c1, t_c2, t_kr]
        qn = [cx.sb([128, SL], BF16, "qn") for _ in range(2)]
        qr = [cx.sb([64, SL], BF16, "qr") for _ in range(2)]
        kn = [cx.sb([128, SL], BF16, "kn") for _ in range(2)]
        vv = [cx.sb([128, NB, 128], BF16, "vv") for _ in range(2)]
        hs = [S.slot() for _ in range(2)]
        head_free = [None, None]
        pS = Rot([cx.ps([128, 512], F32, "pS") for _ in range(2)])
        pO = Rot([cx.ps([128, 512], F32, "pO") for _ in range(2)])
        pD = Rot([cx.ps([128, 512], F32, "pD") for _ in range(2)])
        pts = Rot([cx.sb([128, 512], BF16, "pt") for _ in range(3)])
        gts = Rot([cx.sb([128, 512], BF16, "gt") for _ in range(2)])
        gsl = [S.slot() for _ in range(2)]
        rden = Rot([cx.sb([128, 512], F32, "rden") for _ in range(2)])
        of = Rot([cx.sb([128, 512], F32, "of") for _ in range(2)])
        stg = OutStage(cx, [128, 512], BF16, n=2, name="mixst")

        def load_head(h):
            b = h % 2
            fr = head_free[b]
            S.dma("sp", lambda e: e.dma_start(out=qn[b][:], in_=scr["qa_T"][h, 0:128, :]), hs[b], [fr])
            S.dma("sp", lambda e: e.dma_start(out=qr[b][:], in_=scr["qa_T"][h, 128:192, :]), hs[b], [fr])
            S.dma("sp", lambda e: e.dma_start(out=kn[b][:], in_=scr["ka_T"][h, :, :]), hs[b], [fr])
            tok = None
            for j in range(4):
                tok = S.dma("sp", lambda e, j=j: e.dma_start(
                    out=vv[b][:, j * 8:(j + 1) * 8, :],
                    in_=scr["va"][h, j * 1024:(j + 1) * 1024, :].rearrange("(n p) f -> p n f", p=128)), hs[b], [fr])
            return tok

        blocks = []
        for h in range(HA):
            for qt in range(NQT):
                nkb = 4 * qt + 4
                for kb in range(nkb):
                    blocks.append((h, qt, kb, nkb))
        head_tok = {0: load_head(0)}
        state = {}

        def issue_S(bi):
            h, qt, kb, nkb = blocks[bi]
            b = h % 2
            i_s, Sb, sfr = pS.next()
            diag = kb >= 4 * qt
            S.op("pe", lambda e: e.matmul(out=Sb[:], lhsT=kn[b][:, kb * 128:(kb + 1) * 128], rhs=qn[b][:, qt * 512:(qt + 1) * 512],
                                         start=True, stop=False), [head_tok[h], sfr] + t_const)
            m = S.op("pe", lambda e: e.matmul(out=Sb[:], lhsT=kr[:, kb * 128:(kb + 1) * 128], rhs=qr[b][:, qt * 512:(qt + 1) * 512],
                                             start=False, stop=(not diag)), [])
            if diag:
                m = S.op("pe", lambda e: e.matmul(out=Sb[:], lhsT=ident[:], rhs=maskA[:, kb - 4 * qt, :], start=False, stop=True), [])
            state[bi] = (i_s, Sb, m)

        issue_S(0)
        cur = {}
        for bi, (h, qt, kb, nkb) in enumerate(blocks):
            b = h % 2
            if bi + 1 < len(blocks):
                issue_S(bi + 1)
            i_s, Sb, m = state.pop(bi)
            if kb == 0:
                io, O, ofr = pO.next()
                idn, Dn, dfr = pD.next()
                gi, gt, gfr = gts.next()
                t_gate = S.dma("sp", lambda e, gt=gt, h=h, qt=qt: e.dma_start(
                    out=gt[:], in_=scr["gate_T"][h * 128:(h + 1) * 128, qt * 512:(qt + 1) * 512]), gsl[gi], [gfr])
                cur = dict(io=io, O=O, ofr=ofr, idn=idn, Dn=Dn, dfr=dfr, gi=gi, gt=gt, t_gate=t_gate)
            O, Dn = cur["O"], cur["Dn"]
            ip, pt, pfr = pts.next()
            t_e = S.op("act", lambda e, pt=pt, Sb=Sb: e.activation(out=pt[:], in_=Sb[:], func=AF.Exp), [m, pfr])
            pS.release(i_s, t_e)
            first, last = kb == 0, kb == nkb - 1
            S.op("pe", lambda e, O=O, pt=pt, b=b, kb=kb, first=first, last=last: e.matmul(
                out=O[:], lhsT=vv[b][:, kb, :], rhs=pt[:], start=first, stop=last), [t_e, cur["ofr"] if first else None])
            m5 = S.op("pe", lambda e, Dn=Dn, pt=pt, first=first, last=last: e.matmul(
                out=Dn[:], lhsT=ones[:], rhs=pt[:], start=first, stop=last), [t_e, cur["dfr"] if first else None])
            pts.release(ip, m5)
            if last:
                ir, rd, rfr = rden.next()
                io2, o32, o32fr = of.next()
                t_r = S.op("dve", lambda e, rd=rd, Dn=Dn: e.reciprocal(out=rd[:], in_=Dn[:]), [m5, rfr])
                pD.release(cur["idn"], t_r)
                t_o = S.op("dve", lambda e, o32=o32, O=O, rd=rd: e.tensor_tensor(out=o32[:], in0=O[:], in1=rd[:], op=ALU.mult),
                           [m5, t_r, o32fr])
                pO.release(cur["io"], t_o)
                rden.release(ir, t_o)
                gt = cur["gt"]
                tc = stg.put("dve", lambda e, t, o32=o32, gt=gt: e.tensor_tensor(out=t[:], in0=o32[:], in1=gt[:], op=ALU.mult),
                             [t_o, cur["t_gate"]], scr["mix0_T"][h * 128:(h + 1) * 128, qt * 512:(qt + 1) * 512])
                of.release(io2, tc)
                gts.release(cur["gi"], tc)
                if qt == NQT - 1:
                    head_free[b] = m5
                if qt == 0 and h + 1 < HA:
                    head_tok[h + 1] = load_head(h + 1)
        cx.flush()


DILS = (1, 4, 16)


def dil_cols(T, d, r, n):
    if d == 1:
        return T[:, n * 128:(n + 1) * 128]
    return T[:, :].rearrange("p (m d) -> p d m", d=d)[:, r, n * 128:(n + 1) * 128]


def phase_dil(nc, S, cfg, I, scr):
    SL, HA, HB = cfg.SL, cfg.HA, cfg.HB
    NB = SL // 128
    NST = SL // 2048
    with Ctx(nc, S) as cx:
        ident, t_c0 = load_const(cx, I["ident"], [128, 128], BF16, "ident")
        ones, t_c1 = load_const(cx, I["ones_bf"], [128, 128], BF16, "ones")
        dmask, t_c2 = load_const(cx, I["dmask"], [128, 6, 128], F32, "dmask")
        t_const = [t_c0, t_c1, t_c2]
        qT = [cx.sb([128, SL], BF16, "dq") for _ in range(2)]
        kT = [cx.sb([128, SL], BF16, "dk") for _ in range(2)]
        vd = [[cx.sb([128, NB, 128], BF16, "dv") for _ in range(3)] for _ in range(2)]
        qP = {4: cx.sb([128, SL], BF16, "dqp4"), 16: cx.sb([128, SL], BF16, "dqp16")}
        kP = {4: cx.sb([128, SL], BF16, "dkp4"), 16: cx.sb([128, SL], BF16, "dkp16")}
        perm_free = [None]
        perm_head = [None]
        bmf1 = cx.sb([128, 6, 128], F32, "bmf")
        bsm1 = cx.sb([128, 2, 6, 128], BF16, "bsm")
        blo1 = cx.sb([128, 6, 128], F32, "blo")
        bmf, bsm, blo = [bmf1, bmf1], [bsm1, bsm1], [blo1, blo1]
        prep_free = [None]
        bh = [cx.sb([128, 6, 512], BF16, "bh") for _ in range(2)]
        bl = [cx.sb([128, 6, 512], BF16, "bl") for _ in range(2)]
        hs = [S.slot() for _ in range(2)]
        head_free = [None, None]
        pS = Rot([cx.ps([128, 512], F32, "pS") for _ in range(3)])
        pO = Rot([cx.ps([128, 512], F32, "pO") for _ in range(2)])
        pD = Rot([cx.ps([128, 512], F32, "pD") for _ in range(2)])
        pts = Rot([cx.sb([128, 512], BF16, "pt") for _ in range(4)])
        nacc = [cx.sb([128, 2048], F32, "nacc") for _ in range(2)]
        dacc = [cx.sb([128, 2048], F32, "dacc") for _ in range(2)]
        acc_free = [None, None]
        gts = [cx.sb([128, 2048], BF16, "dgt") for _ in range(2)]
        gsl = [S.slot() for _ in range(2)]
        g_free = [None, None]
        stg = OutStage(cx, [128, 2048], BF16, n=2, name="dmix")

        def load_head(hb):
            b = hb % 2
            fr = head_free[b]
            S.dma("sp", lambda e: e.dma_start(out=qT[b][:], in_=scr["qb_T"][hb * 128:(hb + 1) * 128, :]), hs[b], [fr])
            S.dma("sp", lambda e: e.dma_start(out=kT[b][:], in_=scr["kb_T"][hb * 128:(hb + 1) * 128, :]), hs[b], [fr])
            S.dma("sp", lambda e: e.dma_start(out=bmf[b][:], in_=I["dbias"][hb]), hs[b], [fr, prep_free[0]])
            for j in range(4):
                S.dma("sp", lambda e, j=j: e.dma_start(
                    out=vd[b][0][:, j * 8:(j + 1) * 8, :],
                    in_=scr["vb"][hb, j * 1024:(j + 1) * 1024, :].rearrange("(n p) f -> p n f", p=128)), hs[b], [fr])
            tok = None
            for p, d in ((1, 4), (2, 16)):
                nper = NB // d
                src = scr["vb"][hb].rearrange("(n i r) f -> r i n f", i=128, r=d)
                for r in range(d):
                    tok = S.dma("sp", lambda e, p=p, r=r, nper=nper, src=src: e.dma_start(
                        out=vd[b][p][:, r * nper:(r + 1) * nper, :], in_=src[r]), hs[b], [fr])
            t1 = S.op("dve", lambda e: e.tensor_tensor(out=bmf[b][:], in0=bmf[b][:], in1=dmask[:], op=ALU.add), [tok, t_c2, fr])
            t2 = S.op("dve", lambda e: e.tensor_copy(out=bsm[b][:, 0], in_=bmf[b][:]), [t1])
            t3 = S.op("dve", lambda e: e.tensor_tensor(out=blo[b][:], in0=bmf[b][:], in1=bsm[b][:, 0], op=ALU.subtract), [t2])
            t4 = S.op("dve", lambda e: e.tensor_copy(out=bsm[b][:, 1], in_=blo[b][:]), [t3])
            tl = t4
            for q in range(4):
                tl = S.op("dve", lambda e, q=q: e.tensor_copy(out=bh[b][:, :, q * 128:(q + 1) * 128], in_=bsm[b][:, 0]), [t4, tl])
                tl = S.op("dve", lambda e, q=q: e.tensor_copy(out=bl[b][:, :, q * 128:(q + 1) * 128], in_=bsm[b][:, 1]), [t4, tl])
            prep_free[0] = tl
            return [tok, tl]

        def permute_head(hb):
            b = hb % 2
            toks = []
            for d in (4, 16):
                for src, dst in ((qT[b], qP[d]), (kT[b], kP[d])):
                    toks.append(S.op("pool", lambda e, src=src, dst=dst, d=d: e.tensor_copy(
                        out=dst[:, :].rearrange("p (d m) -> p d m", d=d), in_=src[:, :].rearrange("p (m d) -> p d m", d=d)),
                        [head_tok[hb], perm_free[0]]))
            perm_head[0] = hb
            return toks

        passes = []
        for hb in range(HB):
            for st in range(NST):
                for p, d in enumerate(DILS):
                    for g in range(4):
                        quarters = []
                        for q in range(4):
                            if p == 0:
                                r, n = 0, st * 16 + g * 4 + q
                            elif p == 1:
                                r, n = q, st * 4 + g
                            else:
                                r, n = g * 4 + q, st
                            quarters.append((q, r, n))
                        has_prev = [qq for qq in quarters if qq[2] >= 1]
                        passes.append(dict(hb=hb, st=st, p=p, d=d, g=g, kind=1, qs=quarters, last=(len(has_prev) == 0)))
                        if has_prev:
                            passes.append(dict(hb=hb, st=st, p=p, d=d, g=g, kind=0, qs=has_prev, last=True))
        head_tok = {0: load_head(0)}
        state = {}

        def vslot(p, d, r, n):
            return r * (NB // d) + n

        def pcols(T, TP, d, r, n):
            if d == 1:
                return T[:, n * 128:(n + 1) * 128]
            c = r * (SL // d) + n * 128
            return TP[d][:, c:c + 128]

        def issue_S(pi):
            P = passes[pi]
            hb, st, p, d, g, kind = P["hb"], P["st"], P["p"], P["d"], P["g"], P["kind"]
            b = hb % 2
            if perm_head[0] != hb:
                head_tok[hb] = [head_tok[hb], permute_head(hb)]
            i_s, Sb, sfr = pS.next()
            q0 = P["qs"][0][0]
            c0 = q0 * 128
            bi = 2 * p + kind
            S.op("pe", lambda e: e.matmul(out=Sb[:, c0:512], lhsT=ident[:], rhs=bh[b][:, bi, c0:512], start=True, stop=False,
                                         skip_group_check=True), [head_tok[hb], sfr] + t_const)
            m = S.op("pe", lambda e: e.matmul(out=Sb[:, c0:512], lhsT=ident[:], rhs=bl[b][:, bi, c0:512], start=False, stop=False,
                                             skip_group_check=True), [])
            nq = len(P["qs"])
            for j, (q, r, n) in enumerate(P["qs"]):
                nk = n if kind == 1 else n - 1
                m = S.op("pe", lambda e, q=q, r=r, n=n, nk=nk, j=j: e.matmul(
                    out=Sb[:, q * 128:(q + 1) * 128], lhsT=pcols(kT[b], kP, d, r, nk), rhs=pcols(qT[b], qP, d, r, n),
                    start=False, stop=(j == nq - 1), skip_group_check=True), [])
            state[pi] = (i_s, Sb, m, c0)

        LOOK = 2
        issued = [0]

        def issue_upto(k):
            while issued[0] <= k and issued[0] < len(passes):
                issue_S(issued[0])
                issued[0] += 1

        issue_upto(0)
        cur = {}
        acc_toks = []
        for pi, P in enumerate(passes):
            hb, st, p, d, g, kind = P["hb"], P["st"], P["p"], P["d"], P["g"], P["kind"]
            b = hb % 2
            ab = (hb * NST + st) % 2
            lim = pi
            while lim + 1 < len(passes) and lim + 1 <= pi + LOOK and passes[lim + 1]["hb"] == hb:
                lim += 1
            issue_upto(lim)
            defer = pi + 1 < len(passes) and passes[pi + 1]["hb"] != hb
            i_s, Sb, m, c0 = state.pop(pi)
            if kind == 1:
                io, O, ofr = pO.next()
                idn, Dn, dfr = pD.next()
                cur = dict(io=io, O=O, ofr=ofr, idn=idn, Dn=Dn, dfr=dfr)
                if p == 0 and g == 0:
                    acc_toks = []
                    t_gate = S.dma("sp", lambda e, ab=ab, hb=hb, st=st: e.dma_start(
                        out=gts[ab][:], in_=scr["gate_T"][(HA + hb) * 128:(HA + hb + 1) * 128, st * 2048:(st + 1) * 2048]),
                        gsl[ab], [g_free[ab]])
            O, Dn = cur["O"], cur["Dn"]
            ip, pt, pfr = pts.next()
            t_e = S.op("act", lambda e, pt=pt, Sb=Sb, c0=c0: e.activation(out=pt[:, c0:512], in_=Sb[:, c0:512], func=AF.Exp), [m, pfr])
            pS.release(i_s, t_e)
            nq = len(P["qs"])
            for j, (q, r, n) in enumerate(P["qs"]):
                nk = n if kind == 1 else n - 1
                first = (kind == 1 and j == 0)
                S.op("pe", lambda e, O=O, pt=pt, q=q, b=b, p=p, sl=vslot(p, d, r, nk), first=first, lastmm=(P["last"] and j == nq - 1): e.matmul(
                    out=O[:, q * 128:(q + 1) * 128], lhsT=vd[b][p][:, sl, :], rhs=pt[:, q * 128:(q + 1) * 128],
                    start=first, stop=lastmm, skip_group_check=True), [t_e, cur["ofr"] if first else None])
            m5 = S.op("pe", lambda e, Dn=Dn, pt=pt, c0=c0, kind=kind, lastp=P["last"]: e.matmul(
                out=Dn[:, c0:512], lhsT=ones[:], rhs=pt[:, c0:512], start=(kind == 1), stop=lastp, skip_group_check=True),
                [t_e, cur["dfr"] if kind == 1 else None])
            pts.release(ip, m5)
            if P["last"]:
                if p == 0:
                    nv = nacc[ab][:, g * 512:(g + 1) * 512]
                    dv = dacc[ab][:, g * 512:(g + 1) * 512]
                    Ov, Dv = O[:], Dn[:]
                elif p == 1:
                    nv = nacc[ab][:, g * 512:(g + 1) * 512].rearrange("p (i r) -> p r i", r=4)
                    dv = dacc[ab][:, g * 512:(g + 1) * 512].rearrange("p (i r) -> p r i", r=4)
                    Ov = O[:, :].rearrange("p (r i) -> p r i", r=4)
                    Dv = Dn[:, :].rearrange("p (r i) -> p r i", r=4)
                else:
                    nv = nacc[ab][:, :].rearrange("p (i g r) -> p g r i", g=4, r=4)[:, g]
                    dv = dacc[ab][:, :].rearrange("p (i g r) -> p g r i", g=4, r=4)[:, g]
                    Ov = O[:, :].rearrange("p (r i) -> p r i", r=4)
                    Dv = Dn[:, :].rearrange("p (r i) -> p r i", r=4)
                if p == 0:
                    ta = S.op("act", lambda e, nv=nv, Ov=Ov: e.activation(out=nv, in_=Ov, func=AF.Copy), [m5, acc_free[ab]])
                    tb_ = S.op("dve", lambda e, dv=dv, Dv=Dv: e.tensor_copy(out=dv, in_=Dv), [m5, acc_free[ab]])
                else:
                    ta = S.op("dve", lambda e, nv=nv, Ov=Ov: e.tensor_tensor(out=nv, in0=Ov, in1=nv, op=ALU.add), [m5] + acc_toks)
                    tb_ = S.op("dve", lambda e, dv=dv, Dv=Dv: e.tensor_tensor(out=dv, in0=Dv, in1=dv, op=ALU.add), [m5, ta] + acc_toks)
                acc_toks = acc_toks + [ta, tb_]
                pO.release(cur["io"], ta)
                pD.release(cur["idn"], tb_)
                if p == 2 and g == 3:
                    t_r = S.op("dve", lambda e, ab=ab: e.reciprocal(out=dacc[ab][:], in_=dacc[ab][:]), acc_toks)
                    t_o = S.op("dve", lambda e, ab=ab: e.tensor_tensor(out=nacc[ab][:], in0=nacc[ab][:], in1=dacc[ab][:], op=ALU.mult), [t_r])
                    tc = stg.put("dve", lambda e, t, ab=ab: e.tensor_tensor(out=t[:], in0=nacc[ab][:], in1=gts[ab][:], op=ALU.mult),
                                 [t_o, t_gate], scr["mix0_T"][(HA + hb) * 128:(HA + hb + 1) * 128, st * 2048:(st + 1) * 2048])
                    acc_free[ab] = tc
                    g_free[ab] = tc
                    if st == NST - 1:
                        head_free[b] = m5
                        perm_free[0] = m5
                    if st == 0 and hb + 1 < HB:
                        head_tok[hb + 1] = load_head(hb + 1)
            if defer:
                issue_upto(pi + 1)
        cx.flush()


def phase_sb(nc, S, cfg, I, scr, extra_flush=()):
    SL, HS = cfg.SL, cfg.HS
    NQT, NB = SL // 512, SL // 128
    with Ctx(nc, S) as cx:
        ident, t_c0 = load_const(cx, I["ident"], [128, 128], BF16, "ident")
        ones, t_c1 = load_const(cx, I["ones_bf"], [128, 128], BF16, "ones")
        tri, t_c2 = load_const(cx, I["tri"], [128, 128], BF16, "tri")
        mSn, t_c3 = load_const(cx, I["maskSn"], [128, 4, 512], BF16, "mSn")
        mSp, t_c4 = load_const(cx, I["maskSp"], [128, 4, 512], BF16, "mSp")
        t_const = [t_c0, t_c1, t_c2, t_c3, t_c4]
        qT = [cx.sb([128, SL], BF16, "sq") for _ in range(2)]
        kT = [cx.sb([128, SL], BF16, "sk") for _ in range(2)]
        nkT = [cx.sb([128, SL], BF16, "snk") for _ in range(2)]
        vv = [cx.sb([128, NB, 128], BF16, "sv") for _ in range(2)]
        hs = [S.slot() for _ in range(2)]
        head_free = [None, None]
        pS = Rot([cx.ps([128, 512], F32, "pS") for _ in range(2)])
        pC = Rot([cx.ps([128, 512], F32, "pC") for _ in range(2)])
        pO = Rot([cx.ps([128, 512], F32, "pO") for _ in range(2)])
        Et = Rot([cx.sb([128, 512], F32, "Et") for _ in range(2)])
        Lt = Rot([cx.sb([128, 512], BF16, "Lt") for _ in range(4)])
        Ls = Rot([cx.sb([128, 512], BF16, "Ls") for _ in range(3)])
        At = Rot([cx.sb([128, 512], BF16, "At") for _ in range(3)])
        gts = Rot([cx.sb([128, 512], BF16, "sgt") for _ in range(2)])
        gsl = [S.slot() for _ in range(2)]
        stg = OutStage(cx, [128, 512], BF16, n=2, name="smix")

        def load_head(h):
            b = h % 2
            fr = head_free[b]
            S.dma("sp", lambda e: e.dma_start(out=qT[b][:], in_=scr["q1_T"][h * 128:(h + 1) * 128, :]), hs[b], [fr])
            S.dma("sp", lambda e: e.dma_start(out=kT[b][:], in_=scr["k1_T"][h * 128:(h + 1) * 128, :]), hs[b], [fr])
            tok = None
            for j in range(4):
                tok = S.dma("sp", lambda e, j=j: e.dma_start(
                    out=vv[b][:, j * 8:(j + 1) * 8, :],
                    in_=scr["v1"][h, j * 1024:(j + 1) * 1024, :].rearrange("(n p) f -> p n f", p=128)), hs[b], [fr])
            t_nk = S.op("pool", lambda e: e.tensor_scalar(out=nkT[b][:], in0=kT[b][:], scalar1=-1.0, scalar2=None, op0=ALU.mult),
                        [tok, fr])
            return [tok, t_nk]

        blocks = []
        for h in range(HS):
            for qt in range(NQT):
                for kb in range(4 * qt + 3, -1, -1):
                    blocks.append((h, qt, kb))
        NBK = len(blocks)
        head_tok = {0: load_head(0)}
        st1, st2, st3 = {}, {}, {}
        first_il = [None]
        first_users = [[]]
        ls_cur = {}
        ocur = {}

        def stage_S(bi):
            h, qt, kb = blocks[bi]
            b = h % 2
            i_s, Sb, sfr = pS.next()
            diag = kb >= 4 * qt
            m = S.op("pe", lambda e: e.matmul(out=Sb[:], lhsT=kT[b][:, kb * 128:(kb + 1) * 128], rhs=qT[b][:, qt * 512:(qt + 1) * 512],
                                             start=True, stop=(not diag)), [head_tok[h], sfr] + t_const)
            if diag:
                m = S.op("pe", lambda e: e.matmul(out=Sb[:], lhsT=ident[:], rhs=mSn[:, kb - 4 * qt, :], start=False, stop=True), [])
            st1[bi] = (i_s, Sb, m)

        def stage_EL(bi):
            h, qt, kb = blocks[bi]
            i_s, Sb, m = st1.pop(bi)
            ie, E, efr = Et.next()
            il, L, lfr = Lt.next()
            t_e = S.op("act", lambda e: e.activation(out=E[:], in_=Sb[:], func=AF.Exp), [m, efr])
            pS.release(i_s, t_e)
            t_l = S.op("act", lambda e: e.activation(out=L[:], in_=E[:], func=AF.Ln, bias=1.0, scale=1.0), [t_e])
            Et.release(ie, t_l)
            st2[bi] = (il, L, t_l)

        def stage_C(bi):
            h, qt, kb = blocks[bi]
            b = h % 2
            il, L, t_l = st2.pop(bi)
            first = kb == 4 * qt + 3
            last = kb == 0
            diag = kb >= 4 * qt
            i_c, Cb, cfr = pC.next()
            S.op("pe", lambda e: e.matmul(out=Cb[:], lhsT=tri[:], rhs=L[:], start=True, stop=False), [t_l, cfr])
            m = S.op("pe", lambda e: e.matmul(out=Cb[:], lhsT=nkT[b][:, kb * 128:(kb + 1) * 128], rhs=qT[b][:, qt * 512:(qt + 1) * 512],
                                             start=False, stop=(not diag and first)), [])
            if diag:
                m = S.op("pe", lambda e: e.matmul(out=Cb[:], lhsT=ident[:], rhs=mSp[:, kb - 4 * qt, :], start=False, stop=first), [])
            users = [m]
            if not first:
                ils, Lsum, t_ls = ls_cur[bi]
                m = S.op("pe", lambda e: e.matmul(out=Cb[:], lhsT=ones[:], rhs=Lsum[:], start=False, stop=True), [t_ls])
                users = [m]
            if not last:
                if first:
                    ls_cur[bi + 1] = (None, L, t_l)
                    users.append(("hold",))
                else:
                    iln, Lnew, lnfr = Ls.next()
                    t_n = S.op("dve", lambda e: e.tensor_tensor(out=Lnew[:], in0=Lsum[:], in1=L[:], op=ALU.add), [t_l, t_ls])
                    ls_cur[bi + 1] = (iln, Lnew, t_n)
                    users.append(t_n)
            if not first:
                if ils is not None:
                    Ls.release(ils, [u for u in users if u != ("hold",)])
                else:
                    Lt.release(first_il[0], [u for u in users if u != ("hold",)] + first_users[0])
                ls_cur.pop(bi)
            if ("hold",) in users:
                first_il[0] = il
                first_users[0] = [u for u in users if u != ("hold",)]
            else:
                Lt.release(il, list(users))
            st3[bi] = (i_c, Cb, m)

        def stage_A(bi):
            i_c, Cb, m = st3.pop(bi)
            ia, A, afr = At.next()
            t_a = S.op("act", lambda e: e.activation(out=A[:], in_=Cb[:], func=AF.Exp, scale=-1.0), [m, afr])
            pC.release(i_c, t_a)
            st3[("A", bi)] = (ia, A, t_a)

        def stage_PV(bi):
            h, qt, kb = blocks[bi]
            b = h % 2
            ia, A, t_a = st3.pop(("A", bi))
            first = kb == 4 * qt + 3
            last = kb == 0
            if first:
                io, O, ofr = pO.next()
                gi, gt, gfr = gts.next()
                t_gate = S.dma("sp", lambda e: e.dma_start(
                    out=gt[:], in_=scr["gate1_T"][h * 128:(h + 1) * 128, qt * 512:(qt + 1) * 512]), gsl[gi], [gfr])
                ocur.update(io=io, O=O, ofr=ofr, gi=gi, gt=gt, t_gate=t_gate)
            O = ocur["O"]
            m = S.op("pe", lambda e: e.matmul(out=O[:], lhsT=vv[b][:, kb, :], rhs=A[:], start=first, stop=last),
                     [t_a, ocur["ofr"] if first else None])
            At.release(ia, m)
            if last:
                gt = ocur["gt"]
                tc = stg.put("dve", lambda e, t: e.tensor_tensor(out=t[:], in0=O[:], in1=gt[:], op=ALU.mult),
                             [m, ocur["t_gate"]], scr["mix1_T"][h * 128:(h + 1) * 128, qt * 512:(qt + 1) * 512])
                pO.release(ocur["io"], tc)
                gts.release(ocur["gi"], tc)
                if qt == NQT - 1:
                    head_free[b] = m
                if qt == 0 and h + 1 < HS:
                    head_tok[h + 1] = load_head(h + 1)

        for step in range(-2, NBK):
            if 0 <= step + 2 < NBK:
                stage_S(step + 2)
                stage_EL(step + 2)
            if 0 <= step + 1 < NBK:
                stage_C(step + 1)
                stage_A(step + 1)
            if 0 <= step < NBK:
                stage_PV(step)
        cx.flush(extra_flush)


def phase_sb2(nc, S, cfg, I, scr, extra_flush=()):
    SL, HS = cfg.SL, cfg.HS
    NQT, NB = SL // 512, SL // 128
    with Ctx(nc, S) as cx:
        ident, t_c0 = load_const(cx, I["ident"], [128, 128], BF16, "ident")
        ones, t_c1 = load_const(cx, I["ones_bf"], [128, 128], BF16, "ones")
        tri, t_c2 = load_const(cx, I["tri"], [128, 128], BF16, "tri")
        mSn, t_c3 = load_const(cx, I["maskSn"], [128, 4, 512], BF16, "mSn")
        mSp, t_c4 = load_const(cx, I["maskSp"], [128, 4, 512], BF16, "mSp")
        t_const = [t_c0, t_c1, t_c2, t_c3, t_c4]
        qT = [cx.sb([128, SL], BF16, "sq") for _ in range(2)]
        kT = [cx.sb([128, SL], BF16, "sk") for _ in range(2)]
        nkT = [cx.sb([128, SL], BF16, "snk") for _ in range(2)]
        vv = [cx.sb([128, NB, 128], BF16, "sv") for _ in range(2)]
        hs = [S.slot() for _ in range(2)]
        head_free = [None, None]
        pS = Rot([cx.ps([128, 2, 512], F32, "pS") for _ in range(1)])
        pC = Rot([cx.ps([128, 2, 512], F32, "pC") for _ in range(2)])
        pO = Rot([cx.ps([128, 512], F32, "pO") for _ in range(2)])
        Et = Rot([cx.sb([128, 2, 512], F32, "Et") for _ in range(2)])
        Lt = Rot([cx.sb([128, 2, 512], BF16, "Lt") for _ in range(3)])
        Ls = Rot([cx.sb([128, 512], BF16, "Ls") for _ in range(3)])
        At = Rot([cx.sb([128, 2, 512], BF16, "At") for _ in range(2)])
        gts = Rot([cx.sb([128, 512], BF16, "sgt") for _ in range(2)])
        gsl = [S.slot() for _ in range(2)]
        stg = OutStage(cx, [128, 512], BF16, n=2, name="smix")

        def load_head(h):
            b = h % 2
            fr = head_free[b]
            S.dma("sp", lambda e: e.dma_start(out=qT[b][:], in_=scr["q1_T"][h * 128:(h + 1) * 128, :]), hs[b], [fr])
            S.dma("sp", lambda e: e.dma_start(out=kT[b][:], in_=scr["k1_T"][h * 128:(h + 1) * 128, :]), hs[b], [fr])
            tok = None
            for j in range(4):
                tok = S.dma("sp", lambda e, j=j: e.dma_start(
                    out=vv[b][:, j * 8:(j + 1) * 8, :],
                    in_=scr["v1"][h, j * 1024:(j + 1) * 1024, :].rearrange("(n p) f -> p n f", p=128)), hs[b], [fr])
            t_nk = S.op("pool", lambda e: e.tensor_scalar(out=nkT[b][:], in0=kT[b][:], scalar1=-1.0, scalar2=None, op0=ALU.mult),
                        [tok, fr])
            return [tok, t_nk]

        pairs = []
        for h in range(HS):
            for qt in range(NQT):
                np_ = 2 * qt + 2
                for j in range(np_):
                    pairs.append((h, qt, j, np_))
        NP = len(pairs)
        head_tok = {0: load_head(0)}
        st1, st2, st3, st4 = {}, {}, {}, {}
        ls_cur = {}
        ocur = {}

        def kbs(qt, j):
            return (4 * qt + 3 - 2 * j, 4 * qt + 2 - 2 * j)

        def stage_S(i):
            h, qt, j, np_ = pairs[i]
            b = h % 2
            i_s, Sp, sfr = pS.next()
            diag = j < 2
            m = None
            for t, kb in enumerate(kbs(qt, j)):
                m = S.op("pe", lambda e, t=t, kb=kb: e.matmul(
                    out=Sp[:, t, :], lhsT=kT[b][:, kb * 128:(kb + 1) * 128], rhs=qT[b][:, qt * 512:(qt + 1) * 512],
                    start=True, stop=(not diag)), [head_tok[h], sfr] + t_const)
                if diag:
                    m = S.op("pe", lambda e, t=t, kb=kb: e.matmul(
                        out=Sp[:, t, :], lhsT=ident[:], rhs=mSn[:, kb - 4 * qt, :], start=False, stop=True), [])
            st1[i] = (i_s, Sp, m)

        def stage_EL(i):
            i_s, Sp, m = st1.pop(i)
            ie, E, efr = Et.next()
            il, L, lfr = Lt.next()
            t_e = S.op("act", lambda e: e.activation(out=E[:], in_=Sp[:], func=AF.Exp), [m, efr])
            pS.release(i_s, t_e)
            t_l = S.op("act", lambda e: e.activation(out=L[:], in_=E[:], func=AF.Ln, bias=1.0, scale=1.0), [t_e])
            Et.release(ie, t_l)
            st2[i] = (il, L, t_l)

        def stage_C(i):
            h, qt, j, np_ = pairs[i]
            b = h % 2
            il, L, t_l = st2.pop(i)
            first, last, diag = j == 0, j == np_ - 1, j < 2
            i_c, Cp, cfr = pC.next()
            if not first:
                ils, Lsum, t_ls = ls_cur.pop(i)
            m = None
            for t, kb in enumerate(kbs(qt, j)):
                seq = [(tri[:], L[:, t, :], [t_l, cfr])]
                seq.append((nkT[b][:, kb * 128:(kb + 1) * 128], qT[b][:, qt * 512:(qt + 1) * 512], []))
                if diag:
                    seq.append((ident[:], mSp[:, kb - 4 * qt, :], []))
                if t == 1:
                    seq.append((ones[:], L[:, 0, :], []))
                if not first:
                    seq.append((ones[:], Lsum[:], [t_ls]))
                for k, (lh, rh, deps) in enumerate(seq):
                    m = S.op("pe", lambda e, t=t, lh_ap=lh, rh=rh, k=k, n=len(seq): e.matmul(
                        out=Cp[:, t, :], lhsT=lh_ap, rhs=rh, start=(k == 0), stop=(k == n - 1)), deps)
            users = [m]
            if not last:
                iln, Lnew, lnfr = Ls.next()
                if first:
                    t_n = S.op("dve", lambda e: e.tensor_tensor(out=Lnew[:], in0=L[:, 0, :], in1=L[:, 1, :], op=ALU.add), [t_l])
                else:
                    t_n0 = S.op("dve", lambda e: e.tensor_tensor(out=Lnew[:], in0=Lsum[:], in1=L[:, 0, :], op=ALU.add), [t_l, t_ls])
                    t_n = S.op("dve", lambda e: e.tensor_tensor(out=Lnew[:], in0=Lnew[:], in1=L[:, 1, :], op=ALU.add), [t_n0])
                ls_cur[i + 1] = (iln, Lnew, t_n)
                users.append(t_n)
            if not first:
                Ls.release(ils, list(users))
            Lt.release(il, list(users))
            st3[i] = (i_c, Cp, m)

        def stage_A(i):
            i_c, Cp, m = st3.pop(i)
            ia, A, afr = At.next()
            t_a = S.op("act", lambda e: e.activation(out=A[:], in_=Cp[:], func=AF.Exp, scale=-1.0), [m, afr])
            pC.release(i_c, t_a)
            st4[i] = (ia, A, t_a)

        def stage_PV(i):
            h, qt, j, np_ = pairs[i]
            b = h % 2
            ia, A, t_a = st4.pop(i)
            first, last = j == 0, j == np_ - 1
            if first:
                io, O, ofr = pO.next()
                gi, gt, gfr = gts.next()
                t_gate = S.dma("sp", lambda e: e.dma_start(
                    out=gt[:], in_=scr["gate1_T"][h * 128:(h + 1) * 128, qt * 512:(qt + 1) * 512]), gsl[gi], [gfr])
                ocur.update(io=io, O=O, ofr=ofr, gi=gi, gt=gt, t_gate=t_gate)
            O = ocur["O"]
            m = None
            for t, kb in enumerate(kbs(qt, j)):
                m = S.op("pe", lambda e, t=t, kb=kb: e.matmul(
                    out=O[:], lhsT=vv[b][:, kb, :], rhs=A[:, t, :], start=(first and t == 0), stop=(last and t == 1)),
                    [t_a, ocur["ofr"] if (first and t == 0) else None])
            At.release(ia, m)
            if last:
                gt = ocur["gt"]
                tc = stg.put("dve", lambda e, tl: e.tensor_tensor(out=tl[:], in0=O[:], in1=gt[:], op=ALU.mult),
                             [m, ocur["t_gate"]], scr["mix1_T"][h * 128:(h + 1) * 128, qt * 512:(qt + 1) * 512])
                pO.release(ocur["io"], tc)
                gts.release(ocur["gi"], tc)
                if qt == NQT - 1:
                    head_free[b] = m
                if qt == 0 and h + 1 < HS:
                    head_tok[h + 1] = load_head(h + 1)

        for step in range(-2, NP):
            if 0 <= step + 2 < NP:
                stage_S(step + 2)
                stage_EL(step + 2)
            if 0 <= step + 1 < NP:
                stage_C(step + 1)
                stage_A(step + 1)
            if 0 <= step < NP:
                stage_PV(step)
        cx.flush(extra_flush)


def phase_gather(nc, S, cfg, src, dst):
    groups = [[cfg.NR * i + j for j in range(cfg.NR)] for i in range(8 // cfg.NR)]
    sl = S.slot("pool")
    t = S.dma("pool", lambda e: e.collective_compute("AllGather", ALU.bypass, replica_groups=groups, ins=[src], outs=[dst]), sl)
    S.op("sp", lambda e: e.nop(), [t])
    S.emit()


def load_wo(S, wo, wo_dram):
    sw = S.slot("pool")
    t_w = None
    for cg in range(4):
        t_w = S.dma("pool", lambda e, cg=cg: e.dma_start(
            out=wo[:, :, cg * 512:(cg + 1) * 512],
            in_=wo_dram[:, cg * 512:(cg + 1) * 512].rearrange("(k p) n -> p k n", p=128)), sw)
    return t_w


def phase_out(nc, S, cfg, mixT_dram, wo_dram, xin_dram, xout_dram, final_gain=None, out_dram=None, wo_pre=None):
    SL, D = cfg.SL, cfg.D
    KC = D // 128
    with Ctx(nc, S) as cx:
        if wo_pre is not None:
            wo, t_w = wo_pre
        else:
            wo = cx.sb([128, KC, D], BF16, "wo")
            t_w = load_wo(S, wo, wo_dram)
        mts = Rot([cx.sb([128, KC, 512], BF16, "mt") for _ in range(2)])
        msl = [S.slot() for _ in range(2)]
        xts = Rot([cx.sb([128, D], F32, "xo") for _ in range(2)])
        xsl = [S.slot() for _ in range(2)]
        banks = [cx.ps([128, 512], F32, "po") for _ in range(8)]
        bfree = [None] * 8
        ystg = OutStage(cx, [128, D], F32, n=2, name="yst", dma_eng="sp")
        if final_gain is not None:
            gft, t_g = load_const(cx, final_gain.partition_broadcast(128), [128, D], F32, "gft")
            junk = cx.sb([128, D], BF16, "junk")
            ss = cx.sb([128, SL // 128], F32, "ss")
            rs = cx.sb([128, SL // 128], F32, "rs")
            ostg = OutStage(cx, [128, D], F32, n=2, name="ost", dma_eng="sp")
            ysb = Rot([cx.sb([128, D], F32, "ysb") for _ in range(2)])
        for tb in range(SL // 128):
            if tb % 4 == 0:
                im, mt, mfr = mts.next()
                t_m = S.dma("sp", lambda e, mt=mt, tb=tb: e.dma_start(
                    out=mt[:], in_=mixT_dram[:, tb * 128:(tb + 4) * 128].rearrange("(k p) t -> p k t", p=128)), msl[im], [mfr])
            tq = tb % 4
            ix, xt, xfr = xts.next()
            t_x = S.dma("sp", lambda e, xt=xt, tb=tb: e.dma_start(out=xt[:], in_=xin_dram[tb * 128:(tb + 1) * 128, :]), xsl[ix], [xfr])
            mms = []
            for cg in range(4):
                bi = (tb % 2) * 4 + cg
                bank = banks[bi]
                tm = None
                for kc in range(KC):
                    tm = S.op("pe", lambda e, bank=bank, mt=mt, kc=kc, cg=cg, tq=tq: e.matmul(
                        out=bank[:], lhsT=mt[:, kc, tq * 128:(tq + 1) * 128], rhs=wo[:, kc, cg * 512:(cg + 1) * 512],
                        start=(kc == 0), stop=(kc == KC - 1)), [t_m, t_w, bfree[bi]])
                mms.append(tm)
            if tq == 3:
                mts.release(im, mms[-1])
            if final_gain is None:
                comps = []
                for cg in range(4):
                    bank = banks[(tb % 2) * 4 + cg]
                    comps.append(("dve", lambda e, t, bank=bank, xt=xt, cg=cg: e.tensor_tensor(
                        out=t[:, cg * 512:(cg + 1) * 512], in0=bank[:], in1=xt[:, cg * 512:(cg + 1) * 512], op=ALU.add),
                        [mms[cg], t_x]))
                toks = ystg.put_multi(comps, xout_dram[tb * 128:(tb + 1) * 128, :])
                for cg in range(4):
                    bfree[(tb % 2) * 4 + cg] = toks[cg]
                xts.release(ix, toks[-1])
            else:
                iy, y, yfr = ysb.next()
                tl = yfr
                toks = []
                for cg in range(4):
                    bank = banks[(tb % 2) * 4 + cg]
                    tl = S.op("dve", lambda e, y=y, bank=bank, xt=xt, cg=cg: e.tensor_tensor(
                        out=y[:, cg * 512:(cg + 1) * 512], in0=bank[:], in1=xt[:, cg * 512:(cg + 1) * 512], op=ALU.add),
                        [mms[cg], t_x, tl])
                    bfree[(tb % 2) * 4 + cg] = tl
                    toks.append(tl)
                xts.release(ix, tl)
                t_ss = S.op("act", lambda e, y=y, tb=tb: e.activation(out=junk[:], in_=y[:], func=AF.Square,
                                                                      accum_out=ss[:, tb:tb + 1]), [tl])
                t_a = S.op("act", lambda e, tb=tb: e.activation(out=rs[:, tb:tb + 1], in_=ss[:, tb:tb + 1], func=AF.Sqrt,
                                                                bias=EPS, scale=1.0 / D), [t_ss])
                t_r = S.op("dve", lambda e, tb=tb: e.reciprocal(out=rs[:, tb:tb + 1], in_=rs[:, tb:tb + 1]), [t_a])
                tc = ostg.put("dve", lambda e, t, y=y, tb=tb: e.scalar_tensor_tensor(
                    out=t[:], in0=y[:], scalar=rs[:, tb:tb + 1], in1=gft[:], op0=ALU.mult, op1=ALU.mult),
                    [t_r, t_g], out_dram[tb * 128:(tb + 1) * 128, :])
                ysb.release(iy, tc)
        cx.flush()


def phase_proj1(nc, S, cfg, hT, w1, scr):
    SL, HS = cfg.SL, cfg.HS
    KC = cfg.D // 128
    NTT, NTB = SL // 512, SL // 128
    with Ctx(nc, S) as cx:
        pj = Proj(cx, cfg, hT, KC, None)
        ws = WStream(cx, KC)
        st16 = OutStage(cx, [128, 512], BF16, n=6, name="st16")
        groups = []
        c = 0
        for g in range(HS * 128 // 512):
            groups.append((c + g * 512, "T", [(scr["q1_T"], g * 512, 128.0 ** -0.5, None)]))
        c += HS * 128
        for g in range(HS * 128 // 512):
            groups.append((c + g * 512, "T", [(scr["k1_T"], g * 512, 1.0, None)]))
        c += HS * 128
        for g in range(HS * 128 // 512):
            groups.append((c + g * 512, "N", (scr["v1"], g * 4)))
        c += HS * 128
        for g in range(cfg.FM1 // 512):
            groups.append((c + g * 512, "T", [(scr["gate1_T"], g * 512, 1.0, AF.Silu)]))
        nxt = ws.load(w1[:, groups[0][0]:groups[0][0] + 512], 512)
        nev = 0
        for gi, (c0, kind, spec) in enumerate(groups):
            wi, wt, wtok = nxt
            if gi + 1 < len(groups):
                n0 = groups[gi + 1][0]
                nxt = ws.load(w1[:, n0:n0 + 512], 512)
            last = None
            if kind == "T":
                for cb in range(4):
                    for tt in range(NTT):
                        i, bank, tm = pj.mm_T(wt, cb * 128, 128, tt, wtok)
                        last = tm
                        tcs = []
                        for (dst, r0, scale, func) in spec:
                            eng = "act" if (func is not None or nev % 2 == 0) else "dve"
                            nev += 1
                            tcs.append(st16.put(eng, lambda e, t, eng=eng, bank=bank, scale=scale, func=func: evac(
                                eng, e, t[:], bank[:], scale, func), [tm],
                                dst[r0 + cb * 128:r0 + (cb + 1) * 128, tt * 512:(tt + 1) * 512]))
                        pj.banks.release(i, tcs)
            else:
                dst, h0 = spec
                for tb in range(NTB):
                    i, bank, tm = pj.mm_N(wt, 0, 512, tb, wtok)
                    last = tm
                    eng = "act" if nev % 2 == 0 else "dve"
                    nev += 1
                    tc = st16.put(eng, lambda e, t, eng=eng, bank=bank: evac(eng, e, t[:], bank[:]), [tm],
                                  dst[h0:h0 + 4, tb * 128:(tb + 1) * 128, :].rearrange("h p f -> p h f"),
                                  sub=lambda t: t[:].rearrange("p (h f) -> p h f", h=4))
                    pj.banks.release(i, tc)
            ws.release(wi, last)
        cx.flush()


def const_arrays(cfg):
    SL = cfg.SL
    bf = ml_dtypes.bfloat16
    c = {}
    c["ident"] = np.eye(128, dtype=np.float32).astype(bf)
    c["ones_bf"] = np.ones((128, 128), np.float32).astype(bf)
    c["ones32"] = np.ones((128, 128), np.float32)
    j = np.arange(128)[:, None]
    s = np.arange(128)[None, :]
    c["tri"] = (j >= s).astype(np.float32).astype(bf)
    t = np.arange(512)[None, None, :]
    i = np.arange(4)[None, :, None]
    jj = np.arange(128)[:, None, None]
    c["maskA"] = np.where(128 * i + jj <= t, 0.0, NEG).astype(np.float32).astype(bf)
    mneg = np.where(128 * i + jj < t, 0.0, NEG).astype(np.float32)
    c["maskSn"] = mneg.astype(bf)
    c["maskSp"] = (-mneg).astype(bf)
    qi = np.arange(128)[None, :]
    kj = np.arange(128)[:, None]
    mprev = np.where(kj >= qi, 0.0, NEG)
    mcur = np.where(kj <= qi, 0.0, NEG)
    dm = np.zeros((128, 6, 128), np.float32)
    for p in range(3):
        dm[:, 2 * p, :] = mprev
        dm[:, 2 * p + 1, :] = mcur
    c["dmask"] = dm
    half = 32
    inv = 1.0 / (10000.0 ** (np.arange(half, dtype=np.float32) / half))
    ang = np.arange(SL, dtype=np.float32)[None, :] * inv[:, None]
    ang = ang.astype(np.float32)
    cs = np.zeros((2, 64, SL), np.float32)
    cs[0, :32] = np.cos(ang)
    cs[0, 32:] = np.cos(ang)
    cs[1, :32] = np.sin(ang)
    cs[1, 32:] = np.sin(ang)
    c["cs"] = cs
    return c


def t5_bucket_np(dist):
    max_exact = 16
    d = np.maximum(dist.astype(np.float32), 1.0)
    large = max_exact + (np.log(d / max_exact) / math.log(2048 / max_exact) * (32 - max_exact)).astype(np.int32)
    large = np.minimum(large, 31)
    return np.where(dist < max_exact, dist, large)


def dil_bias_index():
    qi = np.arange(128)[None, :]
    kj = np.arange(128)[:, None]
    idx = np.zeros((6, 128, 128), np.int64)
    for p, d in enumerate((1, 4, 16)):
        rel_prev = np.maximum(128 + qi - kj, 0)
        rel_cur = np.maximum(qi - kj, 0)
        idx[2 * p] = t5_bucket_np((rel_prev * d).astype(np.int32))
        idx[2 * p + 1] = t5_bucket_np((rel_cur * d).astype(np.int32))
    return idx


def build_program(cfg, phases, debug=()):
    nc = bass.Bass("TRN2", target_bir_lowering=False)
    SL, D, HA, HB, HS = cfg.SL, cfg.D, cfg.HA, cfg.HB, cfg.HS

    def din(name, shape, dt=F32):
        return nc.dram_tensor(name, list(shape), dt, kind="ExternalInput").ap()

    def dscr(name, shape, dt):
        kind = "ExternalOutput" if name in debug else "Internal"
        return nc.dram_tensor(name, list(shape), dt, kind=kind).ap()

    I = {}
    I["x"] = din("x", [SL, D])
    I["g0"] = din("g0", [1, D])
    I["g1"] = din("g1", [1, D])
    I["gf"] = din("gf", [1, D])
    I["w0"] = din("w0", [D, cfg.W0C])
    I["qg"] = din("qg", [128, 4])
    I["kvg"] = din("kvg", [128, 4])
    I["wuq"] = din("wuq", [512, HA * 192])
    I["wukv"] = din("wukv", [512, HA * 256])
    I["wo0"] = din("wo0", [D, D])
    I["dbias"] = din("dbias", [HB, 128, 6, 128])
    I["w1"] = din("w1", [D, cfg.W1C])
    I["wo1"] = din("wo1", [D, D])
    I["ident"] = din("ident", [128, 128], BF16)
    I["ones_bf"] = din("ones_bf", [128, 128], BF16)
    I["ones32"] = din("ones32", [128, 128], F32)
    I["tri"] = din("tri", [128, 128], BF16)
    I["maskA"] = din("maskA", [128, 4, 512], BF16)
    I["maskSn"] = din("maskSn", [128, 4, 512], BF16)
    I["maskSp"] = din("maskSp", [128, 4, 512], BF16)
    I["dmask"] = din("dmask", [128, 6, 128], F32)
    I["cs"] = din("cs", [2, 64, SL], F32)
    out = nc.dram_tensor("out", [SL, D], F32, kind="ExternalOutput").ap()

    scr = {}
    scr["cq_T"] = dscr("cq_T", [512, SL], F32)
    scr["ckv_T"] = dscr("ckv_T", [512, SL], F32)
    rows0 = 64 + 3 * HB * 128 + cfg.FM0 + HA * 192 + 2 * HA * 128 + cfg.FM0
    rows1 = 3 * HS * 128 + 2 * cfg.FM1
    assert cfg.FM0 == cfg.FM1
    arena = dscr("arena16", [max(rows0, rows1), SL], BF16)
    pos = [0]

    def carve(nrows):
        v = arena[pos[0]:pos[0] + nrows, :]
        pos[0] += nrows
        return v

    def tokmajor(v, h):
        return v.rearrange("(h a) (b f) -> h (a b) f", h=h, f=128)

    scr["kr_T"] = carve(64)
    scr["qb_T"] = carve(HB * 128)
    scr["kb_T"] = carve(HB * 128)
    scr["vb"] = tokmajor(carve(HB * 128), HB)
    scr["gate_T"] = carve(cfg.FM0)
    scr["qa_T"] = carve(HA * 192).rearrange("(h r) s -> h r s", h=HA)
    scr["ka_T"] = carve(HA * 128).rearrange("(h r) s -> h r s", h=HA)
    scr["va"] = tokmajor(carve(HA * 128), HA)
    if cfg.NR == 1:
        scr["mix0_T"] = carve(cfg.FM0)
        scr["mixg0_T"] = scr["mix0_T"]
    else:
        cc_in = nc.dram_tensor("cc_in", [cfg.FM0, SL], BF16).ap()
        cc_out = nc.dram_tensor("cc_out", [cfg.NR * cfg.FM0, SL], BF16).ap()
        scr["mix0_T"] = cc_in
        scr["mixg0_T"] = cc_out
    pos[0] = 0
    scr["q1_T"] = carve(HS * 128)
    scr["k1_T"] = carve(HS * 128)
    scr["v1"] = tokmajor(carve(HS * 128), HS)
    scr["gate1_T"] = carve(cfg.FM1)
    if cfg.NR == 1:
        scr["mix1_T"] = carve(cfg.FM1)
        scr["mixg1_T"] = scr["mix1_T"]
    else:
        scr["mix1_T"] = cc_in
        scr["mixg1_T"] = cc_out
    scr["x1"] = out

    with contextlib.ExitStack() as es:
        S = Sched(nc, es)
        gcx = Ctx(nc, S)
        es.enter_context(gcx)
        ident = gcx.sb([128, 128], BF16, "ident")
        sl = S.slot()
        t_id = S.dma("sp", lambda e: e.dma_start(out=ident[:], in_=I["ident"]), sl)
        S.op("sp", lambda e: e.nop(), [t_id])
        S.emit()

        if "n0" in phases:
            hcx = Ctx(nc, S)
            hcx.__enter__()
            hT = hcx.sb([128, D // 128, SL], BF16, "hT")
            phase_norm(nc, S, cfg, I["x"], I["g0"], hT, ident)
            if "p0" in phases:
                phase_proj0(nc, S, cfg, hT, None, I["w0"], I["cs"], scr)
            hcx.__exit__(None, None, None)
        for ph in phases:
            if ph.startswith("dummy"):
                S.op("sp", lambda e: e.nop(), [])
                S.emit()
        if "u0" in phases:
            phase_up(nc, S, cfg, I["wuq"], I["wukv"], I["qg"], I["kvg"], I["cs"], I["ones32"], scr)
        if "a0" in phases:
            phase_mla(nc, S, cfg, I, scr)
        if "b0" in phases:
            phase_dil(nc, S, cfg, I, scr)
        if "o0" in phases:
            if cfg.NR > 1:
                phase_gather(nc, S, cfg, scr["mix0_T"], scr["mixg0_T"])
            phase_out(nc, S, cfg, scr["mixg0_T"], I["wo0"], I["x"], scr["x1"])
        if "n1" in phases:
            hcx = Ctx(nc, S)
            hcx.__enter__()
            hT = hcx.sb([128, D // 128, SL], BF16, "hT1")
            phase_norm(nc, S, cfg, I["x"] if "n1x" in phases else scr["x1"], I["g1"], hT, ident)
            if "p1" in phases:
                phase_proj1(nc, S, cfg, hT, I["w1"], scr)
            hcx.__exit__(None, None, None)
        wo_pre = None
        wcx = None
        if "s1" in phases and "o1" in phases:
            wcx = Ctx(nc, S)
            wcx.__enter__()
            wo1 = wcx.sb([128, D // 128, D], BF16, "wo1")
            wo_pre = (wo1, load_wo(S, wo1, I["wo1"]))
        if "s1" in phases:
            (phase_sb2 if SB_PAIRED else phase_sb)(nc, S, cfg, I, scr, extra_flush=[wo_pre[1]] if wo_pre else ())
        if "o1" in phases:
            if cfg.NR > 1:
                phase_gather(nc, S, cfg, scr["mix1_T"], scr["mixg1_T"])
            phase_out(nc, S, cfg, scr["mixg1_T"], I["wo1"], I["x"] if "n1x" in phases else scr["x1"], None,
                      final_gain=I["gf"], out_dram=out, wo_pre=wo_pre)
        if wcx is not None:
            wcx.__exit__(None, None, None)
    return nc


def make_in_maps(cfg, inp, ncores=8):
    HA, HB, HS, NR = cfg.HA, cfg.HB, cfg.HS, cfg.NR
    consts = const_arrays(cfg)
    bidx = dil_bias_index()
    f32 = np.float32
    x = np.asarray(inp["x"], f32)
    wie = np.asarray(inp["w_in_even"], f32)[0]
    wio = np.asarray(inp["w_in_odd"], f32)[0]
    wuq = np.asarray(inp["w_uq"], f32)[0]
    wukv = np.asarray(inp["w_ukv"], f32)[0]
    woe = np.asarray(inp["w_out_even"], f32)[0]
    woo = np.asarray(inp["w_out_odd"], f32)[0]
    rb = np.asarray(inp["rel_bias"], f32)
    ng = np.asarray(inp["norm_gain"], f32)
    rows0 = []
    for r in range(NR):
        rows0.extend(range(r * HA * 128, (r + 1) * HA * 128))
        rows0.extend(range(1024 + r * HB * 128, 1024 + (r + 1) * HB * 128))
    rows0 = np.array(rows0)
    maps = []
    for c in range(ncores):
        b = (c // NR) % x.shape[0]
        p = c % NR
        cols = list(range(0, 1088))
        for base in (1088, 2112, 3136):
            cols.extend(range(base + p * HB * 128, base + (p + 1) * HB * 128))
        cols.extend(range(4160 + p * HA * 128, 4160 + (p + 1) * HA * 128))
        cols.extend(range(4160 + 1024 + p * HB * 128, 4160 + 1024 + (p + 1) * HB * 128))
        cols1 = []
        for base in (0, 2048, 4096, 6144):
            cols1.extend(range(base + p * HS * 128, base + (p + 1) * HS * 128))
        heads_b = np.arange(p * HB, (p + 1) * HB)
        db = rb[bidx][:, :, :, heads_b]
        db = np.ascontiguousarray(db.transpose(3, 1, 0, 2))
        m = {
            "x": np.ascontiguousarray(x[b]),
            "g0": np.ascontiguousarray(ng[0][None, :]),
            "g1": np.ascontiguousarray(ng[1][None, :]),
            "gf": np.ascontiguousarray(np.asarray(inp["final_norm_gain"], f32)[None, :]),
            "w0": np.ascontiguousarray(wie[:, cols]),
            "qg": np.ascontiguousarray(np.asarray(inp["q_norm_gain"], f32)[0].reshape(4, 128).T),
            "kvg": np.ascontiguousarray(np.asarray(inp["kv_norm_gain"], f32)[0].reshape(4, 128).T),
            "wuq": np.ascontiguousarray(wuq[:, p * HA * 192:(p + 1) * HA * 192]),
            "wukv": np.ascontiguousarray(wukv[:, p * HA * 256:(p + 1) * HA * 256]),
            "wo0": np.ascontiguousarray(woe[rows0, :]),
            "dbias": db,
            "w1": np.ascontiguousarray(wio[:, cols1]),
            "wo1": np.ascontiguousarray(woo),
        }
        m.update(consts)
        maps.append(m)
    return maps


ALL_PHASES = ("n0", "p0", "u0", "a0", "b0", "o0", "n1", "p1", "s1", "o1")
PH_A = ("n0", "p0", "u0", "a0", "b0", "o0")
PH_B = ("n1x", "n1", "p1", "s1", "o1")


def kernel(**inputs):
    cfg = Cfg(NR=1)
    nb = np.asarray(inputs["x"]).shape[0]
    ncores = nb * cfg.NR
    nc = build_program(cfg, ALL_PHASES)
    maps = make_in_maps(cfg, inputs, ncores=ncores)
    res = run_bass_kernel_spmd(nc, maps, core_ids=list(range(ncores)))
    outs = [np.asarray(res.results[c]["out"], dtype=np.float32) for c in range(0, ncores, cfg.NR)]
    return np.stack(outs, 0)
```
